# Optimizing a Trainium2 kernel written in Bass

```python
import math
import jax, jax.numpy as jnp
from jax import lax
import numpy as np

D_MODEL = 1024
BATCH = 16
SEQ = 2048
DEPTH = 1

ATT_HEAD_DIM = 64
ATT_HEADS_PER_GROUP = 12
DILATED_PATTERNS = ((128, 1), (512, 4), (2048, 16))
N_ATT_GROUPS = 3
ATT_HEADS = N_ATT_GROUPS * ATT_HEADS_PER_GROUP
ATT_QKV = ATT_HEADS * ATT_HEAD_DIM
ATT_OUT = ATT_HEADS_PER_GROUP * ATT_HEAD_DIM
BAND_BLOCK = 128
NUM_BUCKETS = 32
MAX_DISTANCE = 2048
SSM_EXPAND = 2
D_INNER = SSM_EXPAND * D_MODEL
SSM_HEAD_DIM = 64
SSM_HEADS = D_INNER // SSM_HEAD_DIM
SSM_GROUPS = 4
D_STATE = 128
CONV_WIDTH = 4
CONV_DIM = D_INNER + 2 * SSM_GROUPS * D_STATE
SSD_CHUNK = 128
PLE_DIM = 256
ALPHA = (2.0 * DEPTH) ** 0.25
BETA = (8.0 * DEPTH) ** -0.25
LN_EPS = 1e-5
RMS_EPS = 1e-5
Q_END = ATT_QKV
K_END = Q_END + ATT_QKV
V_END = K_END + ATT_QKV
GATT_END = V_END + ATT_OUT
Z_END = GATT_END + D_INNER
XBC_END = Z_END + CONV_DIM
DT_END = XBC_END + SSM_HEADS
GMERGE_END = DT_END + 2 * D_MODEL
IN_COLS = GMERGE_END + D_MODEL
BRANCH_ROWS = ATT_OUT + D_INNER

kernel_name = "hybrid_dilated_attn_mamba2_deepnorm"


def _layer_norm(x, g, b):
    xf = x.astype(jnp.float32)
    mu = jnp.mean(xf, -1, keepdims=True)
    var = jnp.mean(jnp.square(xf - mu), -1, keepdims=True)
    return ((xf - mu) * lax.rsqrt(var + LN_EPS) * g.astype(jnp.float32) + b.astype(jnp.float32)).astype(x.dtype)


def _t5_bucket(dist):
    max_exact = NUM_BUCKETS // 2
    d_f = jnp.maximum(dist, 1).astype(jnp.float32)
    large = max_exact + (jnp.log(d_f / max_exact) / math.log(MAX_DISTANCE / max_exact)
                         * (NUM_BUCKETS - max_exact)).astype(jnp.int32)
    large = jnp.minimum(large, NUM_BUCKETS - 1)
    return jnp.where(dist < max_exact, dist, large)


def _dilated_group(q, k, v, bias_table, window, dilation):
    b, s, h, dh = q.shape
    span = dilation * BAND_BLOCK
    s_pad = -(-s // span) * span
    sub_len = s_pad // dilation
    nb = sub_len // BAND_BLOCK

    def to_blocks(t):
        t = jnp.pad(t, ((0, 0), (0, s_pad - s), (0, 0), (0, 0)))
        t = t.reshape(b, sub_len, dilation, h, dh).transpose(0, 2, 3, 1, 4)
        return t.reshape(b, dilation, h, nb, BAND_BLOCK, dh)

    def band(t):
        prev = jnp.pad(t, ((0, 0), (0, 0), (0, 0), (1, 0), (0, 0), (0, 0)))[:, :, :, :-1]
        return jnp.concatenate([prev, t], axis=4)

    qb = to_blocks(q)
    kk = band(to_blocks(k))
    vv = band(to_blocks(v))
    scores = jnp.einsum('brhnqd,brhnkd->brhnqk', qb, kk,
                        preferred_element_type=jnp.float32) * (dh ** -0.5)
    qi = jnp.arange(BAND_BLOCK)[:, None]
    kj = jnp.arange(2 * BAND_BLOCK)[None, :]
    delta = qi + BAND_BLOCK - kj
    blk = jnp.arange(nb)[:, None, None]
    valid = (delta >= 0) & (delta <= window // dilation) & (blk * BAND_BLOCK + kj - BAND_BLOCK >= 0)
    bucket = _t5_bucket(jnp.maximum(delta, 0) * dilation)
    bias = bias_table[bucket].astype(jnp.float32).transpose(2, 0, 1)
    scores = jnp.where(valid, scores + bias[:, None], -jnp.inf)
    m = jnp.max(scores, -1, keepdims=True)
    e = jnp.exp(scores - m)
    den = jnp.sum(e, -1)
    out = jnp.einsum('brhnqk,brhnkd->brhnqd', e, vv.astype(jnp.float32)) / den[..., None]
    lse = m[..., 0] + jnp.log(den)
    out = out.reshape(b, dilation, h, sub_len, dh).transpose(0, 3, 1, 2, 4).reshape(b, s_pad, h, dh)[:, :s]
    lse = lse.reshape(b, dilation, h, sub_len).transpose(0, 3, 1, 2).reshape(b, s_pad, h)[:, :s]
    return out, lse


def _causal_conv(u, w, bias):
    c = u.shape[-1]
    out = lax.conv_general_dilated(u, w[:, None, :].astype(u.dtype), window_strides=(1,),
                                   padding=[(CONV_WIDTH - 1, 0)],
                                   dimension_numbers=('NWC', 'WIO', 'NWC'),
                                   feature_group_count=c)
    return out + bias.astype(u.dtype)


def _ssd(xh, dt, a_head, bm, cm):
    b, s, h, p = xh.shape
    g, n = bm.shape[2], bm.shape[3]
    r = h // g
    nc = s // SSD_CHUNK
    lq = SSD_CHUNK
    a = (dt * a_head).reshape(b, nc, lq, h)
    a_cs = jnp.cumsum(a, axis=2)
    xdt = (xh * dt[..., None]).reshape(b, nc, lq, g, r, p)
    bc = bm.reshape(b, nc, lq, g, n)
    cc = cm.reshape(b, nc, lq, g, n)
    seg = a_cs[:, :, :, None, :] - a_cs[:, :, None, :, :]
    causal = jnp.tril(jnp.ones((lq, lq), dtype=bool))
    lmat = jnp.exp(jnp.where(causal[:, :, None], seg, -jnp.inf)).reshape(b, nc, lq, lq, g, r)
    cb = jnp.einsum('bclgn,bcsgn->bclsg', cc, bc)
    y_diag = jnp.einsum('bclsgr,bcsgrp->bclgrp', cb[..., None] * lmat, xdt)
    decay_to_end = jnp.exp(a_cs[:, :, -1:, :] - a_cs).reshape(b, nc, lq, g, r)
    states = jnp.einsum('bclgn,bclgrp->bcgrpn', bc, xdt * decay_to_end[..., None])
    chunk_decay = jnp.exp(a_cs[:, :, -1, :]).reshape(b, nc, g, r)

    def step(carry, inp):
        st, dec = inp
        return carry * dec[..., None, None] + st, carry

    init = jnp.zeros((b, g, r, p, n), jnp.float32)
    _, prev_states = lax.scan(step, init, (jnp.moveaxis(states, 1, 0), jnp.moveaxis(chunk_decay, 1, 0)))
    prev_states = jnp.moveaxis(prev_states, 0, 1)
    decay_from_start = jnp.exp(a_cs).reshape(b, nc, lq, g, r)
    y_off = jnp.einsum('bclgn,bcgrpn->bclgrp', cc, prev_states) * decay_from_start[..., None]
    return (y_diag + y_off).reshape(b, s, h, p)


def _hybrid_layer(x, p_i, w_in, b_gate, conv_w, conv_b, dt_bias, a_log, d_skip, ssm_norm_w,
                  w_branch, w_out, w_ple, ln_g, ln_b, rel_bias):
    b, s, _ = x.shape
    hcat = jnp.einsum('bsd,de->bse', x, w_in)
    q, k, v, g_att, z, xbc, dt_raw, g_merge, g_ple = jnp.split(
        hcat, [Q_END, K_END, V_END, GATT_END, Z_END, XBC_END, DT_END, GMERGE_END], axis=-1)

    q = q.reshape(b, s, N_ATT_GROUPS, ATT_HEADS_PER_GROUP, ATT_HEAD_DIM)
    k = k.reshape(b, s, N_ATT_GROUPS, ATT_HEADS_PER_GROUP, ATT_HEAD_DIM)
    v = v.reshape(b, s, N_ATT_GROUPS, ATT_HEADS_PER_GROUP, ATT_HEAD_DIM)
    outs, lses = [], []
    for gi, (win, dil) in enumerate(DILATED_PATTERNS):
        hs = slice(gi * ATT_HEADS_PER_GROUP, (gi + 1) * ATT_HEADS_PER_GROUP)
        o, l = _dilated_group(q[:, :, gi], k[:, :, gi], v[:, :, gi], rel_bias[:, hs], win, dil)
        outs.append(o)
        lses.append(l)
    wts = jax.nn.softmax(jnp.stack(lses), axis=0)
    o_att = jnp.sum(wts[..., None] * jnp.stack(outs), axis=0).reshape(b, s, ATT_OUT).astype(x.dtype)
    o_att = o_att * jax.nn.silu(g_att)

    xbc = jax.nn.silu(_causal_conv(xbc, conv_w, conv_b))
    xs, bm, cm = jnp.split(xbc, [D_INNER, D_INNER + SSM_GROUPS * D_STATE], axis=-1)
    xh = xs.astype(jnp.float32).reshape(b, s, SSM_HEADS, SSM_HEAD_DIM)
    bm = bm.astype(jnp.float32).reshape(b, s, SSM_GROUPS, D_STATE)
    cm = cm.astype(jnp.float32).reshape(b, s, SSM_GROUPS, D_STATE)
    dt = jax.nn.softplus(dt_raw.astype(jnp.float32) + dt_bias.astype(jnp.float32))
    a_head = -jnp.exp(a_log.astype(jnp.float32))
    y = _ssd(xh, dt, a_head, bm, cm) + d_skip.astype(jnp.float32)[:, None] * xh
    u = (y.reshape(b, s, D_INNER) * jax.nn.silu(z.astype(jnp.float32))).reshape(b, s, SSM_GROUPS, -1)
    u = u * lax.rsqrt(jnp.mean(jnp.square(u), -1, keepdims=True) + RMS_EPS)
    y_ssm = (u.reshape(b, s, D_INNER) * ssm_norm_w.astype(jnp.float32)).astype(x.dtype)

    y_a = jnp.einsum('bse,ed->bsd', o_att, w_branch[:ATT_OUT])
    y_b = jnp.einsum('bse,ed->bsd', y_ssm, w_branch[ATT_OUT:])
    g_a, g_b = jnp.split(g_merge, 2, axis=-1)
    merged = jax.nn.sigmoid(g_a + b_gate[0]) * y_a + jax.nn.sigmoid(g_b + b_gate[1]) * y_b
    mix = jnp.einsum('bsd,de->bse', merged, w_out)
    ple = jax.nn.sigmoid(g_ple + b_gate[2]) * jnp.einsum('bsq,qd->bsd', p_i, w_ple)
    return _layer_norm(ALPHA * x + mix + ple, ln_g, ln_b)


def setup_inputs(seed: int = 0) -> dict:
    key = jax.random.key(seed)
    ks = jax.random.split(key, 16)
    f32 = jnp.float32
    x = jax.random.normal(ks[0], (BATCH, SEQ, D_MODEL), f32)
    p = jax.random.normal(ks[1], (DEPTH, BATCH, SEQ, PLE_DIM), f32)
    col_scale = jnp.ones((IN_COLS,), f32).at[K_END:V_END].set(BETA).at[Z_END:Z_END + D_INNER].set(BETA)
    w_in = jax.random.normal(ks[2], (DEPTH, D_MODEL, IN_COLS), f32) * (D_MODEL ** -0.5) * col_scale
    b_gate = 0.1 * jax.random.normal(ks[3], (DEPTH, 3, D_MODEL), f32)
    conv_w = 0.5 * jax.random.normal(ks[4], (DEPTH, CONV_WIDTH, CONV_DIM), f32)
    conv_b = 0.05 * jax.random.normal(ks[5], (DEPTH, CONV_DIM), f32)
    dt0 = jnp.exp(jax.random.uniform(ks[6], (DEPTH, SSM_HEADS), f32, math.log(1e-3), math.log(1e-1)))
    dt_bias = dt0 + jnp.log(-jnp.expm1(-dt0))
    a_log = jnp.log(jax.random.uniform(ks[7], (DEPTH, SSM_HEADS), f32, 1.0, 16.0))
    d_skip = 1.0 + 0.1 * jax.random.normal(ks[8], (DEPTH, SSM_HEADS), f32)
    ssm_norm_w = 1.0 + 0.05 * jax.random.normal(ks[9], (DEPTH, D_INNER), f32)
    w_branch = jnp.concatenate([
        jax.random.normal(ks[10], (DEPTH, ATT_OUT, D_MODEL), f32) * (ATT_OUT ** -0.5),
        jax.random.normal(ks[11], (DEPTH, D_INNER, D_MODEL), f32) * (D_INNER ** -0.5)], axis=1) * BETA
    w_out = jax.random.normal(ks[12], (DEPTH, D_MODEL, D_MODEL), f32) * (D_MODEL ** -0.5) * BETA
    w_ple = jax.random.normal(ks[13], (DEPTH, PLE_DIM, D_MODEL), f32) * (PLE_DIM ** -0.5) * BETA
    kg, kb = jax.random.split(ks[14])
    ln_g = 1.0 + 0.05 * jax.random.normal(kg, (DEPTH, D_MODEL), f32)
    ln_b = 0.02 * jax.random.normal(kb, (DEPTH, D_MODEL), f32)
    rel_bias = 0.2 * jax.random.normal(ks[15], (NUM_BUCKETS, ATT_HEADS), f32)
    return {"x": x, "p": p, "w_in": w_in, "b_gate": b_gate, "conv_w": conv_w, "conv_b": conv_b,
            "dt_bias": dt_bias, "a_log": a_log, "d_skip": d_skip, "ssm_norm_w": ssm_norm_w,
            "w_branch": w_branch, "w_out": w_out, "w_ple": w_ple, "ln_g": ln_g, "ln_b": ln_b,
            "rel_bias": rel_bias}


def reference(x, p, w_in, b_gate, conv_w, conv_b, dt_bias, a_log, d_skip, ssm_norm_w,
              w_branch, w_out, w_ple, ln_g, ln_b, rel_bias):
    for i in range(DEPTH):
        x = _hybrid_layer(x, p[i], w_in[i], b_gate[i], conv_w[i], conv_b[i], dt_bias[i], a_log[i],
                          d_skip[i], ssm_norm_w[i], w_branch[i], w_out[i], w_ple[i], ln_g[i], ln_b[i],
                          rel_bias)
    return x
```

```python
import math
import os
import contextlib
import numpy as np
import concourse.bass as bass
import concourse.mybir as mybir
from concourse.bass_utils import run_bass_kernel_spmd

F32 = mybir.dt.float32
BF16 = mybir.dt.bfloat16
U8 = mybir.dt.uint8
AF = mybir.ActivationFunctionType
ALU = mybir.AluOpType
AX = mybir.AxisListType

N_CORES = 8
D_MODEL = 1024
SEQ = 2048
NSEQ = 2
K0, V0, GATT0, Z0, XBC0, DT0, GM0, GPLE0, IN_COLS = 2304, 4608, 6912, 7680, 9728, 12800, 12832, 14880, 15904
DILS = (1, 4, 16)
ALPHA = 2.0 ** 0.25
LN_EPS = 1e-5
RMS_EPS = 1e-5
WSLOT = 5120
NRING = 3
RHSA_ACT = int(os.environ.get("MK_RHSA_ACT", "0"))
PENG = "dve"


class Node:
    __slots__ = ("eng", "idx", "fn", "deps", "signal", "sigval", "dma", "cost", "lat", "gidx", "fin", "res", "odeps", "epoch", "table")

    def __init__(self, eng, idx, fn, dma=None):
        self.eng = eng
        self.idx = idx
        self.fn = fn
        self.cost = getattr(fn, "cost", 0.5)
        self.lat = getattr(fn, "lat", 0.0)
        self.table = getattr(fn, "table", None)
        self.gidx = 0
        self.fin = None
        self.res = ()
        self.odeps = []
        self.deps = []
        self.signal = False
        self.sigval = None
        self.dma = dma


class Tracker:
    ENGS = ("pe", "act", "dve", "pool", "sp")

    def __init__(self, nc):
        self.nc = nc
        self.ops = {e: [] for e in self.ENGS}
        self.lastw = {}
        self.readers = {}
        self.dma_cnt = {}
        self.dma_latest = {}
        self.bank_last = {}
        self.pending = {e: [] for e in self.ENGS}

    def _add(self, node, reads, writes):
        self.gcount = getattr(self, "gcount", 0) + 1
        node.gidx = self.gcount
        node.epoch = getattr(self, "epoch", 0)
        node.res = (tuple(reads), tuple(writes))
        deps = {}

        def add_dep(n):
            if n is not None and n is not node:
                deps[id(n)] = n

        for r in reads:
            add_dep(self.lastw.get(r))
        for w in writes:
            add_dep(self.lastw.get(w))
            for n in self.readers.get(w, {}).values():
                add_dep(n)
        banks = {int(r[2]) for r in list(reads) + list(writes) if r.startswith("ps") and r[2].isdigit()}
        for b in banks:
            bl = self.bank_last.setdefault(b, {})
            for e, n in bl.items():
                if e != node.eng:
                    add_dep(n)
            bl[node.eng] = node
        for n in self.pending[node.eng]:
            add_dep(n)
        self.pending[node.eng] = []
        node.deps = list(deps.values())
        key = ("dma", node.dma[0]) if node.dma else node.eng
        for r in reads:
            self.readers.setdefault(r, {})[key] = node
        for w in writes:
            self.lastw[w] = node
            self.readers[w] = {}

    def op(self, eng, fn, reads=(), writes=()):
        node = Node(eng, len(self.ops[eng]), fn)
        self.ops[eng].append(node)
        self._add(node, reads, writes)
        return node

    def dma(self, eng, slot, fn, reads=(), writes=()):
        self.dma_cnt[slot] = self.dma_cnt.get(slot, 0) + 16
        node = Node(eng, len(self.ops[eng]), fn, dma=(slot, self.dma_cnt[slot]))
        self.ops[eng].append(node)
        self._add(node, reads, writes)
        self.dma_latest[slot] = node
        return node

    def barrier(self):
        self.epoch = getattr(self, "epoch", 0) + 1
        last = [self.ops[e][-1] for e in self.ENGS if self.ops[e]]
        last += [n for sl, n in self.dma_latest.items() if sl != "cv"]
        for e in self.ENGS:
            self.pending[e] = list(last)

    def schedule(self, window=24, vis=0.15):
        allnodes = sorted((n for e in self.ENGS for n in self.ops[e]), key=lambda n: n.gidx)
        wcount = {}
        for n in allnodes:
            for w in n.res[1]:
                wcount[w] = wcount.get(w, 0) + 1
        last_acc = {}
        for n in allnodes:
            keys = set()
            for r in n.res[0] + n.res[1]:
                if wcount.get(r, 0) >= 2:
                    keys.add(r)
                if r.startswith("ps") and r[2].isdigit():
                    keys.add(("bank", int(r[2])))
            for k in keys:
                p = last_acc.get((k, n.eng))
                if p is not None:
                    n.odeps.append(p)
                last_acc[(k, n.eng)] = n
        rem = {e: list(self.ops[e]) for e in self.ENGS}
        out = {e: [] for e in self.ENGS}
        free = {e: 0.0 for e in self.ENGS}
        cur_epoch = {e: -1 for e in self.ENGS}
        cur_tab = [None]
        nleft = sum(len(v) for v in rem.values())
        while nleft:
            best = None
            for e in self.ENGS:
                lst = rem[e]
                if not lst:
                    continue
                cand = None
                wnd = lst[:window] if lst[0].epoch == cur_epoch[e] else lst[:1]
                for n in wnd:
                    if n.epoch != lst[0].epoch:
                        break
                    rdy = 0.0
                    ok = True
                    for d in n.odeps:
                        if d.fin is None:
                            ok = False
                            break
                    if not ok:
                        continue
                    for d in n.deps:
                        if d.fin is None:
                            ok = False
                            break
                        if d.eng == "pe" and e == "pe" and d.dma is None and n.dma is None:
                            continue
                        if d.fin + vis > rdy:
                            rdy = d.fin + vis
                    if not ok:
                        continue
                    st = max(rdy, free[e])
                    pen = 0.0
                    if e == "act" and n.table is not None and not (n.table == cur_tab[0] or (n.table == "E" and cur_tab[0] == "L")):
                        pen = 1.3
                    if cand is None or st + pen < cand[0] - 1e-9:
                        cand = (st + pen, n)
                    if st + pen <= free[e] + 1e-9:
                        break
                if cand is not None and (best is None or cand[0] < best[0] - 1e-9 or
                                         (abs(cand[0] - best[0]) <= 1e-9 and cand[1].gidx < best[1].gidx)):
                    best = cand
            st, n = best
            e = n.eng
            if e == "act" and n.table is not None and not (n.table == cur_tab[0] or (n.table == "E" and cur_tab[0] == "L")):
                cur_tab[0] = n.table
            rem[e].remove(n)
            out[e].append(n)
            cur_epoch[e] = n.epoch
            free[e] = st + n.cost
            n.fin = st + n.cost + n.lat
            nleft -= 1
        self.ops = out
        self.est_us = max(free.values())

    def emit(self, final_nodes):
        nc = self.nc
        for e in self.ENGS:
            for n in self.ops[e]:
                for d in n.deps:
                    if d.dma is None:
                        if d.eng == "pe" and n.eng == "pe" and n.dma is None:
                            continue
                        d.signal = True
        for n in final_nodes:
            if n.dma is None:
                n.signal = True
        for e in self.ENGS:
            c = 0
            for n in self.ops[e]:
                if n.dma is None and n.signal:
                    c += 1
                    n.sigval = c
        with contextlib.ExitStack() as st:
            esem = {e: st.enter_context(nc.semaphore("s_" + e)) for e in self.ENGS}
            dsem = {s: st.enter_context(nc.semaphore("d_" + s)) for s in self.dma_cnt}
            block = st.enter_context(nc.Block())

            def run(ename, eng):
                waited = {}
                for n in self.ops[ename]:
                    need = {}
                    for d in n.deps:
                        if d.dma is not None:
                            k, v = ("d", d.dma[0]), d.dma[1]
                        else:
                            if d.eng == "pe" and ename == "pe" and n.dma is None:
                                continue
                            k, v = ("e", d.eng), d.sigval
                        if v > need.get(k, 0):
                            need[k] = v
                    for k, v in need.items():
                        if waited.get(k, 0) >= v:
                            continue
                        waited[k] = v
                        eng.wait_ge(dsem[k[1]] if k[0] == "d" else esem[k[1]], v)
                    ins = n.fn(eng)
                    if n.dma is not None:
                        ins.then_inc(dsem[n.dma[0]], 16)
                    elif n.signal:
                        ins.then_inc(esem[ename], 1)
                if ename == "sp":
                    fin = {}
                    for n in final_nodes:
                        k, v = (("d", n.dma[0]), n.dma[1]) if n.dma is not None else (("e", n.eng), n.sigval)
                        fin[k] = max(fin.get(k, 0), v)
                    for k, v in fin.items():
                        eng.wait_ge(dsem[k[1]] if k[0] == "d" else esem[k[1]], v)

            block.tensor(lambda t: run("pe", t))
            block.scalar(lambda s: run("act", s))
            block.vector(lambda v: run("dve", v))
            block.gpsimd(lambda g: run("pool", g))
            block.sync(lambda sy: run("sp", sy))


class Arena:
    def __init__(self, nc, nbytes):
        self.ap = nc.alloc_sbuf_tensor("arena", [128, nbytes], U8).ap()
        self.nbytes = nbytes
        self.off = 0
        self.peak = 0

    def alloc(self, free, dt):
        esz = 4 if dt == F32 else 2
        n = int(np.prod(free)) * esz
        v = self.ap[:, self.off:self.off + n].bitcast(dt)
        self.off += (n + 31) // 32 * 32
        self.peak = max(self.peak, self.off)
        assert self.off <= self.nbytes, ("SBUF arena overflow", self.off)
        if len(free) == 2:
            v = v.rearrange("p (a b) -> p a b", a=free[0])
        elif len(free) == 3:
            v = v.rearrange("p (a b c) -> p a b c", a=free[0], b=free[1])
        return v


class WRing:
    def __init__(self, T, slots, reqs, cache=None, scratch=None):
        self.T = T
        self.slots = slots
        self.reqs = reqs
        self.plan = []
        self.cur = 0
        self.issued = 0
        self.cidx = 0
        self.cache = cache if cache is not None else {}
        self.scratch = scratch

    def block_start(self):
        self.cidx = 0

    def _views(self, base, parts):
        off = 0
        views = []
        for _, free in parts:
            n = int(np.prod(free))
            v = base[:, off:off + n]
            if len(free) == 2:
                v = v.rearrange("p (a b) -> p a b", a=free[0])
            views.append(v)
            off += n
        assert off <= WSLOT, off
        return views, off

    def emit_conversions(self, lo=0, hi=10 ** 9):
        for ci in sorted(self.cache):
            if not (lo <= ci < hi):
                continue
            parts = self.cache[ci]
            vs, _ = self._views(self.scratch[ci], parts)
            for (src, _), v in zip(parts, vs):
                self.T.dma("pool", "cv", _dma(v, src), writes=["wsc"])

    def next(self, parts, cached=False):
        i = self.cur
        self.cur += 1
        ci = None
        if cached:
            ci = self.cidx
            self.cidx += 1
            if self.reqs is None and ci not in self.cache:
                self.cache[ci] = parts
        self.plan.append((parts, ci))
        if self.reqs is not None:
            while self.issued < min(len(self.reqs), i + NRING):
                j = self.issued
                rparts, rci = self.reqs[j]
                slot = self.slots[j % NRING]
                wn = "w%d" % (j % NRING)
                if rci is None:
                    vs, _ = self._views(slot, rparts)
                    for (src, _), v in zip(rparts, vs):
                        self.T.dma("pool", wn, _dma(v, src), writes=[wn])
                else:
                    _, tot = self._views(slot, rparts)
                    self.T.dma("sp", "v%d" % (j % NRING), _dma(slot[:, 0:tot], self.scratch[rci][:, 0:tot]),
                               reads=["wsc"], writes=[wn])
                self.issued += 1
        return self._views(self.slots[i % NRING], parts)[0], "w%d" % (i % NRING)


def _nfree(ap):
    n = 1
    for d in ap.shape[1:]:
        n *= int(d)
    return n


def _c(f, cost):
    f.cost = cost
    return f


def _dma(out, in_):
    f = lambda e: e.dma_start(out=out, in_=in_)
    f.cost = 1.0
    f.lat = 2.5 + _nfree(out) * 128 * (4 if in_.dtype == F32 else 2) / 200e3
    return f


def _mm(out, lhsT, rhs, start, stop):
    n = _nfree(rhs)
    mult = 4.0 if rhs.dtype == F32 else 1.0
    return _c(lambda e: e.matmul(out, lhsT, rhs, start=start, stop=stop), mult * max(n / 2400.0 + 0.003, 0.096))


def _tr(out, in_, ident):
    return _c(lambda e: e.transpose(out=out, in_=in_, identity=ident), 0.1)


def _act(out, in_, func, bias=None, scale=None, accum_out=None):
    kw = {}
    if bias is not None:
        kw["bias"] = bias
    if scale is not None:
        kw["scale"] = scale
    if accum_out is not None:
        kw["accum_out"] = accum_out
    f = _c(lambda e: e.activation(out=out, in_=in_, func=func, **kw), 0.12 + _nfree(out) / 1000.0)
    f.table = {AF.Silu: "S", AF.Sigmoid: "G", AF.Ln: "L", AF.Exp: "E"}.get(func)
    return f


def _tt(out, in0, in1, op):
    return _c(lambda e: e.tensor_tensor(out=out, in0=in0, in1=in1, op=op), 0.1 + _nfree(out) / 850.0)


def _ts(out, in0, s1, s2, op0, op1=None):
    cost = 0.1 + _nfree(out) / 850.0
    if op1 is None:
        return _c(lambda e: e.tensor_scalar(out=out, in0=in0, scalar1=s1, scalar2=None, op0=op0), cost)
    return _c(lambda e: e.tensor_scalar(out=out, in0=in0, scalar1=s1, scalar2=s2, op0=op0, op1=op1), cost)


def _stt(out, in0, scalar, in1, op0, op1):
    return _c(lambda e: e.scalar_tensor_tensor(out=out, in0=in0, scalar=scalar, in1=in1, op0=op0, op1=op1),
              0.1 + _nfree(out) / 850.0)


def _copy(out, in_):
    return _c(lambda e: e.tensor_copy(out=out, in_=in_), 0.1 + _nfree(out) / 850.0)


def _acopy(out, in_):
    return _c(lambda e: e.copy(out=out, in_=in_), 0.12 + _nfree(out) / 1000.0)


def _memset(ap, val):
    return _c(lambda e: e.memset(ap, val), 0.1 + _nfree(ap) / 1700.0)


def _recip(out, in_):
    return _c(lambda e: e.reciprocal(out=out, in_=in_), 0.1 + _nfree(out) / 850.0)


def _interleave(A, B):
    out = []
    na, nb = len(A), len(B)
    ia = ib = 0
    while ia < na or ib < nb:
        if ib >= nb or (ia < na and ia * nb <= ib * na):
            out.append(A[ia])
            ia += 1
        else:
            out.append(B[ib])
            ib += 1
    return out


def build_program(debug=None):
    nc = bass.Bass("TRN2", target_bir_lowering=False)

    def din(name, shape):
        return nc.dram_tensor(name, list(shape), F32, kind="ExternalInput").ap()

    xT_d = din("xT", [NSEQ, D_MODEL, SEQ])
    x_d = din("x", [NSEQ, SEQ, D_MODEL])
    pT_d = din("pT", [NSEQ, 256, SEQ])
    w_in_d = din("w_in", [D_MODEL, IN_COLS])
    w_br_d = din("w_branch", [2816, D_MODEL])
    w_out_d = din("w_out", [D_MODEL, D_MODEL])
    w_ple_d = din("w_ple", [256, D_MODEL])
    biasT_d = din("biasT", [36, 128, 256])
    maskT_d = din("maskT", [128, 256])
    cmat_d = din("cmat", [4, 128, 128])
    cw_d = din("cw", [128, 24, 4])
    cb_d = din("cb", [128, 24])
    bg_d = din("bg", [128, 2, 8])
    b2_d = din("b2", [D_MODEL])
    dtb_d = din("dt_bias", [32])
    alog_d = din("a_log", [32])
    dsk_d = din("d_skip", [32])
    nw_d = din("ssm_norm_w", [2048])
    lng_d = din("ln_g", [D_MODEL])
    lnb_d = din("ln_b", [D_MODEL])
    out_d = nc.dram_tensor("out", [NSEQ, SEQ, D_MODEL], F32, kind="ExternalOutput").ap()
    dbg_d = None
    if debug == "oatt":
        dbg_d = nc.dram_tensor("dbg", [NSEQ, 768, SEQ], F32, kind="ExternalOutput").ap()
    elif debug == "yssm":
        dbg_d = nc.dram_tensor("dbg", [NSEQ, 2048, SEQ], F32, kind="ExternalOutput").ap()

    w_in_v = w_in_d.rearrange("(c p) n -> p c n", p=128)
    w_br_v = w_br_d.rearrange("(i p) d -> p i d", p=128)
    w_out_v = w_out_d.rearrange("(c p) n -> p c n", p=128)
    w_ple_v = w_ple_d.rearrange("(c p) n -> p c n", p=128)

    NCACHE = 24
    wscr = nc.dram_tensor("wscratch", [NCACHE, 128, WSLOT], BF16, kind="Internal").ap()
    ar = Arena(nc, 206 * 1024)
    ps = [nc.alloc_psum_tensor("ps%d" % i, [128, 512], F32).ap() for i in range(8)]
    psb = [p.bitcast(BF16) for p in ps]

    ident = ar.alloc([128], BF16)
    tri = ar.alloc([128], BF16)
    Umat = ar.alloc([128], BF16)
    tri32 = ar.alloc([128], F32)
    ones32 = ar.alloc([128], F32)
    dtb_bc = ar.alloc([32], F32)
    A_bc = ar.alloc([32], F32)
    D_bc = ar.alloc([32], F32)
    cw = ar.alloc([24, 4], F32)
    cb = ar.alloc([24], F32)
    bg = ar.alloc([2, 8], F32)
    epsln = ar.alloc([1], F32)
    NEGM = ar.alloc([4, 128], BF16)
    epsr = ar.alloc([1], F32)
    oattT = ar.alloc([6, SEQ], BF16)
    wslots = [ar.alloc([WSLOT], BF16) for _ in range(NRING)]
    base = ar.off

    xT = ar.alloc([8, SEQ], BF16)
    EBT = ar.alloc([36, 256], BF16)
    after_ebt = ar.off
    qT = [ar.alloc([SEQ], BF16) for _ in range(2)]
    kT = [ar.alloc([SEQ], BF16) for _ in range(2)]
    Vb = [ar.alloc([16, 4, 64], BF16) for _ in range(2)]
    Vn = [ar.alloc([16, 4, 64], BF16) for _ in range(3)]
    acc = [ar.alloc([SEQ], F32) for _ in range(2)]
    gS = [ar.alloc([SEQ], BF16) for _ in range(2)]
    Eb = [ar.alloc([512], BF16) for _ in range(2)]
    PT = [ar.alloc([512], BF16) for _ in range(4)]
    rden = ar.alloc([SEQ], F32)
    attn_end = ar.off

    ar.off = base
    XT = ar.alloc([24, 512], BF16)
    xTb2 = [ar.alloc([8, 512], BF16) for _ in range(2)]
    pTb2 = [ar.alloc([2, 512], BF16) for _ in range(2)]
    nw_bc = ar.alloc([2048], F32)
    b2_bc = ar.alloc([1024], F32)
    lng_bc = ar.alloc([1024], F32)
    lnb_bc = ar.alloc([1024], F32)
    Sst = ar.alloc([4, 512], F32)
    Sbf = ar.alloc([4, 512], BF16)
    hist = ar.alloc([24, 3], F32)
    dtc = ar.alloc([4, 32], F32)
    ac = ar.alloc([4, 32], F32)
    csb = ar.alloc([4, 32], F32)
    dstart = ar.alloc([4, 32], F32)
    dend = ar.alloc([4, 32], F32)
    cdec = ar.alloc([4, 32], F32)
    sm_t = ar.alloc([4, 32], F32)
    sm_e = ar.alloc([4, 32], F32)
    xres = ar.alloc([4, 1024], F32)
    zsall = ar.alloc([16, 512], BF16)
    lnsq = ar.alloc([1024], BF16)
    lnst = [ar.alloc([4], F32) for _ in range(2)]
    sub = ar.off
    uraw = [ar.alloc([515], F32) for _ in range(2)]
    ctmp = [ar.alloc([512], F32) for _ in range(2)]
    ar.off = sub
    xD = [ar.alloc([512], BF16) for _ in range(2)]
    xdt = [ar.alloc([512], BF16) for _ in range(2)]
    xdtd = [ar.alloc([512], BF16) for _ in range(2)]
    Btm = [ar.alloc([128], BF16) for _ in range(2)]
    Gm = [ar.alloc([128], BF16) for _ in range(2)]
    rhsa = [ar.alloc([8, 128], BF16) for _ in range(2)]
    Es = [ar.alloc([512], BF16) for _ in range(2)]
    MT = [ar.alloc([8, 128], BF16) for _ in range(2)]
    t1 = [ar.alloc([512], F32) for _ in range(2)]
    ug = [ar.alloc([512], F32) for _ in range(2)]
    usq = ar.alloc([512], F32)
    yb = [ar.alloc([512], BF16) for _ in range(2)]
    ssq = [ar.alloc([4], F32) for _ in range(2)]
    ssd_end = ar.off
    ar.off = sub
    mergedT = ar.alloc([8, 512], BF16)
    sa = [ar.alloc([512], F32) for _ in range(2)]
    m1 = [ar.alloc([512], F32) for _ in range(2)]
    sgt = [ar.alloc([512], F32) for _ in range(2)]
    tpt = [ar.alloc([512], F32) for _ in range(2)]
    tail_end = ar.off
    ar.off = after_ebt
    braw = ar.alloc([36, 256], F32)
    mraw = ar.alloc([256], F32)

    def construct(T, W):
        finals = []

        cm = cmat_d.rearrange("k p f -> p k f")
        T.dma("pool", "c0_0", _dma(ident, cm[:, 0, :]), writes=["ident"])
        T.dma("pool", "c0_1", _dma(tri, cm[:, 1, :]), writes=["tri"])
        T.dma("pool", "c0_2", _dma(Umat, cm[:, 2, :]), writes=["U"])
        T.dma("sp", "c1_3", _dma(tri32, cm[:, 1, :]), writes=["tri32"])
        T.dma("sp", "c1_4", _dma(ones32, cm[:, 3, :]), writes=["ones32"])
        T.dma("sp", "c1_5", _dma(dtb_bc, dtb_d.partition_broadcast(128)), writes=["dtb"])
        T.dma("sp", "c1_6", _dma(A_bc, alog_d.partition_broadcast(128)), writes=["A"])
        T.dma("sp", "c1_7", _dma(D_bc, dsk_d.partition_broadcast(128)), writes=["D"])
        T.dma("sp", "c1_8", _dma(cw, cw_d), writes=["cw"])
        T.dma("sp", "c1_9", _dma(cb, cb_d), writes=["cb"])
        T.dma("sp", "c1_10", _dma(bg, bg_d), writes=["bg"])
        T.op("dve", _memset(epsln, LN_EPS), writes=["epsln"])
        T.op("dve", _memset(epsr, RMS_EPS), writes=["epsr"])
        T.op("dve", _ts(NEGM, Umat.unsqueeze(1).broadcast_to([128, 4, 128]), -30000.0, None, ALU.mult),
             reads=["U"], writes=["NEGM"])
        T.op("act", _act(A_bc, A_bc, AF.Exp), reads=["A"], writes=["A"])
        T.op("dve", _ts(A_bc, A_bc, -1.0, None, ALU.mult), reads=["A"], writes=["A"])
        T.barrier()

        def attention(s):
            for c in range(8):
                T.dma("pool", "xT", _dma(xT[:, c, :], xT_d[s, c * 128:(c + 1) * 128, :]), writes=["xT"])
            for h6 in range(6):
                T.dma("sp", "c2", _dma(braw[:, h6 * 6:(h6 + 1) * 6, :], biasT_d[h6 * 6:(h6 + 1) * 6].rearrange("h k q -> k h q")),
                      writes=["braw"])
            T.dma("sp", "c7", _dma(mraw, maskT_d), writes=["mraw"])
            for h6 in range(6):
                hsl = slice(h6 * 6, (h6 + 1) * 6)
                T.op("act", _act(braw[:, hsl, :], braw[:, hsl, :], AF.Exp), reads=["braw"], writes=["braw"])
                T.op("dve", _tt(EBT[:, hsl, :], braw[:, hsl, :], mraw.unsqueeze(1).broadcast_to([128, 6, 256]), ALU.mult),
                     reads=["braw", "mraw"], writes=["EBT"])
            T.barrier()
            for bi in range(2):
                T.op("dve", _memset(Vb[bi][:, :, 1:3, :], 1.0), writes=["V%d" % bi])
            for gi in range(3):
                T.op("dve", _memset(Vn[gi][:, :, 1:3, :], 1.0), writes=["Vn%d" % gi])
            GROUPS = [int(c) for c in os.environ.get("MK_GROUPS", "012")]
            units = [(hp, g) for hp in range(6) for g in GROUPS]
            rot = {"ip": 0, "s": 0, "o": 0, "e": 0, "pt": 0}

            def inproj_steps(u, bi):
                hp, g = u
                D = DILS[g]
                nb = 16 // D
                steps = []
                st = {}

                def s_load():
                    parts = [(w_in_v[:, :, g * 768 + hp * 128 + off: g * 768 + hp * 128 + off + 128], [8, 128])
                             for off in (0, K0)]
                    if hp % 2 == 0:
                        parts.append((w_in_v[:, :, V0 + g * 768 + hp * 128: V0 + g * 768 + hp * 128 + 256], [8, 256]))
                    else:
                        parts.append((w_in_v[:, :, V0 + g * 768 + hp * 128: V0 + g * 768 + hp * 128 + 2], [8, 2]))
                    if g == GROUPS[0]:
                        parts.append((w_in_v[:, :, GATT0 + hp * 128: GATT0 + hp * 128 + 128], [8, 128]))
                    st["w"], st["wr"] = W.next(parts)
                steps.append(s_load)

                def qk_step(which, tb):
                    def f():
                        wv = st["w"][which]
                        b = 4 + rot["ip"] % 4
                        rot["ip"] += 1
                        pr = "ps%d" % b
                        for c in range(8):
                            T.op("pe", _mm(ps[b], wv[:, c, :], xT[:, c, tb * 512:(tb + 1) * 512], c == 0, c == 7),
                                 reads=[st["wr"], "xT"], writes=[pr])
                        dst = (qT if which == 0 else kT)[bi]
                        dv = dst.rearrange("p (r m) -> p r m", r=D)[:, :, tb * (512 // D):(tb + 1) * (512 // D)]
                        sv = ps[b].rearrange("p (m r) -> p r m", r=D)
                        name = ("q%d" if which == 0 else "k%d") % bi
                        if which == 0:
                            T.op("act", lambda e, dv=dv, sv=sv: e.mul(out=dv, in_=sv, mul=0.125), reads=[pr], writes=[name])
                        else:
                            T.op("act", _acopy(dv, sv), reads=[pr], writes=[name])
                    return f
                for which in (0, 1):
                    for tb in range(4):
                        steps.append(qk_step(which, tb))

                def v_step(kb2):
                    def f():
                        wv = st["w"][2]
                        b = 4 + rot["ip"] % 4
                        rot["ip"] += 1
                        pr = "ps%d" % b
                        for kk in range(2):
                            kbp = kb2 * 2 + kk
                            r, n = kbp // nb, kbp % nb
                            t0 = r + D * 128 * n
                            for c in range(8):
                                T.op("pe", _mm(ps[b][:, kk * 256:(kk + 1) * 256], xT[:, c, t0:t0 + D * 127 + 1:D],
                                               wv[:, c, :], c == 0, c == 7),
                                     reads=[st["wr"], "xT"], writes=[pr])
                        sv = ps[b].rearrange("p (k q h d) -> p k q h d", k=2, q=2, h=2)
                        T.op("dve", _copy(Vb[bi][:, kb2 * 2:(kb2 + 1) * 2, 0:4:3, :], sv[:, :, 0, :, :]), reads=[pr], writes=["V%d" % bi])
                        T.op("dve", _copy(Vn[g][:, kb2 * 2:(kb2 + 1) * 2, 0:4:3, :], sv[:, :, 1, :, :]), reads=[pr], writes=["Vn%d" % g])
                    return f
                if hp % 2 == 0:
                    for kb2 in range(8):
                        steps.append(v_step(kb2))

                if g == GROUPS[0]:
                    def g_step(tb):
                        def f():
                            wv = st["w"][3]
                            b = 4 + rot["ip"] % 4
                            rot["ip"] += 1
                            pr = "ps%d" % b
                            for c in range(8):
                                T.op("pe", _mm(ps[b], wv[:, c, :], xT[:, c, tb * 512:(tb + 1) * 512], c == 0, c == 7),
                                     reads=[st["wr"], "xT"], writes=[pr])
                            T.op("act", _act(gS[hp % 2][:, tb * 512:(tb + 1) * 512], ps[b], AF.Silu),
                                 reads=[pr], writes=["gS%d" % (hp % 2)])
                        return f
                    for tb in range(4):
                        steps.append(g_step(tb))
                return steps

            def attend_steps(u, bi):
                hp, g = u
                D = DILS[g]
                nb = 16 // D
                m256 = nb > 1
                nbank = 8 if m256 else 4
                items = [(hd, i) for hd in range(2) for i in range(nbank)]
                info = {}
                steps = []

                def S_rec(k):
                    hd, i = items[k]
                    rows = slice(hd * 64, hd * 64 + 64)
                    b = rot["s"] % 2
                    rot["s"] += 1
                    info[k] = {"sb": b}
                    pr = "ps%d" % b
                    if m256:
                        for kk in range(2):
                            kb = 2 * i + kk
                            N = 256 if (kb % nb) != nb - 1 else 128
                            T.op("pe", _mm(ps[b][:, kk * 256:kk * 256 + N], kT[bi][rows, kb * 128:(kb + 1) * 128],
                                           qT[bi][rows, kb * 128:kb * 128 + N], True, True),
                                 reads=["q%d" % bi, "k%d" % bi], writes=[pr])
                    else:
                        for kk in range(4):
                            kb = 4 * i + kk
                            T.op("pe", _mm(ps[b][:, kk * 128:(kk + 1) * 128], kT[bi][rows, kb * 128:(kb + 1) * 128],
                                           qT[bi][rows, kb * 128:(kb + 1) * 128], True, True),
                                 reads=["q%d" % bi, "k%d" % bi], writes=[pr])

                def rest_rec(k):
                    hd, i = items[k]
                    hh = g * 12 + 2 * hp + hd
                    b = info[k]["sb"]
                    eb = rot["e"] % 2
                    rot["e"] += 1
                    pb = rot["pt"] % 4
                    rot["pt"] += 1
                    info[k]["pt"] = pb
                    T.op("act", _act(Eb[eb], ps[b], AF.Exp), reads=["ps%d" % b], writes=["E%d" % eb])
                    nseg, w = (2, 256) if m256 else (4, 128)
                    T.op("dve", _tt(PT[pb].rearrange("p (s w) -> p s w", s=nseg),
                                    Eb[eb].rearrange("p (s w) -> p s w", s=nseg),
                                    EBT[:, hh, 0:w].unsqueeze(1).broadcast_to([128, nseg, w]), ALU.mult),
                         reads=["E%d" % eb, "EBT"], writes=["PT%d" % pb])
                    vsl = slice(hd * 2, hd * 2 + 2)

                    Vbuf, Vname = (Vb[bi], "V%d" % bi) if hp % 2 == 0 else (Vn[g], "Vn%d" % g)

                    def lhs_v(kb):
                        return Vbuf[:, kb, vsl, :].rearrange("p a d -> p (a d)")
                    qbs = [2 * i, 2 * i + 1] if m256 else [4 * i + j for j in range(4)]
                    for qb in qbs:
                        if qb % 4 == 0:
                            ob = 2 + rot["o"] % 2
                            rot["o"] += 1
                            info[("ob", hd)] = ob
                        ob = info[("ob", hd)]
                        orr = "ps%d" % ob
                        oreg = ps[ob][:, (qb % 4) * 128:(qb % 4 + 1) * 128]
                        if m256:
                            has_prev = (qb % nb) != 0
                            if has_prev:
                                if qb == 2 * i + 1:
                                    T.op("pe", _mm(oreg, lhs_v(qb - 1), PT[pb][:, 128:256], True, False),
                                         reads=["PT%d" % pb, Vname], writes=[orr])
                                else:
                                    ppb = info[k - 1]["pt"]
                                    T.op("pe", _mm(oreg, lhs_v(qb - 1), PT[ppb][:, 384:512], True, False),
                                         reads=["PT%d" % ppb, Vname], writes=[orr])
                            c0 = (qb - 2 * i) * 256
                            T.op("pe", _mm(oreg, lhs_v(qb), PT[pb][:, c0:c0 + 128], not has_prev, True),
                                 reads=["PT%d" % pb, Vname], writes=[orr])
                        else:
                            c0 = (qb - 4 * i) * 128
                            T.op("pe", _mm(oreg, lhs_v(qb), PT[pb][:, c0:c0 + 128], True, True),
                                 reads=["PT%d" % pb, Vname], writes=[orr])
                        if qb % 4 == 3:
                            j = qb // 4
                            an = "acc%d" % hd
                            if g == 0:
                                av, sv = acc[hd][:, j * 512:(j + 1) * 512], ps[ob]
                            elif g == 1:
                                av, sv = acc[hd][:, j:SEQ:4], ps[ob]
                            else:
                                av = acc[hd].rearrange("p (m r) -> p r m", r=16)[:, 4 * j:4 * j + 4, :]
                                sv = ps[ob].rearrange("p (r m) -> p r m", r=4)
                            if g == GROUPS[0]:
                                T.op("dve", _copy(av, sv), reads=[orr], writes=[an])
                            else:
                                T.op("dve", _tt(av, sv, av, ALU.add), reads=[orr, an], writes=[an])

                steps.append(lambda: S_rec(0))
                for k in range(len(items)):
                    def f(k=k):
                        if k + 1 < len(items):
                            S_rec(k + 1)
                        rest_rec(k)
                    steps.append(f)
                if g == GROUPS[-1]:
                    def fin():
                        gb = "gS%d" % (hp % 2)
                        gs = gS[hp % 2]
                        T.op("act", _act(rden[0:64, :], acc[0][64:128, :], AF.Ln), reads=["acc0"], writes=["rdenA"])
                        T.op("act", _act(rden[0:64, :], rden[0:64, :], AF.Exp, scale=-1.0), reads=["rdenA"], writes=["rdenA"])
                        T.op("dve", _tt(acc[0][0:64, :], acc[0][0:64, :], rden[0:64, :], ALU.mult),
                             reads=["acc0", "rdenA"], writes=["acc0"])
                        T.op("dve", _tt(oattT[0:64, hp, :], acc[0][0:64, :], gs[0:64, :], ALU.mult),
                             reads=["acc0", gb], writes=["oattT"])
                        T.op("act", _act(rden[64:128, :], acc[1][0:64, :], AF.Ln), reads=["acc1"], writes=["rdenB"])
                        T.op("act", _act(rden[64:128, :], rden[64:128, :], AF.Exp, scale=-1.0), reads=["rdenB"], writes=["rdenB"])
                        T.op("dve", _tt(acc[1][64:128, :], acc[1][64:128, :], rden[64:128, :], ALU.mult),
                             reads=["acc1", "rdenB"], writes=["acc1"])
                        T.op("dve", _tt(oattT[64:128, hp, :], acc[1][64:128, :], gs[64:128, :], ALU.mult),
                             reads=["acc1", gb], writes=["oattT"])
                    steps.append(fin)
                return steps

            for f in inproj_steps(units[0], 0):
                f()
            for i, u in enumerate(units):
                A = attend_steps(u, i % 2)
                B = inproj_steps(units[i + 1], (i + 1) % 2) if i + 1 < len(units) else []
                for f in _interleave(A, B):
                    f()
                if s == 0:
                    W.emit_conversions(2 * i, 2 * i + 2 if i + 1 < len(units) else 10 ** 9)
            if debug == "oatt":
                for hp in range(6):
                    finals.append(T.dma("pool", "dbg", _dma(dbg_d[s, hp * 128:(hp + 1) * 128, :], oattT[:, hp, :]),
                                        reads=["oattT"]))

        def stream(s):
            T.dma("sp", "c3", _dma(nw_bc, nw_d.partition_broadcast(128)), writes=["nw"])
            T.dma("sp", "c4", _dma(b2_bc, b2_d.partition_broadcast(128)), writes=["b2"])
            T.dma("sp", "c5", _dma(lng_bc, lng_d.partition_broadcast(128)), writes=["lng"])
            T.dma("sp", "c6", _dma(lnb_bc, lnb_d.partition_broadcast(128)), writes=["lnb"])
            T.op("dve", _memset(Sst, 0.0), writes=["S0", "S1", "S2", "S3"])
            T.op("dve", _memset(Sbf, 0.0), writes=["Sb0", "Sb1", "Sb2", "Sb3"])
            T.op("dve", _memset(hist, 0.0), writes=["hist%d" % q for q in range(24)])
            rot = {"u": 0, "c": 0, "k": 0}
            def load_xp(blk):
                tsl_ = slice(blk * 512, (blk + 1) * 512)
                q2 = blk % 2
                T.dma("pool", "xb%d" % q2, _dma(xTb2[q2], xT_d[s].rearrange("(c p) t -> p c t", p=128)[:, :, tsl_]),
                      writes=["xTb%d" % q2])
                T.dma("pool", "pb%d" % q2, _dma(pTb2[q2], pT_d[s].rearrange("(c p) t -> p c t", p=128)[:, :, tsl_]),
                      writes=["pTb%d" % q2])

            load_xp(0)
            for blk in range(4):
                tsl = slice(blk * 512, (blk + 1) * 512)
                xTb, pTb = xTb2[blk % 2], pTb2[blk % 2]
                XB, PB = "xTb%d" % (blk % 2), "pTb%d" % (blk % 2)
                if blk + 1 < 4:
                    load_xp(blk + 1)
                W.block_start()
                T.dma("sp", "xr", _dma(xres, x_d[s, tsl, :].rearrange("(t p) d -> p t d", p=128)),
                      writes=["xres", "xres0", "xres1", "xres2", "xres3"])

                for cg in range(6):
                    (wv,), wr = W.next([(w_in_v[:, :, XBC0 + cg * 512: XBC0 + (cg + 1) * 512], [8, 512])], cached=True)
                    for j in range(4):
                        cc = cg * 4 + j
                        b = rot["u"] % 2
                        rot["u"] += 1
                        pr = "ps%d" % b
                        for c in range(8):
                            T.op("pe", _mm(ps[b], wv[:, c, j * 128:(j + 1) * 128], xTb[:, c, :], c == 0, c == 7),
                                 reads=[wr, XB], writes=[pr])
                        ur, ct = uraw[b], ctmp[b]
                        T.op("act", _acopy(ur[:, 0:3], hist[:, cc, :]), reads=["hist%d" % cc], writes=["urh%d" % b])
                        T.op("act", _acopy(ur[:, 3:515], ps[b]), reads=[pr], writes=["ur%d" % b])
                        T.op("act", _acopy(hist[:, cc, :], ur[:, 512:515]), reads=["ur%d" % b], writes=["hist%d" % cc])
                        T.op("act", _act(ct, ps[b], AF.Identity, bias=cb[:, cc:cc + 1], scale=cw[:, cc, 3:4]),
                             reads=[pr, "cw", "cb"], writes=["ct%d" % b])
                        for k in (2, 1, 0):
                            T.op("dve", _stt(ct, ur[:, k:k + 512], cw[:, cc, k:k + 1], ct, ALU.mult, ALU.add),
                                 reads=["ur%d" % b, "urh%d" % b, "cw", "ct%d" % b], writes=["ct%d" % b])
                        xw = ["XT%d_%d" % (cc // 4, q4) for q4 in range(4)] if cc < 16 else ["XTbc"]
                        T.op("act", _act(XT[:, cc, :], ct, AF.Silu), reads=["ct%d" % b], writes=xw)

                T.barrier()
                (wdt,), wr = W.next([(w_in_v[:, :, DT0:DT0 + 32], [8, 32])], cached=True)
                for c4 in range(4):
                    csl = slice(c4 * 128, (c4 + 1) * 128)
                    for c in range(8):
                        T.op("pe", _mm(ps[2][:, c4 * 32:(c4 + 1) * 32], xTb[:, c, csl], wdt[:, c, :], c == 0, c == 7),
                             reads=[wr, XB], writes=["ps2"])
                T.op("dve", _tt(sm_t, ps[2][:, 0:128].rearrange("p (a h) -> p a h", a=4),
                                dtb_bc.unsqueeze(1).broadcast_to([128, 4, 32]), ALU.add),
                     reads=["ps2", "dtb"], writes=["sm_t"])
                T.op("act", _act(sm_e, sm_t, AF.Exp), reads=["sm_t"], writes=["sm_e"])
                T.op("act", _act(dtc, sm_e, AF.Ln, bias=1.0), reads=["sm_e"], writes=["dtc"])
                T.op("dve", _tt(ac, dtc, A_bc.unsqueeze(1).broadcast_to([128, 4, 32]), ALU.mult),
                     reads=["dtc", "A"], writes=["ac"])
                for c4 in range(4):
                    T.op("pe", _mm(ps[3][:, c4 * 32:(c4 + 1) * 32], tri32, ac[:, c4, :], True, True),
                         reads=["tri32", "ac"], writes=["ps3"])
                    T.op("pe", _mm(ps[3][:, 128 + c4 * 32:128 + (c4 + 1) * 32], ones32, ac[:, c4, :], True, True),
                         reads=["ones32", "ac"], writes=["ps3"])
                cs_ps = ps[3][:, 0:128].rearrange("p (a h) -> p a h", a=4)
                tot_ps = ps[3][:, 128:256].rearrange("p (a h) -> p a h", a=4)
                T.op("act", _acopy(csb, cs_ps), reads=["ps3"], writes=["csb"])
                T.op("act", _act(dstart, cs_ps, AF.Exp), reads=["ps3"], writes=["dstart"])
                T.op("act", _act(cdec, tot_ps, AF.Exp), reads=["ps3"], writes=["cdec"])
                T.op("dve", _tt(sm_t, tot_ps, csb, ALU.subtract), reads=["ps3", "csb"], writes=["sm_t"])
                T.op("act", _act(dend, sm_t, AF.Exp), reads=["sm_t"], writes=["dend"])

                its = [(g, c4) for g in range(4) for c4 in range(4)]
                wzs = {}

                def ssd_front(i):
                    g, c4 = its[i]
                    k = i % 2
                    kk = str(k)
                    hs = slice(8 * g, 8 * g + 8)
                    csl = slice(c4 * 128, (c4 + 1) * 128)
                    xn = "XT%d_%d" % (g, c4)
                    if c4 == 0:
                        wzs[g] = W.next([(w_in_v[:, :, Z0 + g * 512: Z0 + (g + 1) * 512], [8, 512])], cached=True)
                    (wz,), wr = wzs[g]
                    if c4 == 0:
                        for c4b in range(4):
                            for c in range(8):
                                T.op("pe", _mm(ps[0], xTb[:, c, c4b * 128:(c4b + 1) * 128], wz[:, c, :], c == 0, c == 7),
                                     reads=[wr, XB], writes=["ps0"])
                            T.op("act", _act(zsall[:, i + c4b, :], ps[0], AF.Silu), reads=["ps0"], writes=["zs%d" % (i + c4b)])
                    NA = RHSA_ACT
                    for h in range(NA):
                        T.op("act", _act(rhsa[k][:, h, :], tri, AF.Copy, scale=ac[:, c4, 8 * g + h:8 * g + h + 1]),
                             reads=["tri", "ac"], writes=["rhsa%s_%d" % (kk, h // 4)])
                    if NA < 8:
                        T.op("dve", _tt(rhsa[k][:, NA:8, :], tri.unsqueeze(1).broadcast_to([128, 8 - NA, 128]),
                                        ac[:, c4, 8 * g + NA:8 * g + 8].unsqueeze(2).broadcast_to([128, 8 - NA, 128]), ALU.mult),
                             reads=["tri", "ac"], writes=["rhsa%s_1" % kk] + (["rhsa%s_0" % kk] if NA < 4 else []))
                    for j in range(4):
                        T.op("pe", _tr(psb[2][:, j * 128:(j + 1) * 128], XT[:, 4 * g + j, csl], ident),
                             reads=[xn, "ident"], writes=["ps2"])
                    T.op("pe", _tr(psb[2][:, 512:640], XT[:, 16 + g, csl], ident), reads=["XTbc", "ident"], writes=["ps2"])
                    xtm = psb[2][:, 0:512].rearrange("p (h d) -> p h d", h=8)
                    T.op("dve", _tt(xdt[k].rearrange("p (h d) -> p h d", h=8), xtm,
                                    dtc[:, c4, hs].unsqueeze(2).broadcast_to([128, 8, 64]), ALU.mult),
                         reads=["ps2", "dtc"], writes=["xdt" + kk])
                    T.op("dve", _tt(xD[k].rearrange("p (h d) -> p h d", h=8), xtm,
                                    D_bc[:, hs].unsqueeze(2).broadcast_to([128, 8, 64]), ALU.mult),
                         reads=["ps2", "D"], writes=["xD" + kk])
                    T.op("act", _acopy(Btm[k], psb[2][:, 512:640]), reads=["ps2"], writes=["Btm" + kk])
                    T.op(PENG, _tt(xdtd[k].rearrange("p (h d) -> p h d", h=8),
                                     xdt[k].rearrange("p (h d) -> p h d", h=8),
                                     dend[:, c4, hs].unsqueeze(2).broadcast_to([128, 8, 64]), ALU.mult),
                         reads=["xdt" + kk, "dend"], writes=["xdtd" + kk])
                    T.op("pe", _mm(ps[3][:, 256:384], XT[:, 16 + g, csl], XT[:, 20 + g, csl], True, True),
                         reads=["XTbc"], writes=["ps3g"])
                    T.op("act", _acopy(Gm[k], ps[3][:, 256:384]), reads=["ps3g"], writes=["Gm" + kk])
                    for q in range(2):
                        T.op("pe", _mm(ps[4 + q], Umat, rhsa[k][:, 4 * q:4 * q + 4, :].rearrange("p h l -> p (h l)"),
                                       True, False), reads=["U", "rhsa%s_%d" % (kk, q)], writes=["ps%d" % (4 + q)])
                        T.op("pe", _mm(ps[4 + q], ident, NEGM.rearrange("p h l -> p (h l)"), False, True),
                             reads=["ident", "NEGM"], writes=["ps%d" % (4 + q)])
                        T.op("act", _act(Es[q], ps[4 + q], AF.Exp), reads=["ps%d" % (4 + q)], writes=["Es%d" % q])
                        T.op("dve", _tt(MT[k][:, 4 * q:4 * q + 4, :], Es[q].rearrange("p (h l) -> p h l", h=4),
                                        Gm[k].unsqueeze(1).broadcast_to([128, 4, 128]), ALU.mult),
                             reads=["Es%d" % q, "Gm" + kk], writes=["MT" + kk])

                def ssd_back(i):
                    g, c4 = its[i]
                    k = i % 2
                    kk = str(k)
                    hs = slice(8 * g, 8 * g + 8)
                    csl = slice(c4 * 128, (c4 + 1) * 128)
                    xn = "XT%d_%d" % (g, c4)
                    T.op("pe", _c(lambda e, k=k: e.matmul(ps[6], ident, xD[k], start=True, stop=False, skip_group_check=True), 0.216),
                         reads=["ident", "xD" + kk], writes=["ps6"])
                    for h in range(8):
                        T.op("pe", _c(lambda e, k=k, h=h: e.matmul(ps[6][:, h * 64:(h + 1) * 64], MT[k][:, h, :],
                                                                   xdt[k][:, h * 64:(h + 1) * 64], start=False, stop=True,
                                                                   skip_group_check=True), 0.096),
                             reads=["MT" + kk, "xdt" + kk], writes=["ps6"])
                    T.op("pe", _mm(ps[7], XT[:, 20 + g, csl], Sbf[:, g, :], True, True),
                         reads=["XTbc", "Sb%d" % g], writes=["ps7"])
                    T.op("pe", _mm(ps[1], Btm[k], xdtd[k], True, True), reads=["Btm" + kk, "xdtd" + kk], writes=["ps1"])
                    T.op("dve", _tt(t1[k].rearrange("p (h d) -> p h d", h=8), ps[7].rearrange("p (h d) -> p h d", h=8),
                                    dstart[:, c4, hs].unsqueeze(2).broadcast_to([128, 8, 64]), ALU.mult),
                         reads=["ps7", "dstart"], writes=["t1" + kk])
                    T.op("dve", _tt(t1[k], ps[6], t1[k], ALU.add), reads=["ps6", "t1" + kk], writes=["t1" + kk])
                    T.op("dve", _tt(ug[k], t1[k], zsall[:, i, :], ALU.mult), reads=["t1" + kk, "zs%d" % i], writes=["ug" + kk])
                    T.op("act", _act(usq, ug[k], AF.Square, accum_out=ssq[k][:, 0:1]), reads=["ug" + kk],
                         writes=["usq", "ssq" + kk])
                    T.op("act", _act(ssq[k][:, 1:2], ssq[k][:, 0:1], AF.Ln, bias=epsr[:, 0:1], scale=1.0 / 512.0),
                         reads=["ssq" + kk, "epsr"], writes=["ssqb" + kk])
                    T.op("act", _act(ssq[k][:, 3:4], ssq[k][:, 1:2], AF.Exp, scale=-0.5), reads=["ssqb" + kk], writes=["ssqd" + kk])
                    T.op("dve", _stt(yb[k], ug[k], ssq[k][:, 3:4], nw_bc[:, g * 512:(g + 1) * 512], ALU.mult, ALU.mult),
                         reads=["ug" + kk, "ssqd" + kk, "nw"], writes=["yb" + kk])
                    Sg = Sst[:, g, :]
                    T.op(PENG, _tt(Sg.rearrange("p (h d) -> p h d", h=8), Sg.rearrange("p (h d) -> p h d", h=8),
                                     cdec[:, c4, hs].unsqueeze(2).broadcast_to([128, 8, 64]), ALU.mult),
                         reads=["S%d" % g, "cdec"], writes=["S%d" % g])
                    T.op("dve", _tt(Sg, ps[1], Sg, ALU.add), reads=["ps1", "S%d" % g], writes=["S%d" % g])
                    T.op("act", _acopy(Sbf[:, g, :], Sg), reads=["S%d" % g], writes=["Sb%d" % g])
                    for j in range(4):
                        T.op("pe", _tr(psb[3][:, j * 128:(j + 1) * 128], yb[k][:, j * 128:(j + 1) * 128], ident),
                             reads=["yb" + kk, "ident"], writes=["ps3t"])
                    T.op("act", _acopy(XT[:, 4 * g:4 * g + 4, csl], psb[3][:, 0:512].rearrange("p (j t) -> p j t", j=4)),
                         reads=["ps3t"], writes=[xn])

                ssd_front(0)
                for i in range(16):
                    if i + 1 < 16:
                        ssd_front(i + 1)
                    ssd_back(i)

                if debug == "yssm":
                    for j in range(16):
                        finals.append(T.dma("pool", "dbg", _dma(dbg_d[s, j * 128:(j + 1) * 128, tsl], XT[:, j, :]),
                                            reads=["XT%d_%d" % (j // 4, q4) for q4 in range(4)]))
                T.barrier()

                for j in range(8):
                    dsl = slice(j * 128, (j + 1) * 128)
                    (wb0, wb1, wga, wgb), wr = W.next([(w_br_v[:, 0:11, dsl], [11, 128]), (w_br_v[:, 11:22, dsl], [11, 128]),
                                                 (w_in_v[:, :, GM0 + j * 128: GM0 + (j + 1) * 128], [8, 128]),
                                                 (w_in_v[:, :, GM0 + 1024 + j * 128: GM0 + 1024 + (j + 1) * 128], [8, 128])], cached=True)
                    o = 4 * (j % 2)
                    pn = ["ps%d" % (o + q) for q in range(4)]
                    for i in range(6):
                        T.op("pe", _mm(ps[o], wb0[:, i, :], oattT[:, i, tsl], i == 0, i == 5), reads=[wr, "oattT"], writes=[pn[0]])
                    for i in range(16):
                        T.op("pe", _mm(ps[o + 1], (wb0[:, 6 + i, :] if i < 5 else wb1[:, i - 5, :]), XT[:, i, :], i == 0, i == 15),
                             reads=[wr] + ["XT%d_%d" % (i // 4, q4) for q4 in range(4)], writes=[pn[1]])
                    for c in range(8):
                        T.op("pe", _mm(ps[o + 2], wga[:, c, :], xTb[:, c, :], c == 0, c == 7), reads=[wr, XB], writes=[pn[2]])
                    for c in range(8):
                        T.op("pe", _mm(ps[o + 3], wgb[:, c, :], xTb[:, c, :], c == 0, c == 7), reads=[wr, XB], writes=[pn[3]])
                    k = j % 2
                    kk = str(k)
                    T.op("act", _act(sa[k], ps[o + 2], AF.Sigmoid, bias=bg[:, 0, j:j + 1]), reads=[pn[2], "bg"], writes=["sa" + kk])
                    T.op("dve", _tt(m1[k], ps[o], sa[k], ALU.mult), reads=[pn[0], "sa" + kk], writes=["m1" + kk])
                    T.op("act", _act(sgt[k], ps[o + 3], AF.Sigmoid, bias=bg[:, 1, j:j + 1]), reads=[pn[3], "bg"], writes=["sgt" + kk])
                    T.op("dve", _tt(tpt[k], ps[o + 1], sgt[k], ALU.mult), reads=[pn[1], "sgt" + kk], writes=["tpt" + kk])
                    T.op(PENG, _tt(mergedT[:, j, :], m1[k], tpt[k], ALU.add), reads=["m1" + kk, "tpt" + kk], writes=["mergedT"])
                for half in range(2):
                    hsl = slice(half * 512, (half + 1) * 512)
                    (wo,), wr = W.next([(w_out_v[:, :, hsl], [8, 512])], cached=True)
                    for tt in range(4):
                        tts = slice(tt * 128, (tt + 1) * 128)
                        b = tt % 2
                        for j in range(8):
                            T.op("pe", _mm(ps[b], mergedT[:, j, tts], wo[:, j, :], j == 0, j == 7),
                                 reads=[wr, "mergedT"], writes=["ps%d" % b])
                        T.op("dve", _stt(xres[:, tt, hsl], xres[:, tt, hsl], ALPHA, ps[b], ALU.mult, ALU.add),
                             reads=["ps%d" % b, "xres"], writes=["xres"])
                    (wgp, wpl), wr = W.next([(w_in_v[:, :, GPLE0 + half * 512: GPLE0 + (half + 1) * 512], [8, 512]),
                                             (w_ple_v[:, :, hsl], [2, 512])], cached=True)
                    for tt in range(4):
                        tts = slice(tt * 128, (tt + 1) * 128)
                        k = tt % 2
                        kk = str(k)
                        for c in range(8):
                            T.op("pe", _mm(ps[2 + k], xTb[:, c, tts], wgp[:, c, :], c == 0, c == 7),
                                 reads=[wr, XB], writes=["ps%d" % (2 + k)])
                        for c in range(2):
                            T.op("pe", _mm(ps[4 + k], pTb[:, c, tts], wpl[:, c, :], c == 0, c == 1),
                                 reads=[wr, PB], writes=["ps%d" % (4 + k)])
                        T.op("dve", _tt(sa[k], ps[2 + k], b2_bc[:, hsl], ALU.add), reads=["ps%d" % (2 + k), "b2"], writes=["sa" + kk])
                        T.op("act", _act(sgt[k], sa[k], AF.Sigmoid), reads=["sa" + kk], writes=["sgt" + kk])
                        T.op("dve", _tt(tpt[k], ps[4 + k], sgt[k], ALU.mult), reads=["ps%d" % (4 + k), "sgt" + kk], writes=["tpt" + kk])
                        T.op("dve", _tt(xres[:, tt, hsl], xres[:, tt, hsl], tpt[k], ALU.add), reads=["xres", "tpt" + kk], writes=["xres"])
                T.barrier()
                for tt in range(4):
                    k = tt % 2
                    kk = str(k)
                    r = xres[:, tt, :]
                    rn = "xres%d" % tt
                    st_ = lnst[k]
                    T.op("dve", lambda e, st_=st_, r=r: e.reduce_sum(out=st_[:, 0:1], in_=r, axis=AX.X), reads=["xres", rn], writes=["lnst" + kk])
                    T.op("dve", _ts(st_[:, 1:2], st_[:, 0:1], 1.0 / 1024.0, None, ALU.mult), reads=["lnst" + kk], writes=["lnstb" + kk])
                    T.op("dve", _ts(r, r, st_[:, 1:2], None, ALU.subtract), reads=["xres", rn, "lnstb" + kk], writes=[rn])
                    T.op("act", _act(lnsq, r, AF.Square, accum_out=st_[:, 2:3]), reads=[rn], writes=["lnsq", "lnstc" + kk])
                    T.op("act", _act(st_[:, 3:4], st_[:, 2:3], AF.Ln, bias=epsln[:, 0:1], scale=1.0 / 1024.0),
                         reads=["lnstc" + kk, "epsln"], writes=["lnstd" + kk])
                    T.op("act", _act(st_[:, 0:1], st_[:, 3:4], AF.Exp, scale=-0.5), reads=["lnstd" + kk], writes=["lnst" + kk])
                    T.op("dve", _stt(r, r, st_[:, 0:1], lng_bc, ALU.mult, ALU.mult), reads=[rn, "lnst" + kk, "lng"], writes=[rn])
                    T.op("dve", _tt(r, r, lnb_bc, ALU.add), reads=[rn, "lnb"], writes=[rn])
                    t0 = blk * 512 + tt * 128
                    finals.append(T.dma("sp", "st%d" % k, _dma(out_d[s, t0:t0 + 128, :], r), reads=[rn]))

        for s in range(NSEQ):
            attention(s)
            T.barrier()
            if debug != "oatt":
                stream(s)
                T.barrier()
        return finals

    T0 = Tracker(nc)
    W0 = WRing(T0, wslots, None, scratch=wscr)
    construct(T0, W0)
    assert len(W0.cache) <= NCACHE, len(W0.cache)
    T = Tracker(nc)
    W = WRing(T, wslots, W0.plan, cache=W0.cache, scratch=wscr)
    finals = construct(T, W)
    assert W.cur == len(W0.plan) and W.issued == len(W0.plan)
    if os.environ.get("MK_NOSCHED") is None:
        T.schedule()
    T.emit(finals)
    return nc


def _t5_bucket_np(dist):
    max_exact = 16
    d_f = np.maximum(dist, 1).astype(np.float32)
    large = max_exact + (np.log(d_f / np.float32(max_exact)) / np.float32(math.log(2048 / max_exact))
                         * np.float32(32 - max_exact)).astype(np.int32)
    large = np.minimum(large, 31)
    return np.where(dist < max_exact, dist, large)


def _host_consts():
    ki = np.arange(128)[:, None]
    qi = np.arange(128)[None, :]
    d_cur = qi - ki
    d_nxt = qi + 128 - ki
    delta = np.concatenate([d_cur, d_nxt], axis=1)
    valid = (delta >= 0) & (delta <= 128)
    maskT = valid.astype(np.float32)
    idx = np.stack([_t5_bucket_np(np.maximum(delta, 0) * d) for d in DILS])
    p = np.arange(128)[:, None]
    f = np.arange(128)[None, :]
    cmat = np.stack([(p == f), (p <= f), (p > f), np.ones((128, 128), bool)]).astype(np.float32)
    return maskT, idx, cmat


_NC_CACHE = {}


def kernel(x, p, w_in, b_gate, conv_w, conv_b, dt_bias, a_log, d_skip, ssm_norm_w,
           w_branch, w_out, w_ple, ln_g, ln_b, rel_bias):
    debug = os.environ.get("MK_DEBUG") or None
    f32 = np.float32
    x = np.asarray(x, f32)
    p = np.asarray(p, f32)[0]
    maskT, idx, cmat = _host_consts()
    rel_bias = np.asarray(rel_bias, f32)
    biasT = np.stack([rel_bias[idx[hh // 12], hh] for hh in range(36)]).astype(f32)
    cw = np.ascontiguousarray(np.asarray(conv_w, f32)[0].T.reshape(24, 128, 4).transpose(1, 0, 2))
    cb = np.ascontiguousarray(np.asarray(conv_b, f32)[0].reshape(24, 128).T)
    bgate = np.asarray(b_gate, f32)[0]
    bg = np.ascontiguousarray(bgate[0:2].reshape(2, 8, 128).transpose(2, 0, 1))
    shared = {
        "w_in": np.ascontiguousarray(np.asarray(w_in, f32)[0]),
        "w_branch": np.ascontiguousarray(np.asarray(w_branch, f32)[0]),
        "w_out": np.ascontiguousarray(np.asarray(w_out, f32)[0]),
        "w_ple": np.ascontiguousarray(np.asarray(w_ple, f32)[0]),
        "biasT": biasT, "maskT": maskT, "cmat": cmat, "cw": cw, "cb": cb, "bg": bg,
        "b2": np.ascontiguousarray(bgate[2]),
        "dt_bias": np.ascontiguousarray(np.asarray(dt_bias, f32)[0]),
        "a_log": np.ascontiguousarray(np.asarray(a_log, f32)[0]),
        "d_skip": np.ascontiguousarray(np.asarray(d_skip, f32)[0]),
        "ssm_norm_w": np.ascontiguousarray(np.asarray(ssm_norm_w, f32)[0]),
        "ln_g": np.ascontiguousarray(np.asarray(ln_g, f32)[0]),
        "ln_b": np.ascontiguousarray(np.asarray(ln_b, f32)[0]),
    }
    in_maps = []
    for c in range(N_CORES):
        xs = x[c * NSEQ:(c + 1) * NSEQ]
        m = dict(shared)
        m["x"] = np.ascontiguousarray(xs)
        m["xT"] = np.ascontiguousarray(xs.transpose(0, 2, 1))
        m["pT"] = np.ascontiguousarray(p[c * NSEQ:(c + 1) * NSEQ].transpose(0, 2, 1))
        in_maps.append(m)
    if debug not in _NC_CACHE:
        _NC_CACHE[debug] = build_program(debug)
    nc = _NC_CACHE[debug]
    ncores = int(os.environ.get("MK_CORES", N_CORES))
    res = run_bass_kernel_spmd(nc, in_maps[:ncores], core_ids=list(range(ncores)))
    if debug:
        return [r["dbg"] for r in res.results]
    return np.concatenate([r["out"] for r in res.results], axis=0).astype(f32)
```

```python
import math
import os
import contextlib
import numpy as np
import concourse.bass as bass
import concourse.mybir as mybir
from concourse.bass_utils import run_bass_kernel_spmd

F32 = mybir.dt.float32
BF16 = mybir.dt.bfloat16
U8 = mybir.dt.uint8
AF = mybir.ActivationFunctionType
ALU = mybir.AluOpType
AX = mybir.AxisListType

N_CORES = 8
D_MODEL = 1024
SEQ = 2048
NSEQ = 2
K0, V0, GATT0, Z0, XBC0, DT0, GM0, GPLE0, IN_COLS = 2304, 4608, 6912, 7680, 9728, 12800, 12832, 14880, 15904
DILS = (1, 4, 16)
ALPHA = 2.0 ** 0.25
LN_EPS = 1e-5
RMS_EPS = 1e-5
WSLOT = 5120
NRING = 3
RHSA_ACT = int(os.environ.get("MK_RHSA_ACT", "0"))
PENG = "dve"


class Node:
    __slots__ = ("eng", "idx", "fn", "deps", "signal", "sigval", "dma", "cost", "lat", "gidx", "fin", "res", "odeps", "epoch", "table")

    def __init__(self, eng, idx, fn, dma=None):
        self.eng = eng
        self.idx = idx
        self.fn = fn
        self.cost = getattr(fn, "cost", 0.5)
        self.lat = getattr(fn, "lat", 0.0)
        self.table = getattr(fn, "table", None)
        self.gidx = 0
        self.fin = None
        self.res = ()
        self.odeps = []
        self.deps = []
        self.signal = False
        self.sigval = None
        self.dma = dma


class Tracker:
    ENGS = ("pe", "act", "dve", "pool", "sp")

    def __init__(self, nc):
        self.nc = nc
        self.ops = {e: [] for e in self.ENGS}
        self.lastw = {}
        self.readers = {}
        self.dma_cnt = {}
        self.dma_latest = {}
        self.bank_last = {}
        self.pending = {e: [] for e in self.ENGS}

    def _add(self, node, reads, writes):
        self.gcount = getattr(self, "gcount", 0) + 1
        node.gidx = self.gcount
        node.epoch = getattr(self, "epoch", 0)
        node.res = (tuple(reads), tuple(writes))
        deps = {}

        def add_dep(n):
            if n is not None and n is not node:
                deps[id(n)] = n

        for r in reads:
            add_dep(self.lastw.get(r))
        for w in writes:
            add_dep(self.lastw.get(w))
            for n in self.readers.get(w, {}).values():
                add_dep(n)
        banks = {int(r[2]) for r in list(reads) + list(writes) if r.startswith("ps") and r[2].isdigit()}
        for b in banks:
            bl = self.bank_last.setdefault(b, {})
            for e, n in bl.items():
                if e != node.eng:
                    add_dep(n)
            bl[node.eng] = node
        for n in self.pending[node.eng]:
            add_dep(n)
        self.pending[node.eng] = []
        node.deps = list(deps.values())
        key = ("dma", node.dma[0]) if node.dma else node.eng
        for r in reads:
            self.readers.setdefault(r, {})[key] = node
        for w in writes:
            self.lastw[w] = node
            self.readers[w] = {}

    def op(self, eng, fn, reads=(), writes=()):
        node = Node(eng, len(self.ops[eng]), fn)
        self.ops[eng].append(node)
        self._add(node, reads, writes)
        return node

    def dma(self, eng, slot, fn, reads=(), writes=()):
        self.dma_cnt[slot] = self.dma_cnt.get(slot, 0) + 16
        node = Node(eng, len(self.ops[eng]), fn, dma=(slot, self.dma_cnt[slot]))
        self.ops[eng].append(node)
        self._add(node, reads, writes)
        self.dma_latest[slot] = node
        return node

    def barrier(self):
        self.epoch = getattr(self, "epoch", 0) + 1
        last = [self.ops[e][-1] for e in self.ENGS if self.ops[e]]
        last += [n for sl, n in self.dma_latest.items() if sl != "cv"]
        for e in self.ENGS:
            self.pending[e] = list(last)

    def schedule(self, window=48, vis=0.15):
        allnodes = sorted((n for e in self.ENGS for n in self.ops[e]), key=lambda n: n.gidx)
        wcount = {}
        for n in allnodes:
            for w in n.res[1]:
                wcount[w] = wcount.get(w, 0) + 1
        last_acc = {}
        for n in allnodes:
            keys = set()
            for r in n.res[0] + n.res[1]:
                if wcount.get(r, 0) >= 2:
                    keys.add(r)
                if r.startswith("ps") and r[2].isdigit():
                    keys.add(("bank", int(r[2])))
            for k in keys:
                p = last_acc.get((k, n.eng))
                if p is not None:
                    n.odeps.append(p)
                last_acc[(k, n.eng)] = n
        rem = {e: list(self.ops[e]) for e in self.ENGS}
        out = {e: [] for e in self.ENGS}
        free = {e: 0.0 for e in self.ENGS}
        cur_epoch = {e: -1 for e in self.ENGS}
        cur_tab = [None]
        nleft = sum(len(v) for v in rem.values())
        while nleft:
            best = None
            for e in self.ENGS:
                lst = rem[e]
                if not lst:
                    continue
                cand = None
                wnd = lst[:window] if lst[0].epoch == cur_epoch[e] else lst[:1]
                for n in wnd:
                    if n.epoch != lst[0].epoch:
                        break
                    rdy = 0.0
                    ok = True
                    for d in n.odeps:
                        if d.fin is None:
                            ok = False
                            break
                    if not ok:
                        continue
                    for d in n.deps:
                        if d.fin is None:
                            ok = False
                            break
                        if d.eng == "pe" and e == "pe" and d.dma is None and n.dma is None:
                            continue
                        if d.fin + vis > rdy:
                            rdy = d.fin + vis
                    if not ok:
                        continue
                    st = max(rdy, free[e])
                    pen = 0.0
                    if e == "act" and n.table is not None and not (n.table == cur_tab[0] or (n.table == "E" and cur_tab[0] == "L")):
                        pen = 1.3
                    if cand is None or st + pen < cand[0] - 1e-9:
                        cand = (st + pen, n)
                    if st + pen <= free[e] + 1e-9:
                        break
                if cand is not None and (best is None or cand[0] < best[0] - 1e-9 or
                                         (abs(cand[0] - best[0]) <= 1e-9 and cand[1].gidx < best[1].gidx)):
                    best = cand
            st, n = best
            e = n.eng
            if e == "act" and n.table is not None and not (n.table == cur_tab[0] or (n.table == "E" and cur_tab[0] == "L")):
                cur_tab[0] = n.table
            rem[e].remove(n)
            out[e].append(n)
            cur_epoch[e] = n.epoch
            free[e] = st + n.cost
            n.fin = st + n.cost + n.lat
            nleft -= 1
        self.ops = out
        self.est_us = max(free.values())

    def emit(self, final_nodes):
        nc = self.nc
        for e in self.ENGS:
            for n in self.ops[e]:
                for d in n.deps:
                    if d.dma is None:
                        if d.eng == "pe" and n.eng == "pe" and n.dma is None:
                            continue
                        d.signal = True
        for n in final_nodes:
            if n.dma is None:
                n.signal = True
        for e in self.ENGS:
            c = 0
            for n in self.ops[e]:
                if n.dma is None and n.signal:
                    c += 1
                    n.sigval = c
        with contextlib.ExitStack() as st:
            esem = {e: st.enter_context(nc.semaphore("s_" + e)) for e in self.ENGS}
            dsem = {s: st.enter_context(nc.semaphore("d_" + s)) for s in self.dma_cnt}
            block = st.enter_context(nc.Block())

            def run(ename, eng):
                waited = {}
                for n in self.ops[ename]:
                    need = {}
                    for d in n.deps:
                        if d.dma is not None:
                            k, v = ("d", d.dma[0]), d.dma[1]
                        else:
                            if d.eng == "pe" and ename == "pe" and n.dma is None:
                                continue
                            k, v = ("e", d.eng), d.sigval
                        if v > need.get(k, 0):
                            need[k] = v
                    for k, v in need.items():
                        if waited.get(k, 0) >= v:
                            continue
                        waited[k] = v
                        eng.wait_ge(dsem[k[1]] if k[0] == "d" else esem[k[1]], v)
                    ins = n.fn(eng)
                    if n.dma is not None:
                        ins.then_inc(dsem[n.dma[0]], 16)
                    elif n.signal:
                        ins.then_inc(esem[ename], 1)
                if ename == "sp":
                    fin = {}
                    for n in final_nodes:
                        k, v = (("d", n.dma[0]), n.dma[1]) if n.dma is not None else (("e", n.eng), n.sigval)
                        fin[k] = max(fin.get(k, 0), v)
                    for k, v in fin.items():
                        eng.wait_ge(dsem[k[1]] if k[0] == "d" else esem[k[1]], v)

            block.tensor(lambda t: run("pe", t))
            block.scalar(lambda s: run("act", s))
            block.vector(lambda v: run("dve", v))
            block.gpsimd(lambda g: run("pool", g))
            block.sync(lambda sy: run("sp", sy))


class Arena:
    def __init__(self, nc, nbytes):
        self.ap = nc.alloc_sbuf_tensor("arena", [128, nbytes], U8).ap()
        self.nbytes = nbytes
        self.off = 0
        self.peak = 0

    def alloc(self, free, dt):
        esz = 4 if dt == F32 else 2
        n = int(np.prod(free)) * esz
        v = self.ap[:, self.off:self.off + n].bitcast(dt)
        self.off += (n + 31) // 32 * 32
        self.peak = max(self.peak, self.off)
        assert self.off <= self.nbytes, ("SBUF arena overflow", self.off)
        if len(free) == 2:
            v = v.rearrange("p (a b) -> p a b", a=free[0])
        elif len(free) == 3:
            v = v.rearrange("p (a b c) -> p a b c", a=free[0], b=free[1])
        return v


class WRing:
    def __init__(self, T, slots, reqs, cache=None, scratch=None):
        self.T = T
        self.slots = slots
        self.reqs = reqs
        self.plan = []
        self.cur = 0
        self.issued = 0
        self.cidx = 0
        self.cache = cache if cache is not None else {}
        self.scratch = scratch

    def block_start(self):
        self.cidx = 0

    def _views(self, base, parts):
        off = 0
        views = []
        for _, free in parts:
            n = int(np.prod(free))
            v = base[:, off:off + n]
            if len(free) == 2:
                v = v.rearrange("p (a b) -> p a b", a=free[0])
            views.append(v)
            off += n
        assert off <= WSLOT, off
        return views, off

    def emit_conversions(self, lo=0, hi=10 ** 9):
        for ci in sorted(self.cache):
            if not (lo <= ci < hi):
                continue
            parts = self.cache[ci]
            vs, _ = self._views(self.scratch[ci], parts)
            for (src, _), v in zip(parts, vs):
                self.T.dma("pool", "cv", _dma(v, src), writes=["wsc"])

    def next(self, parts, cached=False):
        i = self.cur
        self.cur += 1
        ci = None
        if cached:
            ci = self.cidx
            self.cidx += 1
            if self.reqs is None and ci not in self.cache:
                self.cache[ci] = parts
        self.plan.append((parts, ci))
        if self.reqs is not None:
            while self.issued < min(len(self.reqs), i + NRING):
                j = self.issued
                rparts, rci = self.reqs[j]
                slot = self.slots[j % NRING]
                wn = "w%d" % (j % NRING)
                if rci is None:
                    vs, _ = self._views(slot, rparts)
                    for (src, _), v in zip(rparts, vs):
                        self.T.dma("pool", wn, _dma(v, src), writes=[wn])
                else:
                    _, tot = self._views(slot, rparts)
                    self.T.dma("sp", "v%d" % (j % NRING), _dma(slot[:, 0:tot], self.scratch[rci][:, 0:tot]),
                               reads=["wsc"], writes=[wn])
                self.issued += 1
        return self._views(self.slots[i % NRING], parts)[0], "w%d" % (i % NRING)


def _nfree(ap):
    n = 1
    for d in ap.shape[1:]:
        n *= int(d)
    return n


def _c(f, cost):
    f.cost = cost
    return f


def _dma(out, in_):
    f = lambda e: e.dma_start(out=out, in_=in_)
    f.cost = 1.0
    f.lat = 2.5 + _nfree(out) * 128 * (4 if in_.dtype == F32 else 2) / 200e3
    return f


def _mm(out, lhsT, rhs, start, stop):
    n = _nfree(rhs)
    mult = 4.0 if rhs.dtype == F32 else 1.0
    return _c(lambda e: e.matmul(out, lhsT, rhs, start=start, stop=stop), mult * max(n / 2400.0 + 0.003, 0.096))


def _tr(out, in_, ident):
    return _c(lambda e: e.transpose(out=out, in_=in_, identity=ident), 0.1)


def _act(out, in_, func, bias=None, scale=None, accum_out=None):
    kw = {}
    if bias is not None:
        kw["bias"] = bias
    if scale is not None:
        kw["scale"] = scale
    if accum_out is not None:
        kw["accum_out"] = accum_out
    f = _c(lambda e: e.activation(out=out, in_=in_, func=func, **kw), 0.12 + _nfree(out) / 1000.0)
    f.table = {AF.Silu: "S", AF.Sigmoid: "G", AF.Ln: "L", AF.Exp: "E"}.get(func)
    return f


def _tt(out, in0, in1, op):
    fast = out.dtype == BF16 and in0.dtype == BF16 and in1.dtype == BF16
    return _c(lambda e: e.tensor_tensor(out=out, in0=in0, in1=in1, op=op),
              (0.07 + _nfree(out) / 1950.0) if fast else (0.1 + _nfree(out) / 850.0))


def _ts(out, in0, s1, s2, op0, op1=None):
    cost = 0.1 + _nfree(out) / 850.0
    if op1 is None:
        return _c(lambda e: e.tensor_scalar(out=out, in0=in0, scalar1=s1, scalar2=None, op0=op0), cost)
    return _c(lambda e: e.tensor_scalar(out=out, in0=in0, scalar1=s1, scalar2=s2, op0=op0, op1=op1), cost)


def _stt(out, in0, scalar, in1, op0, op1):
    return _c(lambda e: e.scalar_tensor_tensor(out=out, in0=in0, scalar=scalar, in1=in1, op0=op0, op1=op1),
              0.1 + _nfree(out) / 850.0)


def _copy(out, in_):
    return _c(lambda e: e.tensor_copy(out=out, in_=in_), 0.1 + _nfree(out) / 850.0)


def _acopy(out, in_):
    return _c(lambda e: e.copy(out=out, in_=in_), 0.12 + _nfree(out) / 1000.0)


def _memset(ap, val):
    return _c(lambda e: e.memset(ap, val), 0.1 + _nfree(ap) / 1700.0)


def _recip(out, in_):
    return _c(lambda e: e.reciprocal(out=out, in_=in_), 0.1 + _nfree(out) / 850.0)


def _interleave(A, B):
    out = []
    na, nb = len(A), len(B)
    ia = ib = 0
    while ia < na or ib < nb:
        if ib >= nb or (ia < na and ia * nb <= ib * na):
            out.append(A[ia])
            ia += 1
        else:
            out.append(B[ib])
            ib += 1
    return out


def build_program(debug=None):
    nc = bass.Bass("TRN2", target_bir_lowering=False)

    def din(name, shape):
        return nc.dram_tensor(name, list(shape), F32, kind="ExternalInput").ap()

    xT_d = din("xT", [NSEQ, D_MODEL, SEQ])
    x_d = din("x", [NSEQ, SEQ, D_MODEL])
    pT_d = din("pT", [NSEQ, 256, SEQ])
    w_in_d = din("w_in", [D_MODEL, IN_COLS])
    w_br_d = din("w_branch", [2816, D_MODEL])
    w_out_d = din("w_out", [D_MODEL, D_MODEL])
    w_ple_d = din("w_ple", [256, D_MODEL])
    biasT_d = din("biasT", [36, 128, 256])
    maskT_d = din("maskT", [128, 256])
    cmat_d = din("cmat", [4, 128, 128])
    cw_d = din("cw", [128, 24, 4])
    cb_d = din("cb", [128, 24])
    bg_d = din("bg", [128, 2, 8])
    b2_d = din("b2", [D_MODEL])
    dtb_d = din("dt_bias", [32])
    alog_d = din("a_log", [32])
    dsk_d = din("d_skip", [32])
    nw_d = din("ssm_norm_w", [2048])
    lng_d = din("ln_g", [D_MODEL])
    lnb_d = din("ln_b", [D_MODEL])
    out_d = nc.dram_tensor("out", [NSEQ, SEQ, D_MODEL], F32, kind="ExternalOutput").ap()
    dbg_d = None
    if debug == "oatt":
        dbg_d = nc.dram_tensor("dbg", [NSEQ, 768, SEQ], F32, kind="ExternalOutput").ap()
    elif debug == "yssm":
        dbg_d = nc.dram_tensor("dbg", [NSEQ, 2048, SEQ], F32, kind="ExternalOutput").ap()

    w_in_v = w_in_d.rearrange("(c p) n -> p c n", p=128)
    w_br_v = w_br_d.rearrange("(i p) d -> p i d", p=128)
    w_out_v = w_out_d.rearrange("(c p) n -> p c n", p=128)
    w_ple_v = w_ple_d.rearrange("(c p) n -> p c n", p=128)

    NCACHE = 24
    wscr = nc.dram_tensor("wscratch", [NCACHE, 128, WSLOT], BF16, kind="Internal").ap()
    ar = Arena(nc, 206 * 1024)
    ps = [nc.alloc_psum_tensor("ps%d" % i, [128, 512], F32).ap() for i in range(8)]
    psb = [p.bitcast(BF16) for p in ps]

    ident = ar.alloc([128], BF16)
    tri = ar.alloc([128], BF16)
    Umat = ar.alloc([128], BF16)
    tri32 = ar.alloc([128], F32)
    ones32 = ar.alloc([128], F32)
    dtb_bc = ar.alloc([32], F32)
    A_bc = ar.alloc([32], F32)
    D_bc = ar.alloc([32], F32)
    cw = ar.alloc([24, 4], F32)
    cb = ar.alloc([24], F32)
    bg = ar.alloc([2, 8], F32)
    epsln = ar.alloc([1], F32)
    NEGM = ar.alloc([4, 128], BF16)
    epsr = ar.alloc([1], F32)
    oattT = ar.alloc([6, SEQ], BF16)
    wslots = [ar.alloc([WSLOT], BF16) for _ in range(NRING)]
    base = ar.off

    xT = ar.alloc([8, SEQ], BF16)
    EBT = ar.alloc([36, 256], BF16)
    after_ebt = ar.off
    qT = [ar.alloc([SEQ], BF16) for _ in range(2)]
    kT = [ar.alloc([SEQ], BF16) for _ in range(2)]
    Vb = [ar.alloc([16, 4, 64], BF16) for _ in range(2)]
    Vn = [ar.alloc([16, 4, 64], BF16) for _ in range(3)]
    acc = [ar.alloc([SEQ], F32) for _ in range(2)]
    gS = [ar.alloc([SEQ], BF16) for _ in range(2)]
    Eb = [ar.alloc([512], BF16) for _ in range(2)]
    PT = [ar.alloc([512], BF16) for _ in range(4)]
    rden = ar.alloc([SEQ], F32)
    attn_end = ar.off

    ar.off = base
    XT = ar.alloc([24, 512], BF16)
    xTb2 = [ar.alloc([8, 512], BF16) for _ in range(2)]
    pTb2 = [ar.alloc([2, 512], BF16) for _ in range(2)]
    nw_bc = ar.alloc([2048], F32)
    b2_bc = ar.alloc([1024], F32)
    lng_bc = ar.alloc([1024], F32)
    lnb_bc = ar.alloc([1024], F32)
    Sst = ar.alloc([4, 512], F32)
    Sbf = ar.alloc([4, 512], BF16)
    hist = ar.alloc([24, 3], F32)
    dtc = ar.alloc([4, 32], F32)
    ac = ar.alloc([4, 32], F32)
    csb = ar.alloc([4, 32], F32)
    dstart = ar.alloc([4, 32], F32)
    dend = ar.alloc([4, 32], F32)
    cdec = ar.alloc([4, 32], F32)
    sm_t = ar.alloc([4, 32], F32)
    sm_e = ar.alloc([4, 32], F32)
    xres = ar.alloc([4, 1024], F32)
    zsall = ar.alloc([16, 512], BF16)
    lnsq = ar.alloc([1024], BF16)
    lnst = [ar.alloc([4], F32) for _ in range(2)]
    sub = ar.off
    uraw = [ar.alloc([515], F32) for _ in range(2)]
    ctmp = [ar.alloc([512], F32) for _ in range(2)]
    ar.off = sub
    xD = [ar.alloc([512], BF16) for _ in range(2)]
    xdt = [ar.alloc([512], BF16) for _ in range(2)]
    xdtd = [ar.alloc([512], BF16) for _ in range(2)]
    Btm = [ar.alloc([128], BF16) for _ in range(2)]
    Gm = [ar.alloc([128], BF16) for _ in range(2)]
    rhsa = [ar.alloc([8, 128], BF16) for _ in range(2)]
    Es = [ar.alloc([512], BF16) for _ in range(2)]
    MT = [ar.alloc([8, 128], BF16) for _ in range(2)]
    t1 = [ar.alloc([512], F32) for _ in range(2)]
    ug = [ar.alloc([512], F32) for _ in range(2)]
    usq = ar.alloc([512], F32)
    yb = [ar.alloc([512], BF16) for _ in range(2)]
    ssq = [ar.alloc([4], F32) for _ in range(2)]
    ssd_end = ar.off
    ar.off = sub
    mergedT = ar.alloc([8, 512], BF16)
    sa = [ar.alloc([512], F32) for _ in range(2)]
    m1 = [ar.alloc([512], F32) for _ in range(2)]
    sgt = [ar.alloc([512], F32) for _ in range(2)]
    tpt = [ar.alloc([512], F32) for _ in range(2)]
    tail_end = ar.off
    ar.off = after_ebt
    braw = ar.alloc([36, 256], F32)
    mraw = ar.alloc([256], F32)

    def construct(T, W):
        finals = []

        cm = cmat_d.rearrange("k p f -> p k f")
        T.dma("pool", "c0_0", _dma(ident, cm[:, 0, :]), writes=["ident"])
        T.dma("pool", "c0_1", _dma(tri, cm[:, 1, :]), writes=["tri"])
        T.dma("pool", "c0_2", _dma(Umat, cm[:, 2, :]), writes=["U"])
        T.dma("sp", "c1_3", _dma(tri32, cm[:, 1, :]), writes=["tri32"])
        T.dma("sp", "c1_4", _dma(ones32, cm[:, 3, :]), writes=["ones32"])
        T.dma("sp", "c1_5", _dma(dtb_bc, dtb_d.partition_broadcast(128)), writes=["dtb"])
        T.dma("sp", "c1_6", _dma(A_bc, alog_d.partition_broadcast(128)), writes=["A"])
        T.dma("sp", "c1_7", _dma(D_bc, dsk_d.partition_broadcast(128)), writes=["D"])
        T.dma("sp", "c1_8", _dma(cw, cw_d), writes=["cw"])
        T.dma("sp", "c1_9", _dma(cb, cb_d), writes=["cb"])
        T.dma("sp", "c1_10", _dma(bg, bg_d), writes=["bg"])
        T.op("dve", _memset(epsln, LN_EPS), writes=["epsln"])
        T.op("dve", _memset(epsr, RMS_EPS), writes=["epsr"])
        T.op("dve", _ts(NEGM, Umat.unsqueeze(1).broadcast_to([128, 4, 128]), -30000.0, None, ALU.mult),
             reads=["U"], writes=["NEGM"])
        T.op("act", _act(A_bc, A_bc, AF.Exp), reads=["A"], writes=["A"])
        T.op("dve", _ts(A_bc, A_bc, -1.0, None, ALU.mult), reads=["A"], writes=["A"])
        T.barrier()

        def attention(s):
            for c in range(8):
                T.dma("pool", "xT", _dma(xT[:, c, :], xT_d[s, c * 128:(c + 1) * 128, :]), writes=["xT"])
            for h6 in range(6):
                T.dma("sp", "c2", _dma(braw[:, h6 * 6:(h6 + 1) * 6, :], biasT_d[h6 * 6:(h6 + 1) * 6].rearrange("h k q -> k h q")),
                      writes=["braw"])
            T.dma("sp", "c7", _dma(mraw, maskT_d), writes=["mraw"])
            for h6 in range(6):
                hsl = slice(h6 * 6, (h6 + 1) * 6)
                T.op("act", _act(braw[:, hsl, :], braw[:, hsl, :], AF.Exp), reads=["braw"], writes=["braw"])
                T.op("dve", _tt(EBT[:, hsl, :], braw[:, hsl, :], mraw.unsqueeze(1).broadcast_to([128, 6, 256]), ALU.mult),
                     reads=["braw", "mraw"], writes=["EBT"])
            T.barrier()
            for bi in range(2):
                T.op("dve", _memset(Vb[bi][:, :, 1:3, :], 1.0), writes=["V%d" % bi])
            for gi in range(3):
                T.op("dve", _memset(Vn[gi][:, :, 1:3, :], 1.0), writes=["Vn%d" % gi])
            GROUPS = [int(c) for c in os.environ.get("MK_GROUPS", "012")]
            units = [(hp, g) for hp in range(6) for g in GROUPS]
            rot = {"ip": 0, "s": 0, "o": 0, "e": 0, "pt": 0}

            def inproj_steps(u, bi):
                hp, g = u
                D = DILS[g]
                nb = 16 // D
                steps = []
                st = {}

                def s_load():
                    parts = [(w_in_v[:, :, g * 768 + hp * 128 + off: g * 768 + hp * 128 + off + 128], [8, 128])
                             for off in (0, K0)]
                    if hp % 2 == 0:
                        parts.append((w_in_v[:, :, V0 + g * 768 + hp * 128: V0 + g * 768 + hp * 128 + 256], [8, 256]))
                    else:
                        parts.append((w_in_v[:, :, V0 + g * 768 + hp * 128: V0 + g * 768 + hp * 128 + 2], [8, 2]))
                    if g == GROUPS[0]:
                        parts.append((w_in_v[:, :, GATT0 + hp * 128: GATT0 + hp * 128 + 128], [8, 128]))
                    st["w"], st["wr"] = W.next(parts)
                steps.append(s_load)

                def qk_step(which, tb):
                    def f():
                        wv = st["w"][which]
                        b = 4 + rot["ip"] % 4
                        rot["ip"] += 1
                        pr = "ps%d" % b
                        for c in range(8):
                            T.op("pe", _mm(ps[b], wv[:, c, :], xT[:, c, tb * 512:(tb + 1) * 512], c == 0, c == 7),
                                 reads=[st["wr"], "xT"], writes=[pr])
                        dst = (qT if which == 0 else kT)[bi]
                        dv = dst.rearrange("p (r m) -> p r m", r=D)[:, :, tb * (512 // D):(tb + 1) * (512 // D)]
                        sv = ps[b].rearrange("p (m r) -> p r m", r=D)
                        name = ("q%d" if which == 0 else "k%d") % bi
                        if which == 0:
                            T.op("act", lambda e, dv=dv, sv=sv: e.mul(out=dv, in_=sv, mul=0.125), reads=[pr], writes=[name])
                        else:
                            T.op("act", _acopy(dv, sv), reads=[pr], writes=[name])
                    return f
                for which in (0, 1):
                    for tb in range(4):
                        steps.append(qk_step(which, tb))

                def v_step(kb2):
                    def f():
                        wv = st["w"][2]
                        b = 4 + rot["ip"] % 4
                        rot["ip"] += 1
                        pr = "ps%d" % b
                        for kk in range(2):
                            kbp = kb2 * 2 + kk
                            r, n = kbp // nb, kbp % nb
                            t0 = r + D * 128 * n
                            for c in range(8):
                                T.op("pe", _mm(ps[b][:, kk * 256:(kk + 1) * 256], xT[:, c, t0:t0 + D * 127 + 1:D],
                                               wv[:, c, :], c == 0, c == 7),
                                     reads=[st["wr"], "xT"], writes=[pr])
                        sv = ps[b].rearrange("p (k q h d) -> p k q h d", k=2, q=2, h=2)
                        T.op("dve", _copy(Vb[bi][:, kb2 * 2:(kb2 + 1) * 2, 0:4:3, :], sv[:, :, 0, :, :]), reads=[pr], writes=["V%d" % bi])
                        T.op("dve", _copy(Vn[g][:, kb2 * 2:(kb2 + 1) * 2, 0:4:3, :], sv[:, :, 1, :, :]), reads=[pr], writes=["Vn%d" % g])
                    return f
                if hp % 2 == 0:
                    for kb2 in range(8):
                        steps.append(v_step(kb2))

                if g == GROUPS[0]:
                    def g_step(tb):
                        def f():
                            wv = st["w"][3]
                            b = 4 + rot["ip"] % 4
                            rot["ip"] += 1
                            pr = "ps%d" % b
                            for c in range(8):
                                T.op("pe", _mm(ps[b], wv[:, c, :], xT[:, c, tb * 512:(tb + 1) * 512], c == 0, c == 7),
                                     reads=[st["wr"], "xT"], writes=[pr])
                            T.op("act", _act(gS[hp % 2][:, tb * 512:(tb + 1) * 512], ps[b], AF.Silu),
                                 reads=[pr], writes=["gS%d" % (hp % 2)])
                        return f
                    for tb in range(4):
                        steps.append(g_step(tb))
                return steps

            def attend_steps(u, bi):
                hp, g = u
                D = DILS[g]
                nb = 16 // D
                m256 = nb > 1
                nbank = 8 if m256 else 4
                items = [(hd, i) for hd in range(2) for i in range(nbank)]
                info = {}
                steps = []

                def S_rec(k):
                    hd, i = items[k]
                    rows = slice(hd * 64, hd * 64 + 64)
                    b = rot["s"] % 2
                    rot["s"] += 1
                    info[k] = {"sb": b}
                    pr = "ps%d" % b
                    if m256:
                        for kk in range(2):
                            kb = 2 * i + kk
                            N = 256 if (kb % nb) != nb - 1 else 128
                            T.op("pe", _mm(ps[b][:, kk * 256:kk * 256 + N], kT[bi][rows, kb * 128:(kb + 1) * 128],
                                           qT[bi][rows, kb * 128:kb * 128 + N], True, True),
                                 reads=["q%d" % bi, "k%d" % bi], writes=[pr])
                    else:
                        for kk in range(4):
                            kb = 4 * i + kk
                            T.op("pe", _mm(ps[b][:, kk * 128:(kk + 1) * 128], kT[bi][rows, kb * 128:(kb + 1) * 128],
                                           qT[bi][rows, kb * 128:(kb + 1) * 128], True, True),
                                 reads=["q%d" % bi, "k%d" % bi], writes=[pr])

                def rest_rec(k):
                    hd, i = items[k]
                    hh = g * 12 + 2 * hp + hd
                    b = info[k]["sb"]
                    eb = rot["e"] % 2
                    rot["e"] += 1
                    pb = rot["pt"] % 4
                    rot["pt"] += 1
                    info[k]["pt"] = pb
                    T.op("act", _act(Eb[eb], ps[b], AF.Exp), reads=["ps%d" % b], writes=["E%d" % eb])
                    nseg, w = (2, 256) if m256 else (4, 128)
                    T.op("dve", _tt(PT[pb].rearrange("p (s w) -> p s w", s=nseg),
                                    Eb[eb].rearrange("p (s w) -> p s w", s=nseg),
                                    EBT[:, hh, 0:w].unsqueeze(1).broadcast_to([128, nseg, w]), ALU.mult),
                         reads=["E%d" % eb, "EBT"], writes=["PT%d" % pb])
                    vsl = slice(hd * 2, hd * 2 + 2)

                    Vbuf, Vname = (Vb[bi], "V%d" % bi) if hp % 2 == 0 else (Vn[g], "Vn%d" % g)

                    def lhs_v(kb):
                        return Vbuf[:, kb, vsl, :].rearrange("p a d -> p (a d)")
                    qbs = [2 * i, 2 * i + 1] if m256 else [4 * i + j for j in range(4)]
                    for qb in qbs:
                        if qb % 4 == 0:
                            ob = 2 + rot["o"] % 2
                            rot["o"] += 1
                            info[("ob", hd)] = ob
                        ob = info[("ob", hd)]
                        orr = "ps%d" % ob
                        oreg = ps[ob][:, (qb % 4) * 128:(qb % 4 + 1) * 128]
                        if m256:
                            has_prev = (qb % nb) != 0
                            if has_prev:
                                if qb == 2 * i + 1:
                                    T.op("pe", _mm(oreg, lhs_v(qb - 1), PT[pb][:, 128:256], True, False),
                                         reads=["PT%d" % pb, Vname], writes=[orr])
                                else:
                                    ppb = info[k - 1]["pt"]
                                    T.op("pe", _mm(oreg, lhs_v(qb - 1), PT[ppb][:, 384:512], True, False),
                                         reads=["PT%d" % ppb, Vname], writes=[orr])
                            c0 = (qb - 2 * i) * 256
                            T.op("pe", _mm(oreg, lhs_v(qb), PT[pb][:, c0:c0 + 128], not has_prev, True),
                                 reads=["PT%d" % pb, Vname], writes=[orr])
                        else:
                            c0 = (qb - 4 * i) * 128
                            T.op("pe", _mm(oreg, lhs_v(qb), PT[pb][:, c0:c0 + 128], True, True),
                                 reads=["PT%d" % pb, Vname], writes=[orr])
                        if qb % 4 == 3:
                            j = qb // 4
                            an = "acc%d" % hd
                            if g == 0:
                                av, sv = acc[hd][:, j * 512:(j + 1) * 512], ps[ob]
                            elif g == 1:
                                av, sv = acc[hd][:, j:SEQ:4], ps[ob]
                            else:
                                av = acc[hd].rearrange("p (m r) -> p r m", r=16)[:, 4 * j:4 * j + 4, :]
                                sv = ps[ob].rearrange("p (r m) -> p r m", r=4)
                            if g == GROUPS[0]:
                                T.op("dve", _copy(av, sv), reads=[orr], writes=[an])
                            else:
                                T.op("dve", _tt(av, sv, av, ALU.add), reads=[orr, an], writes=[an])

                steps.append(lambda: S_rec(0))
                for k in range(len(items)):
                    def f(k=k):
                        if k + 1 < len(items):
                            S_rec(k + 1)
                        rest_rec(k)
                    steps.append(f)
                if g == GROUPS[-1]:
                    def fin():
                        gb = "gS%d" % (hp % 2)
                        gs = gS[hp % 2]
                        T.op("act", _act(rden[0:64, :], acc[0][64:128, :], AF.Ln), reads=["acc0"], writes=["rdenA"])
                        T.op("act", _act(rden[0:64, :], rden[0:64, :], AF.Exp, scale=-1.0), reads=["rdenA"], writes=["rdenA"])
                        T.op("dve", _tt(acc[0][0:64, :], acc[0][0:64, :], rden[0:64, :], ALU.mult),
                             reads=["acc0", "rdenA"], writes=["acc0"])
                        T.op("dve", _tt(oattT[0:64, hp, :], acc[0][0:64, :], gs[0:64, :], ALU.mult),
                             reads=["acc0", gb], writes=["oattT"])
                        T.op("act", _act(rden[64:128, :], acc[1][0:64, :], AF.Ln), reads=["acc1"], writes=["rdenB"])
                        T.op("act", _act(rden[64:128, :], rden[64:128, :], AF.Exp, scale=-1.0), reads=["rdenB"], writes=["rdenB"])
                        T.op("dve", _tt(acc[1][64:128, :], acc[1][64:128, :], rden[64:128, :], ALU.mult),
                             reads=["acc1", "rdenB"], writes=["acc1"])
                        T.op("dve", _tt(oattT[64:128, hp, :], acc[1][64:128, :], gs[64:128, :], ALU.mult),
                             reads=["acc1", gb], writes=["oattT"])
                    steps.append(fin)
                return steps

            for f in inproj_steps(units[0], 0):
                f()
            for i, u in enumerate(units):
                A = attend_steps(u, i % 2)
                B = inproj_steps(units[i + 1], (i + 1) % 2) if i + 1 < len(units) else []
                for f in _interleave(A, B):
                    f()
                if s == 0:
                    W.emit_conversions(2 * i, 2 * i + 2 if i + 1 < len(units) else 10 ** 9)
            if debug == "oatt":
                for hp in range(6):
                    finals.append(T.dma("pool", "dbg", _dma(dbg_d[s, hp * 128:(hp + 1) * 128, :], oattT[:, hp, :]),
                                        reads=["oattT"]))

        def stream(s):
            T.dma("sp", "c3", _dma(nw_bc, nw_d.partition_broadcast(128)), writes=["nw"])
            T.dma("sp", "c4", _dma(b2_bc, b2_d.partition_broadcast(128)), writes=["b2"])
            T.dma("sp", "c5", _dma(lng_bc, lng_d.partition_broadcast(128)), writes=["lng"])
            T.dma("sp", "c6", _dma(lnb_bc, lnb_d.partition_broadcast(128)), writes=["lnb"])
            T.op("dve", _memset(Sst, 0.0), writes=["S0", "S1", "S2", "S3"])
            T.op("dve", _memset(Sbf, 0.0), writes=["Sb0", "Sb1", "Sb2", "Sb3"])
            T.op("dve", _memset(hist, 0.0), writes=["hist%d" % q for q in range(24)])
            rot = {"u": 0, "c": 0, "k": 0}
            def load_xp(blk):
                tsl_ = slice(blk * 512, (blk + 1) * 512)
                q2 = blk % 2
                T.dma("pool", "xb%d" % q2, _dma(xTb2[q2], xT_d[s].rearrange("(c p) t -> p c t", p=128)[:, :, tsl_]),
                      writes=["xTb%d" % q2])
                T.dma("pool", "pb%d" % q2, _dma(pTb2[q2], pT_d[s].rearrange("(c p) t -> p c t", p=128)[:, :, tsl_]),
                      writes=["pTb%d" % q2])

            load_xp(0)
            for blk in range(4):
                tsl = slice(blk * 512, (blk + 1) * 512)
                xTb, pTb = xTb2[blk % 2], pTb2[blk % 2]
                XB, PB = "xTb%d" % (blk % 2), "pTb%d" % (blk % 2)
                if blk + 1 < 4:
                    load_xp(blk + 1)
                W.block_start()
                T.dma("sp", "xr", _dma(xres, x_d[s, tsl, :].rearrange("(t p) d -> p t d", p=128)),
                      writes=["xres", "xres0", "xres1", "xres2", "xres3"])

                for cg in range(6):
                    (wv,), wr = W.next([(w_in_v[:, :, XBC0 + cg * 512: XBC0 + (cg + 1) * 512], [8, 512])], cached=True)
                    for j in range(4):
                        cc = cg * 4 + j
                        b = rot["u"] % 2
                        rot["u"] += 1
                        pr = "ps%d" % b
                        for c in range(8):
                            T.op("pe", _mm(ps[b], wv[:, c, j * 128:(j + 1) * 128], xTb[:, c, :], c == 0, c == 7),
                                 reads=[wr, XB], writes=[pr])
                        ur, ct = uraw[b], ctmp[b]
                        T.op("act", _acopy(ur[:, 0:3], hist[:, cc, :]), reads=["hist%d" % cc], writes=["urh%d" % b])
                        T.op("act", _acopy(ur[:, 3:515], ps[b]), reads=[pr], writes=["ur%d" % b])
                        T.op("act", _acopy(hist[:, cc, :], ur[:, 512:515]), reads=["ur%d" % b], writes=["hist%d" % cc])
                        T.op("act", _act(ct, ps[b], AF.Identity, bias=cb[:, cc:cc + 1], scale=cw[:, cc, 3:4]),
                             reads=[pr, "cw", "cb"], writes=["ct%d" % b])
                        for k in (2, 1, 0):
                            T.op("dve", _stt(ct, ur[:, k:k + 512], cw[:, cc, k:k + 1], ct, ALU.mult, ALU.add),
                                 reads=["ur%d" % b, "urh%d" % b, "cw", "ct%d" % b], writes=["ct%d" % b])
                        xw = ["XT%d_%d" % (cc // 4, q4) for q4 in range(4)] if cc < 16 else ["XTbc"]
                        T.op("act", _act(XT[:, cc, :], ct, AF.Silu), reads=["ct%d" % b], writes=xw)

                T.barrier()
                (wdt,), wr = W.next([(w_in_v[:, :, DT0:DT0 + 32], [8, 32])], cached=True)
                for c4 in range(4):
                    csl = slice(c4 * 128, (c4 + 1) * 128)
                    for c in range(8):
                        T.op("pe", _mm(ps[2][:, c4 * 32:(c4 + 1) * 32], xTb[:, c, csl], wdt[:, c, :], c == 0, c == 7),
                             reads=[wr, XB], writes=["ps2"])
                T.op("dve", _tt(sm_t, ps[2][:, 0:128].rearrange("p (a h) -> p a h", a=4),
                                dtb_bc.unsqueeze(1).broadcast_to([128, 4, 32]), ALU.add),
                     reads=["ps2", "dtb"], writes=["sm_t"])
                T.op("act", _act(sm_e, sm_t, AF.Exp), reads=["sm_t"], writes=["sm_e"])
                T.op("act", _act(dtc, sm_e, AF.Ln, bias=1.0), reads=["sm_e"], writes=["dtc"])
                T.op("dve", _tt(ac, dtc, A_bc.unsqueeze(1).broadcast_to([128, 4, 32]), ALU.mult),
                     reads=["dtc", "A"], writes=["ac"])
                for c4 in range(4):
                    T.op("pe", _mm(ps[3][:, c4 * 32:(c4 + 1) * 32], tri32, ac[:, c4, :], True, True),
                         reads=["tri32", "ac"], writes=["ps3"])
                    T.op("pe", _mm(ps[3][:, 128 + c4 * 32:128 + (c4 + 1) * 32], ones32, ac[:, c4, :], True, True),
                         reads=["ones32", "ac"], writes=["ps3"])
                cs_ps = ps[3][:, 0:128].rearrange("p (a h) -> p a h", a=4)
                tot_ps = ps[3][:, 128:256].rearrange("p (a h) -> p a h", a=4)
                T.op("act", _acopy(csb, cs_ps), reads=["ps3"], writes=["csb"])
                T.op("act", _act(dstart, cs_ps, AF.Exp), reads=["ps3"], writes=["dstart"])
                T.op("act", _act(cdec, tot_ps, AF.Exp), reads=["ps3"], writes=["cdec"])
                T.op("dve", _tt(sm_t, tot_ps, csb, ALU.subtract), reads=["ps3", "csb"], writes=["sm_t"])
                T.op("act", _act(dend, sm_t, AF.Exp), reads=["sm_t"], writes=["dend"])

                its = [(g, c4) for g in range(4) for c4 in range(4)]
                wzs = {}

                def ssd_front(i):
                    g, c4 = its[i]
                    k = i % 2
                    kk = str(k)
                    hs = slice(8 * g, 8 * g + 8)
                    csl = slice(c4 * 128, (c4 + 1) * 128)
                    xn = "XT%d_%d" % (g, c4)
                    if c4 == 0:
                        wzs[g] = W.next([(w_in_v[:, :, Z0 + g * 512: Z0 + (g + 1) * 512], [8, 512])], cached=True)
                    (wz,), wr = wzs[g]
                    if c4 == 0:
                        for c4b in range(4):
                            for c in range(8):
                                T.op("pe", _mm(ps[0], xTb[:, c, c4b * 128:(c4b + 1) * 128], wz[:, c, :], c == 0, c == 7),
                                     reads=[wr, XB], writes=["ps0"])
                            T.op("act", _act(zsall[:, i + c4b, :], ps[0], AF.Silu), reads=["ps0"], writes=["zs%d" % (i + c4b)])
                    NA = RHSA_ACT
                    for h in range(NA):
                        T.op("act", _act(rhsa[k][:, h, :], tri, AF.Copy, scale=ac[:, c4, 8 * g + h:8 * g + h + 1]),
                             reads=["tri", "ac"], writes=["rhsa%s_%d" % (kk, h // 4)])
                    if NA < 8:
                        T.op("dve", _tt(rhsa[k][:, NA:8, :], tri.unsqueeze(1).broadcast_to([128, 8 - NA, 128]),
                                        ac[:, c4, 8 * g + NA:8 * g + 8].unsqueeze(2).broadcast_to([128, 8 - NA, 128]), ALU.mult),
                             reads=["tri", "ac"], writes=["rhsa%s_1" % kk] + (["rhsa%s_0" % kk] if NA < 4 else []))
                    for j in range(4):
                        T.op("pe", _tr(psb[2][:, j * 128:(j + 1) * 128], XT[:, 4 * g + j, csl], ident),
                             reads=[xn, "ident"], writes=["ps2"])
                    T.op("pe", _tr(psb[2][:, 512:640], XT[:, 16 + g, csl], ident), reads=["XTbc", "ident"], writes=["ps2"])
                    xtm = psb[2][:, 0:512].rearrange("p (h d) -> p h d", h=8)
                    T.op("dve", _tt(xdt[k].rearrange("p (h d) -> p h d", h=8), xtm,
                                    dtc[:, c4, hs].unsqueeze(2).broadcast_to([128, 8, 64]), ALU.mult),
                         reads=["ps2", "dtc"], writes=["xdt" + kk])
                    T.op("dve", _tt(xD[k].rearrange("p (h d) -> p h d", h=8), xtm,
                                    D_bc[:, hs].unsqueeze(2).broadcast_to([128, 8, 64]), ALU.mult),
                         reads=["ps2", "D"], writes=["xD" + kk])
                    T.op("act", _acopy(Btm[k], psb[2][:, 512:640]), reads=["ps2"], writes=["Btm" + kk])
                    T.op(PENG, _tt(xdtd[k].rearrange("p (h d) -> p h d", h=8),
                                     xdt[k].rearrange("p (h d) -> p h d", h=8),
                                     dend[:, c4, hs].unsqueeze(2).broadcast_to([128, 8, 64]), ALU.mult),
                         reads=["xdt" + kk, "dend"], writes=["xdtd" + kk])
                    T.op("pe", _mm(ps[3][:, 256:384], XT[:, 16 + g, csl], XT[:, 20 + g, csl], True, True),
                         reads=["XTbc"], writes=["ps3g"])
                    T.op("act", _acopy(Gm[k], ps[3][:, 256:384]), reads=["ps3g"], writes=["Gm" + kk])
                    for q in range(2):
                        T.op("pe", _mm(ps[4 + q], Umat, rhsa[k][:, 4 * q:4 * q + 4, :].rearrange("p h l -> p (h l)"),
                                       True, False), reads=["U", "rhsa%s_%d" % (kk, q)], writes=["ps%d" % (4 + q)])
                        T.op("pe", _mm(ps[4 + q], ident, NEGM.rearrange("p h l -> p (h l)"), False, True),
                             reads=["ident", "NEGM"], writes=["ps%d" % (4 + q)])
                        T.op("act", _act(Es[q], ps[4 + q], AF.Exp), reads=["ps%d" % (4 + q)], writes=["Es%d" % q])
                        T.op("dve", _tt(MT[k][:, 4 * q:4 * q + 4, :], Es[q].rearrange("p (h l) -> p h l", h=4),
                                        Gm[k].unsqueeze(1).broadcast_to([128, 4, 128]), ALU.mult),
                             reads=["Es%d" % q, "Gm" + kk], writes=["MT" + kk])

                def ssd_back(i):
                    g, c4 = its[i]
                    k = i % 2
                    kk = str(k)
                    hs = slice(8 * g, 8 * g + 8)
                    csl = slice(c4 * 128, (c4 + 1) * 128)
                    xn = "XT%d_%d" % (g, c4)
                    T.op("pe", _c(lambda e, k=k: e.matmul(ps[6], ident, xD[k], start=True, stop=False, skip_group_check=True), 0.216),
                         reads=["ident", "xD" + kk], writes=["ps6"])
                    for h in range(8):
                        T.op("pe", _c(lambda e, k=k, h=h: e.matmul(ps[6][:, h * 64:(h + 1) * 64], MT[k][:, h, :],
                                                                   xdt[k][:, h * 64:(h + 1) * 64], start=False, stop=True,
                                                                   skip_group_check=True), 0.096),
                             reads=["MT" + kk, "xdt" + kk], writes=["ps6"])
                    T.op("pe", _mm(ps[7], XT[:, 20 + g, csl], Sbf[:, g, :], True, True),
                         reads=["XTbc", "Sb%d" % g], writes=["ps7"])
                    T.op("pe", _mm(ps[1], Btm[k], xdtd[k], True, True), reads=["Btm" + kk, "xdtd" + kk], writes=["ps1"])
                    T.op("dve", _tt(t1[k].rearrange("p (h d) -> p h d", h=8), ps[7].rearrange("p (h d) -> p h d", h=8),
                                    dstart[:, c4, hs].unsqueeze(2).broadcast_to([128, 8, 64]), ALU.mult),
                         reads=["ps7", "dstart"], writes=["t1" + kk])
                    T.op("dve", _tt(t1[k], ps[6], t1[k], ALU.add), reads=["ps6", "t1" + kk], writes=["t1" + kk])
                    T.op("dve", _tt(ug[k], t1[k], zsall[:, i, :], ALU.mult), reads=["t1" + kk, "zs%d" % i], writes=["ug" + kk])
                    T.op("act", _act(usq, ug[k], AF.Square, accum_out=ssq[k][:, 0:1]), reads=["ug" + kk],
                         writes=["usq", "ssq" + kk])
                    T.op("act", _act(ssq[k][:, 1:2], ssq[k][:, 0:1], AF.Ln, bias=epsr[:, 0:1], scale=1.0 / 512.0),
                         reads=["ssq" + kk, "epsr"], writes=["ssqb" + kk])
                    T.op("act", _act(ssq[k][:, 3:4], ssq[k][:, 1:2], AF.Exp, scale=-0.5), reads=["ssqb" + kk], writes=["ssqd" + kk])
                    T.op("dve", _stt(yb[k], ug[k], ssq[k][:, 3:4], nw_bc[:, g * 512:(g + 1) * 512], ALU.mult, ALU.mult),
                         reads=["ug" + kk, "ssqd" + kk, "nw"], writes=["yb" + kk])
                    Sg = Sst[:, g, :]
                    T.op(PENG, _tt(Sg.rearrange("p (h d) -> p h d", h=8), Sg.rearrange("p (h d) -> p h d", h=8),
                                     cdec[:, c4, hs].unsqueeze(2).broadcast_to([128, 8, 64]), ALU.mult),
                         reads=["S%d" % g, "cdec"], writes=["S%d" % g])
                    T.op("dve", _tt(Sg, ps[1], Sg, ALU.add), reads=["ps1", "S%d" % g], writes=["S%d" % g])
                    T.op("act", _acopy(Sbf[:, g, :], Sg), reads=["S%d" % g], writes=["Sb%d" % g])
                    for j in range(4):
                        T.op("pe", _tr(psb[3][:, j * 128:(j + 1) * 128], yb[k][:, j * 128:(j + 1) * 128], ident),
                             reads=["yb" + kk, "ident"], writes=["ps3t"])
                    T.op("act", _acopy(XT[:, 4 * g:4 * g + 4, csl], psb[3][:, 0:512].rearrange("p (j t) -> p j t", j=4)),
                         reads=["ps3t"], writes=[xn])

                ssd_front(0)
                for i in range(16):
                    if i + 1 < 16:
                        ssd_front(i + 1)
                    ssd_back(i)

                if debug == "yssm":
                    for j in range(16):
                        finals.append(T.dma("pool", "dbg", _dma(dbg_d[s, j * 128:(j + 1) * 128, tsl], XT[:, j, :]),
                                            reads=["XT%d_%d" % (j // 4, q4) for q4 in range(4)]))
                T.barrier()

                for j in range(8):
                    dsl = slice(j * 128, (j + 1) * 128)
                    (wb0, wb1, wga, wgb), wr = W.next([(w_br_v[:, 0:11, dsl], [11, 128]), (w_br_v[:, 11:22, dsl], [11, 128]),
                                                 (w_in_v[:, :, GM0 + j * 128: GM0 + (j + 1) * 128], [8, 128]),
                                                 (w_in_v[:, :, GM0 + 1024 + j * 128: GM0 + 1024 + (j + 1) * 128], [8, 128])], cached=True)
                    o = 4 * (j % 2)
                    pn = ["ps%d" % (o + q) for q in range(4)]
                    for i in range(6):
                        T.op("pe", _mm(ps[o], wb0[:, i, :], oattT[:, i, tsl], i == 0, i == 5), reads=[wr, "oattT"], writes=[pn[0]])
                    for i in range(16):
                        T.op("pe", _mm(ps[o + 1], (wb0[:, 6 + i, :] if i < 5 else wb1[:, i - 5, :]), XT[:, i, :], i == 0, i == 15),
                             reads=[wr] + ["XT%d_%d" % (i // 4, q4) for q4 in range(4)], writes=[pn[1]])
                    for c in range(8):
                        T.op("pe", _mm(ps[o + 2], wga[:, c, :], xTb[:, c, :], c == 0, c == 7), reads=[wr, XB], writes=[pn[2]])
                    for c in range(8):
                        T.op("pe", _mm(ps[o + 3], wgb[:, c, :], xTb[:, c, :], c == 0, c == 7), reads=[wr, XB], writes=[pn[3]])
                    k = j % 2
                    kk = str(k)
                    T.op("act", _act(sa[k], ps[o + 2], AF.Sigmoid, bias=bg[:, 0, j:j + 1]), reads=[pn[2], "bg"], writes=["sa" + kk])
                    T.op("dve", _tt(m1[k], ps[o], sa[k], ALU.mult), reads=[pn[0], "sa" + kk], writes=["m1" + kk])
                    T.op("act", _act(sgt[k], ps[o + 3], AF.Sigmoid, bias=bg[:, 1, j:j + 1]), reads=[pn[3], "bg"], writes=["sgt" + kk])
                    T.op("dve", _tt(tpt[k], ps[o + 1], sgt[k], ALU.mult), reads=[pn[1], "sgt" + kk], writes=["tpt" + kk])
                    T.op(PENG, _tt(mergedT[:, j, :], m1[k], tpt[k], ALU.add), reads=["m1" + kk, "tpt" + kk], writes=["mergedT"])
                for half in range(2):
                    hsl = slice(half * 512, (half + 1) * 512)
                    (wo,), wr = W.next([(w_out_v[:, :, hsl], [8, 512])], cached=True)
                    for tt in range(4):
                        tts = slice(tt * 128, (tt + 1) * 128)
                        b = tt % 2
                        for j in range(8):
                            T.op("pe", _mm(ps[b], mergedT[:, j, tts], wo[:, j, :], j == 0, j == 7),
                                 reads=[wr, "mergedT"], writes=["ps%d" % b])
                        T.op("dve", _stt(xres[:, tt, hsl], xres[:, tt, hsl], ALPHA, ps[b], ALU.mult, ALU.add),
                             reads=["ps%d" % b, "xres"], writes=["xres"])
                    (wgp, wpl), wr = W.next([(w_in_v[:, :, GPLE0 + half * 512: GPLE0 + (half + 1) * 512], [8, 512]),
                                             (w_ple_v[:, :, hsl], [2, 512])], cached=True)
                    for tt in range(4):
                        tts = slice(tt * 128, (tt + 1) * 128)
                        k = tt % 2
                        kk = str(k)
                        for c in range(8):
                            T.op("pe", _mm(ps[2 + k], xTb[:, c, tts], wgp[:, c, :], c == 0, c == 7),
                                 reads=[wr, XB], writes=["ps%d" % (2 + k)])
                        for c in range(2):
                            T.op("pe", _mm(ps[4 + k], pTb[:, c, tts], wpl[:, c, :], c == 0, c == 1),
                                 reads=[wr, PB], writes=["ps%d" % (4 + k)])
                        T.op("dve", _tt(sa[k], ps[2 + k], b2_bc[:, hsl], ALU.add), reads=["ps%d" % (2 + k), "b2"], writes=["sa" + kk])
                        T.op("act", _act(sgt[k], sa[k], AF.Sigmoid), reads=["sa" + kk], writes=["sgt" + kk])
                        T.op("dve", _tt(tpt[k], ps[4 + k], sgt[k], ALU.mult), reads=["ps%d" % (4 + k), "sgt" + kk], writes=["tpt" + kk])
                        T.op("dve", _tt(xres[:, tt, hsl], xres[:, tt, hsl], tpt[k], ALU.add), reads=["xres", "tpt" + kk], writes=["xres"])
                T.barrier()
                for tt in range(4):
                    k = tt % 2
                    kk = str(k)
                    r = xres[:, tt, :]
                    rn = "xres%d" % tt
                    st_ = lnst[k]
                    T.op("dve", lambda e, st_=st_, r=r: e.reduce_sum(out=st_[:, 0:1], in_=r, axis=AX.X), reads=["xres", rn], writes=["lnst" + kk])
                    T.op("dve", _ts(st_[:, 1:2], st_[:, 0:1], 1.0 / 1024.0, None, ALU.mult), reads=["lnst" + kk], writes=["lnstb" + kk])
                    T.op("dve", _ts(r, r, st_[:, 1:2], None, ALU.subtract), reads=["xres", rn, "lnstb" + kk], writes=[rn])
                    T.op("act", _act(lnsq, r, AF.Square, accum_out=st_[:, 2:3]), reads=[rn], writes=["lnsq", "lnstc" + kk])
                    T.op("act", _act(st_[:, 3:4], st_[:, 2:3], AF.Ln, bias=epsln[:, 0:1], scale=1.0 / 1024.0),
                         reads=["lnstc" + kk, "epsln"], writes=["lnstd" + kk])
                    T.op("act", _act(st_[:, 0:1], st_[:, 3:4], AF.Exp, scale=-0.5), reads=["lnstd" + kk], writes=["lnst" + kk])
                    T.op("dve", _stt(r, r, st_[:, 0:1], lng_bc, ALU.mult, ALU.mult), reads=[rn, "lnst" + kk, "lng"], writes=[rn])
                    T.op("dve", _tt(r, r, lnb_bc, ALU.add), reads=[rn, "lnb"], writes=[rn])
                    t0 = blk * 512 + tt * 128
                    finals.append(T.dma("sp", "st%d" % k, _dma(out_d[s, t0:t0 + 128, :], r), reads=[rn]))

        for s in range(NSEQ):
            attention(s)
            T.barrier()
            if debug != "oatt":
                stream(s)
                T.barrier()
        return finals

    T0 = Tracker(nc)
    W0 = WRing(T0, wslots, None, scratch=wscr)
    construct(T0, W0)
    assert len(W0.cache) <= NCACHE, len(W0.cache)
    T = Tracker(nc)
    W = WRing(T, wslots, W0.plan, cache=W0.cache, scratch=wscr)
    finals = construct(T, W)
    assert W.cur == len(W0.plan) and W.issued == len(W0.plan)
    if os.environ.get("MK_NOSCHED") is None:
        T.schedule()
    T.emit(finals)
    return nc


def _t5_bucket_np(dist):
    max_exact = 16
    d_f = np.maximum(dist, 1).astype(np.float32)
    large = max_exact + (np.log(d_f / np.float32(max_exact)) / np.float32(math.log(2048 / max_exact))
                         * np.float32(32 - max_exact)).astype(np.int32)
    large = np.minimum(large, 31)
    return np.where(dist < max_exact, dist, large)


def _host_consts():
    ki = np.arange(128)[:, None]
    qi = np.arange(128)[None, :]
    d_cur = qi - ki
    d_nxt = qi + 128 - ki
    delta = np.concatenate([d_cur, d_nxt], axis=1)
    valid = (delta >= 0) & (delta <= 128)
    maskT = valid.astype(np.float32)
    idx = np.stack([_t5_bucket_np(np.maximum(delta, 0) * d) for d in DILS])
    p = np.arange(128)[:, None]
    f = np.arange(128)[None, :]
    cmat = np.stack([(p == f), (p <= f), (p > f), np.ones((128, 128), bool)]).astype(np.float32)
    return maskT, idx, cmat


_NC_CACHE = {}


def kernel(x, p, w_in, b_gate, conv_w, conv_b, dt_bias, a_log, d_skip, ssm_norm_w,
           w_branch, w_out, w_ple, ln_g, ln_b, rel_bias):
    debug = os.environ.get("MK_DEBUG") or None
    f32 = np.float32
    x = np.asarray(x, f32)
    p = np.asarray(p, f32)[0]
    maskT, idx, cmat = _host_consts()
    rel_bias = np.asarray(rel_bias, f32)
    biasT = np.stack([rel_bias[idx[hh // 12], hh] for hh in range(36)]).astype(f32)
    cw = np.ascontiguousarray(np.asarray(conv_w, f32)[0].T.reshape(24, 128, 4).transpose(1, 0, 2))
    cb = np.ascontiguousarray(np.asarray(conv_b, f32)[0].reshape(24, 128).T)
    bgate = np.asarray(b_gate, f32)[0]
    bg = np.ascontiguousarray(bgate[0:2].reshape(2, 8, 128).transpose(2, 0, 1))
    shared = {
        "w_in": np.ascontiguousarray(np.asarray(w_in, f32)[0]),
        "w_branch": np.ascontiguousarray(np.asarray(w_branch, f32)[0]),
        "w_out": np.ascontiguousarray(np.asarray(w_out, f32)[0]),
        "w_ple": np.ascontiguousarray(np.asarray(w_ple, f32)[0]),
        "biasT": biasT, "maskT": maskT, "cmat": cmat, "cw": cw, "cb": cb, "bg": bg,
        "b2": np.ascontiguousarray(bgate[2]),
        "dt_bias": np.ascontiguousarray(np.asarray(dt_bias, f32)[0]),
        "a_log": np.ascontiguousarray(np.asarray(a_log, f32)[0]),
        "d_skip": np.ascontiguousarray(np.asarray(d_skip, f32)[0]),
        "ssm_norm_w": np.ascontiguousarray(np.asarray(ssm_norm_w, f32)[0]),
        "ln_g": np.ascontiguousarray(np.asarray(ln_g, f32)[0]),
        "ln_b": np.ascontiguousarray(np.asarray(ln_b, f32)[0]),
    }
    in_maps = []
    for c in range(N_CORES):
        xs = x[c * NSEQ:(c + 1) * NSEQ]
        m = dict(shared)
        m["x"] = np.ascontiguousarray(xs)
        m["xT"] = np.ascontiguousarray(xs.transpose(0, 2, 1))
        m["pT"] = np.ascontiguousarray(p[c * NSEQ:(c + 1) * NSEQ].transpose(0, 2, 1))
        in_maps.append(m)
    if debug not in _NC_CACHE:
        _NC_CACHE[debug] = build_program(debug)
    nc = _NC_CACHE[debug]
    ncores = int(os.environ.get("MK_CORES", N_CORES))
    res = run_bass_kernel_spmd(nc, in_maps[:ncores], core_ids=list(range(ncores)))
    if debug:
        return [r["dbg"] for r in res.results]
    return np.concatenate([r["out"] for r in res.results], axis=0).astype(f32)
```

```python
import math
import os
import contextlib
import numpy as np
import concourse.bass as bass
import concourse.mybir as mybir
from concourse.bass_utils import run_bass_kernel_spmd

F32 = mybir.dt.float32
BF16 = mybir.dt.bfloat16
U8 = mybir.dt.uint8
AF = mybir.ActivationFunctionType
ALU = mybir.AluOpType
AX = mybir.AxisListType

N_CORES = 8
D_MODEL = 1024
SEQ = 2048
NSEQ = 2
K0, V0, GATT0, Z0, XBC0, DT0, GM0, GPLE0, IN_COLS = 2304, 4608, 6912, 7680, 9728, 12800, 12832, 14880, 15904
DILS = (1, 4, 16)
ALPHA = 2.0 ** 0.25
LN_EPS = 1e-5
RMS_EPS = 1e-5
WSLOT = 5120
NRING = 3
RHSA_ACT = int(os.environ.get("MK_RHSA_ACT", "0"))
PENG = "dve"


class Node:
    __slots__ = ("eng", "idx", "fn", "deps", "signal", "sigval", "dma", "cost", "lat", "gidx", "fin", "res", "odeps", "epoch", "table")

    def __init__(self, eng, idx, fn, dma=None):
        self.eng = eng
        self.idx = idx
        self.fn = fn
        self.cost = getattr(fn, "cost", 0.5)
        self.lat = getattr(fn, "lat", 0.0)
        self.table = getattr(fn, "table", None)
        self.gidx = 0
        self.fin = None
        self.res = ()
        self.odeps = []
        self.deps = []
        self.signal = False
        self.sigval = None
        self.dma = dma


class Tracker:
    ENGS = ("pe", "act", "dve", "pool", "sp")

    def __init__(self, nc):
        self.nc = nc
        self.ops = {e: [] for e in self.ENGS}
        self.lastw = {}
        self.readers = {}
        self.dma_cnt = {}
        self.dma_latest = {}
        self.bank_last = {}
        self.pending = {e: [] for e in self.ENGS}

    def _add(self, node, reads, writes):
        self.gcount = getattr(self, "gcount", 0) + 1
        node.gidx = self.gcount
        node.epoch = getattr(self, "epoch", 0)
        node.res = (tuple(reads), tuple(writes))
        deps = {}

        def add_dep(n):
            if n is not None and n is not node:
                deps[id(n)] = n

        for r in reads:
            add_dep(self.lastw.get(r))
        for w in writes:
            add_dep(self.lastw.get(w))
            for n in self.readers.get(w, {}).values():
                add_dep(n)
        banks = {int(r[2]) for r in list(reads) + list(writes) if r.startswith("ps") and r[2].isdigit()}
        for b in banks:
            bl = self.bank_last.setdefault(b, {})
            for e, n in bl.items():
                if e != node.eng:
                    add_dep(n)
            bl[node.eng] = node
        for n in self.pending[node.eng]:
            add_dep(n)
        self.pending[node.eng] = []
        node.deps = list(deps.values())
        key = ("dma", node.dma[0]) if node.dma else node.eng
        for r in reads:
            self.readers.setdefault(r, {})[key] = node
        for w in writes:
            self.lastw[w] = node
            self.readers[w] = {}

    def op(self, eng, fn, reads=(), writes=()):
        node = Node(eng, len(self.ops[eng]), fn)
        self.ops[eng].append(node)
        self._add(node, reads, writes)
        return node

    def dma(self, eng, slot, fn, reads=(), writes=()):
        self.dma_cnt[slot] = self.dma_cnt.get(slot, 0) + 16
        node = Node(eng, len(self.ops[eng]), fn, dma=(slot, self.dma_cnt[slot]))
        self.ops[eng].append(node)
        self._add(node, reads, writes)
        self.dma_latest[slot] = node
        return node

    def barrier(self):
        self.epoch = getattr(self, "epoch", 0) + 1
        last = [self.ops[e][-1] for e in self.ENGS if self.ops[e]]
        last += [n for sl, n in self.dma_latest.items() if sl != "cv"]
        for e in self.ENGS:
            self.pending[e] = list(last)

    def schedule(self, window=48, vis=0.15):
        allnodes = sorted((n for e in self.ENGS for n in self.ops[e]), key=lambda n: n.gidx)
        wcount = {}
        for n in allnodes:
            for w in n.res[1]:
                wcount[w] = wcount.get(w, 0) + 1
        last_acc = {}
        for n in allnodes:
            keys = set()
            for r in n.res[0] + n.res[1]:
                if wcount.get(r, 0) >= 2:
                    keys.add(r)
                if r.startswith("ps") and r[2].isdigit():
                    keys.add(("bank", int(r[2])))
            for k in keys:
                p = last_acc.get((k, n.eng))
                if p is not None:
                    n.odeps.append(p)
                last_acc[(k, n.eng)] = n
        rem = {e: list(self.ops[e]) for e in self.ENGS}
        out = {e: [] for e in self.ENGS}
        free = {e: 0.0 for e in self.ENGS}
        cur_epoch = {e: -1 for e in self.ENGS}
        cur_tab = [None]
        nleft = sum(len(v) for v in rem.values())
        while nleft:
            best = None
            for e in self.ENGS:
                lst = rem[e]
                if not lst:
                    continue
                cand = None
                wnd = lst[:window] if lst[0].epoch == cur_epoch[e] else lst[:1]
                for n in wnd:
                    if n.epoch != lst[0].epoch:
                        break
                    rdy = 0.0
                    ok = True
                    for d in n.odeps:
                        if d.fin is None:
                            ok = False
                            break
                    if not ok:
                        continue
                    for d in n.deps:
                        if d.fin is None:
                            ok = False
                            break
                        if d.eng == "pe" and e == "pe" and d.dma is None and n.dma is None:
                            continue
                        if d.fin + vis > rdy:
                            rdy = d.fin + vis
                    if not ok:
                        continue
                    st = max(rdy, free[e])
                    pen = 0.0
                    if e == "act" and n.table is not None and not (n.table == cur_tab[0] or (n.table == "E" and cur_tab[0] == "L")):
                        pen = 1.3
                    if cand is None or st + pen < cand[0] - 1e-9:
                        cand = (st + pen, n)
                    if st + pen <= free[e] + 1e-9:
                        break
                if cand is not None and (best is None or cand[0] < best[0] - 1e-9 or
                                         (abs(cand[0] - best[0]) <= 1e-9 and cand[1].gidx < best[1].gidx)):
                    best = cand
            st, n = best
            e = n.eng
            if e == "act" and n.table is not None and not (n.table == cur_tab[0] or (n.table == "E" and cur_tab[0] == "L")):
                cur_tab[0] = n.table
            rem[e].remove(n)
            out[e].append(n)
            cur_epoch[e] = n.epoch
            free[e] = st + n.cost
            n.fin = st + n.cost + n.lat
            nleft -= 1
        self.ops = out
        self.est_us = max(free.values())

    def emit(self, final_nodes):
        nc = self.nc
        for e in self.ENGS:
            for n in self.ops[e]:
                for d in n.deps:
                    if d.dma is None:
                        if d.eng == "pe" and n.eng == "pe" and n.dma is None:
                            continue
                        d.signal = True
        for n in final_nodes:
            if n.dma is None:
                n.signal = True
        for e in self.ENGS:
            c = 0
            for n in self.ops[e]:
                if n.dma is None and n.signal:
                    c += 1
                    n.sigval = c
        with contextlib.ExitStack() as st:
            esem = {e: st.enter_context(nc.semaphore("s_" + e)) for e in self.ENGS}
            dsem = {s: st.enter_context(nc.semaphore("d_" + s)) for s in self.dma_cnt}
            block = st.enter_context(nc.Block())

            def run(ename, eng):
                waited = {}
                for n in self.ops[ename]:
                    need = {}
                    for d in n.deps:
                        if d.dma is not None:
                            k, v = ("d", d.dma[0]), d.dma[1]
                        else:
                            if d.eng == "pe" and ename == "pe" and n.dma is None:
                                continue
                            k, v = ("e", d.eng), d.sigval
                        if v > need.get(k, 0):
                            need[k] = v
                    for k, v in need.items():
                        if waited.get(k, 0) >= v:
                            continue
                        waited[k] = v
                        eng.wait_ge(dsem[k[1]] if k[0] == "d" else esem[k[1]], v)
                    ins = n.fn(eng)
                    if n.dma is not None:
                        ins.then_inc(dsem[n.dma[0]], 16)
                    elif n.signal:
                        ins.then_inc(esem[ename], 1)
                if ename == "sp":
                    fin = {}
                    for n in final_nodes:
                        k, v = (("d", n.dma[0]), n.dma[1]) if n.dma is not None else (("e", n.eng), n.sigval)
                        fin[k] = max(fin.get(k, 0), v)
                    for k, v in fin.items():
                        eng.wait_ge(dsem[k[1]] if k[0] == "d" else esem[k[1]], v)

            block.tensor(lambda t: run("pe", t))
            block.scalar(lambda s: run("act", s))
            block.vector(lambda v: run("dve", v))
            block.gpsimd(lambda g: run("pool", g))
            block.sync(lambda sy: run("sp", sy))


class Arena:
    def __init__(self, nc, nbytes):
        self.ap = nc.alloc_sbuf_tensor("arena", [128, nbytes], U8).ap()
        self.nbytes = nbytes
        self.off = 0
        self.peak = 0

    def alloc(self, free, dt):
        esz = 4 if dt == F32 else 2
        n = int(np.prod(free)) * esz
        v = self.ap[:, self.off:self.off + n].bitcast(dt)
        self.off += (n + 31) // 32 * 32
        self.peak = max(self.peak, self.off)
        assert self.off <= self.nbytes, ("SBUF arena overflow", self.off)
        if len(free) == 2:
            v = v.rearrange("p (a b) -> p a b", a=free[0])
        elif len(free) == 3:
            v = v.rearrange("p (a b c) -> p a b c", a=free[0], b=free[1])
        return v


class WRing:
    def __init__(self, T, slots, reqs, cache=None, scratch=None):
        self.T = T
        self.slots = slots
        self.reqs = reqs
        self.plan = []
        self.cur = 0
        self.issued = 0
        self.cidx = 0
        self.cache = cache if cache is not None else {}
        self.scratch = scratch

    def block_start(self):
        self.cidx = 0

    def _views(self, base, parts):
        off = 0
        views = []
        for _, free in parts:
            n = int(np.prod(free))
            v = base[:, off:off + n]
            if len(free) == 2:
                v = v.rearrange("p (a b) -> p a b", a=free[0])
            views.append(v)
            off += n
        assert off <= WSLOT, off
        return views, off

    def emit_conversions(self, lo=0, hi=10 ** 9):
        for ci in sorted(self.cache):
            if not (lo <= ci < hi):
                continue
            parts = self.cache[ci]
            vs, _ = self._views(self.scratch[ci], parts)
            for (src, _), v in zip(parts, vs):
                self.T.dma("pool", "cv", _dma(v, src), writes=["wsc"])

    def next(self, parts, cached=False):
        i = self.cur
        self.cur += 1
        ci = None
        if cached:
            ci = self.cidx
            self.cidx += 1
            if self.reqs is None and ci not in self.cache:
                self.cache[ci] = parts
        self.plan.append((parts, ci))
        if self.reqs is not None:
            while self.issued < min(len(self.reqs), i + NRING):
                j = self.issued
                rparts, rci = self.reqs[j]
                slot = self.slots[j % NRING]
                wn = "w%d" % (j % NRING)
                if rci is None:
                    vs, _ = self._views(slot, rparts)
                    for (src, _), v in zip(rparts, vs):
                        self.T.dma("pool", wn, _dma(v, src), writes=[wn])
                else:
                    _, tot = self._views(slot, rparts)
                    self.T.dma("sp", "v%d" % (j % NRING), _dma(slot[:, 0:tot], self.scratch[rci][:, 0:tot]),
                               reads=["wsc"], writes=[wn])
                self.issued += 1
        return self._views(self.slots[i % NRING], parts)[0], "w%d" % (i % NRING)


def _nfree(ap):
    n = 1
    for d in ap.shape[1:]:
        n *= int(d)
    return n


def _c(f, cost):
    f.cost = cost
    return f


def _dma(out, in_):
    f = lambda e: e.dma_start(out=out, in_=in_)
    f.cost = 1.0
    f.lat = 2.5 + _nfree(out) * 128 * (4 if in_.dtype == F32 else 2) / 200e3
    return f


def _mm(out, lhsT, rhs, start, stop):
    n = _nfree(rhs)
    mult = 4.0 if rhs.dtype == F32 else 1.0
    return _c(lambda e: e.matmul(out, lhsT, rhs, start=start, stop=stop), mult * max(n / 2400.0 + 0.003, 0.096))


def _tr(out, in_, ident):
    return _c(lambda e: e.transpose(out=out, in_=in_, identity=ident), 0.1)


def _act(out, in_, func, bias=None, scale=None, accum_out=None):
    kw = {}
    if bias is not None:
        kw["bias"] = bias
    if scale is not None:
        kw["scale"] = scale
    if accum_out is not None:
        kw["accum_out"] = accum_out
    f = _c(lambda e: e.activation(out=out, in_=in_, func=func, **kw), 0.12 + _nfree(out) / 1000.0)
    f.table = {AF.Silu: "S", AF.Sigmoid: "G", AF.Ln: "L", AF.Exp: "E"}.get(func)
    return f


def _tt(out, in0, in1, op):
    fast = out.dtype == BF16 and in0.dtype == BF16 and in1.dtype == BF16
    return _c(lambda e: e.tensor_tensor(out=out, in0=in0, in1=in1, op=op),
              (0.07 + _nfree(out) / 1950.0) if fast else (0.1 + _nfree(out) / 850.0))


def _ts(out, in0, s1, s2, op0, op1=None):
    cost = 0.1 + _nfree(out) / 850.0
    if op1 is None:
        return _c(lambda e: e.tensor_scalar(out=out, in0=in0, scalar1=s1, scalar2=None, op0=op0), cost)
    return _c(lambda e: e.tensor_scalar(out=out, in0=in0, scalar1=s1, scalar2=s2, op0=op0, op1=op1), cost)


def _stt(out, in0, scalar, in1, op0, op1):
    return _c(lambda e: e.scalar_tensor_tensor(out=out, in0=in0, scalar=scalar, in1=in1, op0=op0, op1=op1),
              0.1 + _nfree(out) / 850.0)


def _copy(out, in_):
    return _c(lambda e: e.tensor_copy(out=out, in_=in_), 0.1 + _nfree(out) / 850.0)


def _acopy(out, in_):
    return _c(lambda e: e.copy(out=out, in_=in_), 0.12 + _nfree(out) / 1000.0)


def _memset(ap, val):
    return _c(lambda e: e.memset(ap, val), 0.1 + _nfree(ap) / 1700.0)


def _recip(out, in_):
    return _c(lambda e: e.reciprocal(out=out, in_=in_), 0.1 + _nfree(out) / 850.0)


def _interleave(A, B):
    out = []
    na, nb = len(A), len(B)
    ia = ib = 0
    while ia < na or ib < nb:
        if ib >= nb or (ia < na and ia * nb <= ib * na):
            out.append(A[ia])
            ia += 1
        else:
            out.append(B[ib])
            ib += 1
    return out


def build_program(debug=None):
    nc = bass.Bass("TRN2", target_bir_lowering=False)

    def din(name, shape):
        return nc.dram_tensor(name, list(shape), F32, kind="ExternalInput").ap()

    xT_d = din("xT", [NSEQ, D_MODEL, SEQ])
    x_d = din("x", [NSEQ, SEQ, D_MODEL])
    pT_d = din("pT", [NSEQ, 256, SEQ])
    w_in_d = din("w_in", [D_MODEL, IN_COLS])
    w_br_d = din("w_branch", [2816, D_MODEL])
    w_out_d = din("w_out", [D_MODEL, D_MODEL])
    w_ple_d = din("w_ple", [256, D_MODEL])
    biasT_d = din("biasT", [36, 128, 256])
    maskT_d = din("maskT", [128, 256])
    cmat_d = din("cmat", [4, 128, 128])
    cw_d = din("cw", [128, 24, 4])
    cb_d = din("cb", [128, 24])
    bg_d = din("bg", [128, 2, 8])
    b2_d = din("b2", [D_MODEL])
    dtb_d = din("dt_bias", [32])
    alog_d = din("a_log", [32])
    dsk_d = din("d_skip", [32])
    nw_d = din("ssm_norm_w", [2048])
    lng_d = din("ln_g", [D_MODEL])
    lnb_d = din("ln_b", [D_MODEL])
    out_d = nc.dram_tensor("out", [NSEQ, SEQ, D_MODEL], F32, kind="ExternalOutput").ap()
    dbg_d = None
    if debug == "oatt":
        dbg_d = nc.dram_tensor("dbg", [NSEQ, 768, SEQ], F32, kind="ExternalOutput").ap()
    elif debug == "yssm":
        dbg_d = nc.dram_tensor("dbg", [NSEQ, 2048, SEQ], F32, kind="ExternalOutput").ap()

    w_in_v = w_in_d.rearrange("(c p) n -> p c n", p=128)
    w_br_v = w_br_d.rearrange("(i p) d -> p i d", p=128)
    w_out_v = w_out_d.rearrange("(c p) n -> p c n", p=128)
    w_ple_v = w_ple_d.rearrange("(c p) n -> p c n", p=128)

    NCACHE = 24
    wscr = nc.dram_tensor("wscratch", [NCACHE, 128, WSLOT], BF16, kind="Internal").ap()
    ar = Arena(nc, 206 * 1024)
    ps = [nc.alloc_psum_tensor("ps%d" % i, [128, 512], F32).ap() for i in range(8)]
    psb = [p.bitcast(BF16) for p in ps]

    ident = ar.alloc([128], BF16)
    tri = ar.alloc([128], BF16)
    Umat = ar.alloc([128], BF16)
    tri32 = ar.alloc([128], F32)
    ones32 = ar.alloc([128], F32)
    dtb_bc = ar.alloc([32], F32)
    A_bc = ar.alloc([32], F32)
    D_bc = ar.alloc([32], F32)
    cw = ar.alloc([24, 4], F32)
    cb = ar.alloc([24], F32)
    bg = ar.alloc([2, 8], F32)
    epsln = ar.alloc([1], F32)
    NEGM = ar.alloc([4, 128], BF16)
    epsr = ar.alloc([1], F32)
    oattT = ar.alloc([6, SEQ], BF16)
    wslots = [ar.alloc([WSLOT], BF16) for _ in range(NRING)]
    base = ar.off

    xT = ar.alloc([8, SEQ], BF16)
    EBT = ar.alloc([36, 256], BF16)
    after_ebt = ar.off
    qT = [ar.alloc([SEQ], BF16) for _ in range(2)]
    kT = [ar.alloc([SEQ], BF16) for _ in range(2)]
    Vb = [ar.alloc([16, 4, 64], BF16) for _ in range(2)]
    Vn = [ar.alloc([16, 4, 64], BF16) for _ in range(3)]
    acc_off = ar.off
    acc = [ar.alloc([SEQ], F32) for _ in range(2)]
    gS = [ar.alloc([SEQ], BF16) for _ in range(2)]
    Eb = [ar.alloc([512], BF16) for _ in range(2)]
    PT = [ar.alloc([512], BF16) for _ in range(4)]
    rden = ar.alloc([SEQ], F32)
    attn_end = ar.off

    ar.off = base
    XT = ar.alloc([24, 512], BF16)
    xTb2 = [ar.alloc([8, 512], BF16) for _ in range(2)]
    pTb2 = [ar.alloc([2, 512], BF16) for _ in range(2)]
    nw_bc = ar.alloc([2048], F32)
    b2_bc = ar.alloc([1024], F32)
    lng_bc = ar.alloc([1024], F32)
    lnb_bc = ar.alloc([1024], F32)
    Sst = ar.alloc([4, 512], F32)
    Sbf = ar.alloc([4, 512], BF16)
    hist = ar.alloc([24, 3], F32)
    dtc = ar.alloc([4, 32], F32)
    ac = ar.alloc([4, 32], F32)
    csb = ar.alloc([4, 32], F32)
    dstart = ar.alloc([4, 32], F32)
    dend = ar.alloc([4, 32], F32)
    cdec = ar.alloc([4, 32], F32)
    sm_t = ar.alloc([4, 32], F32)
    sm_e = ar.alloc([4, 32], F32)
    xres = ar.alloc([4, 1024], F32)
    zsall = ar.alloc([16, 512], BF16)
    lnsq = ar.alloc([1024], BF16)
    lnst = [ar.alloc([4], F32) for _ in range(2)]
    sub = ar.off
    uraw = [ar.alloc([515], F32) for _ in range(2)]
    ctmp = [ar.alloc([512], F32) for _ in range(2)]
    ar.off = sub
    xD = [ar.alloc([512], BF16) for _ in range(2)]
    xdt = [ar.alloc([512], BF16) for _ in range(2)]
    xdtd = [ar.alloc([512], BF16) for _ in range(2)]
    Btm = [ar.alloc([128], BF16) for _ in range(2)]
    Gm = [ar.alloc([128], BF16) for _ in range(2)]
    rhsa = [ar.alloc([8, 128], BF16) for _ in range(2)]
    Es = [ar.alloc([512], BF16) for _ in range(2)]
    MT = [ar.alloc([8, 128], BF16) for _ in range(2)]
    t1 = [ar.alloc([512], F32) for _ in range(2)]
    ug = [ar.alloc([512], F32) for _ in range(2)]
    usq = ar.alloc([512], F32)
    yb = [ar.alloc([512], BF16) for _ in range(2)]
    ssq = [ar.alloc([4], F32) for _ in range(2)]
    ssd_end = ar.off
    ar.off = sub
    mergedT = ar.alloc([8, 512], BF16)
    sa = [ar.alloc([512], F32) for _ in range(2)]
    m1 = [ar.alloc([512], F32) for _ in range(2)]
    sgt = [ar.alloc([512], F32) for _ in range(2)]
    tpt = [ar.alloc([512], F32) for _ in range(2)]
    tail_end = ar.off
    ar.off = acc_off
    braw = ar.alloc([36, 256], F32)
    mraw = ar.alloc([256], F32)

    def construct(T, W):
        finals = []

        cm = cmat_d.rearrange("k p f -> p k f")
        T.dma("pool", "c0_0", _dma(ident, cm[:, 0, :]), writes=["ident"])
        T.dma("pool", "c0_1", _dma(tri, cm[:, 1, :]), writes=["tri"])
        T.dma("pool", "c0_2", _dma(Umat, cm[:, 2, :]), writes=["U"])
        T.dma("sp", "c1_3", _dma(tri32, cm[:, 1, :]), writes=["tri32"])
        T.dma("sp", "c1_4", _dma(ones32, cm[:, 3, :]), writes=["ones32"])
        T.dma("sp", "c1_5", _dma(dtb_bc, dtb_d.partition_broadcast(128)), writes=["dtb"])
        T.dma("sp", "c1_6", _dma(A_bc, alog_d.partition_broadcast(128)), writes=["A"])
        T.dma("sp", "c1_7", _dma(D_bc, dsk_d.partition_broadcast(128)), writes=["D"])
        T.dma("sp", "c1_8", _dma(cw, cw_d), writes=["cw"])
        T.dma("sp", "c1_9", _dma(cb, cb_d), writes=["cb"])
        T.dma("sp", "c1_10", _dma(bg, bg_d), writes=["bg"])
        T.op("dve", _memset(epsln, LN_EPS), writes=["epsln"])
        T.op("dve", _memset(epsr, RMS_EPS), writes=["epsr"])
        T.op("dve", _ts(NEGM, Umat.unsqueeze(1).broadcast_to([128, 4, 128]), -30000.0, None, ALU.mult),
             reads=["U"], writes=["NEGM"])
        T.op("act", _act(A_bc, A_bc, AF.Exp), reads=["A"], writes=["A"])
        T.op("dve", _ts(A_bc, A_bc, -1.0, None, ALU.mult), reads=["A"], writes=["A"])
        T.barrier()

        def attention(s):
            for c in range(8):
                T.dma("pool", "xT", _dma(xT[:, c, :], xT_d[s, c * 128:(c + 1) * 128, :]), writes=["xT"])
            OVL = ["acc0", "acc1", "gS0", "gS1", "E0", "E1", "PT0", "PT1", "PT2", "PT3", "rdenA", "rdenB"]
            for h6 in range(6):
                T.dma("sp", "c2", _dma(braw[:, h6 * 6:(h6 + 1) * 6, :], biasT_d[h6 * 6:(h6 + 1) * 6].rearrange("h k q -> k h q")),
                      writes=["braw"] + OVL)
            T.dma("sp", "c7", _dma(mraw, maskT_d), writes=["mraw"] + OVL)
            for h6 in range(6):
                hsl = slice(h6 * 6, (h6 + 1) * 6)
                T.op("act", _act(braw[:, hsl, :], braw[:, hsl, :], AF.Exp), reads=["braw"], writes=["braw"])
                T.op("dve", _tt(EBT[:, hsl, :], braw[:, hsl, :], mraw.unsqueeze(1).broadcast_to([128, 6, 256]), ALU.mult),
                     reads=["braw", "mraw"] + OVL, writes=["EBT"])
            for bi in range(2):
                T.op("dve", _memset(Vb[bi][:, :, 1:3, :], 1.0), writes=["V%d" % bi])
            for gi in range(3):
                T.op("dve", _memset(Vn[gi][:, :, 1:3, :], 1.0), writes=["Vn%d" % gi])
            GROUPS = [int(c) for c in os.environ.get("MK_GROUPS", "012")]
            units = [(hp, g) for hp in range(6) for g in GROUPS]
            rot = {"ip": 0, "s": 0, "o": 0, "e": 0, "pt": 0}

            def inproj_steps(u, bi):
                hp, g = u
                D = DILS[g]
                nb = 16 // D
                steps = []
                st = {}

                def s_load():
                    parts = [(w_in_v[:, :, g * 768 + hp * 128 + off: g * 768 + hp * 128 + off + 128], [8, 128])
                             for off in (0, K0)]
                    if hp % 2 == 0:
                        parts.append((w_in_v[:, :, V0 + g * 768 + hp * 128: V0 + g * 768 + hp * 128 + 256], [8, 256]))
                    else:
                        parts.append((w_in_v[:, :, V0 + g * 768 + hp * 128: V0 + g * 768 + hp * 128 + 2], [8, 2]))
                    if g == GROUPS[0]:
                        parts.append((w_in_v[:, :, GATT0 + hp * 128: GATT0 + hp * 128 + 128], [8, 128]))
                    st["w"], st["wr"] = W.next(parts)
                steps.append(s_load)

                def qk_step(which, tb):
                    def f():
                        wv = st["w"][which]
                        b = 4 + rot["ip"] % 4
                        rot["ip"] += 1
                        pr = "ps%d" % b
                        for c in range(8):
                            T.op("pe", _mm(ps[b], wv[:, c, :], xT[:, c, tb * 512:(tb + 1) * 512], c == 0, c == 7),
                                 reads=[st["wr"], "xT"], writes=[pr])
                        dst = (qT if which == 0 else kT)[bi]
                        dv = dst.rearrange("p (r m) -> p r m", r=D)[:, :, tb * (512 // D):(tb + 1) * (512 // D)]
                        sv = ps[b].rearrange("p (m r) -> p r m", r=D)
                        name = ("q%d" if which == 0 else "k%d") % bi
                        if which == 0:
                            T.op("act", lambda e, dv=dv, sv=sv: e.mul(out=dv, in_=sv, mul=0.125), reads=[pr], writes=[name])
                        else:
                            T.op("act", _acopy(dv, sv), reads=[pr], writes=[name])
                    return f
                for which in (0, 1):
                    for tb in range(4):
                        steps.append(qk_step(which, tb))

                def v_step(kb2):
                    def f():
                        wv = st["w"][2]
                        b = 4 + rot["ip"] % 4
                        rot["ip"] += 1
                        pr = "ps%d" % b
                        for kk in range(2):
                            kbp = kb2 * 2 + kk
                            r, n = kbp // nb, kbp % nb
                            t0 = r + D * 128 * n
                            for c in range(8):
                                T.op("pe", _mm(ps[b][:, kk * 256:(kk + 1) * 256], xT[:, c, t0:t0 + D * 127 + 1:D],
                                               wv[:, c, :], c == 0, c == 7),
                                     reads=[st["wr"], "xT"], writes=[pr])
                        sv = ps[b].rearrange("p (k q h d) -> p k q h d", k=2, q=2, h=2)
                        T.op("dve", _copy(Vb[bi][:, kb2 * 2:(kb2 + 1) * 2, 0:4:3, :], sv[:, :, 0, :, :]), reads=[pr], writes=["V%d" % bi])
                        T.op("dve", _copy(Vn[g][:, kb2 * 2:(kb2 + 1) * 2, 0:4:3, :], sv[:, :, 1, :, :]), reads=[pr], writes=["Vn%d" % g])
                    return f
                if hp % 2 == 0:
                    for kb2 in range(8):
                        steps.append(v_step(kb2))

                if g == GROUPS[0]:
                    def g_step(tb):
                        def f():
                            wv = st["w"][3]
                            b = 4 + rot["ip"] % 4
                            rot["ip"] += 1
                            pr = "ps%d" % b
                            for c in range(8):
                                T.op("pe", _mm(ps[b], wv[:, c, :], xT[:, c, tb * 512:(tb + 1) * 512], c == 0, c == 7),
                                     reads=[st["wr"], "xT"], writes=[pr])
                            T.op("act", _act(gS[hp % 2][:, tb * 512:(tb + 1) * 512], ps[b], AF.Silu),
                                 reads=[pr], writes=["gS%d" % (hp % 2)])
                        return f
                    for tb in range(4):
                        steps.append(g_step(tb))
                return steps

            def attend_steps(u, bi):
                hp, g = u
                D = DILS[g]
                nb = 16 // D
                m256 = nb > 1
                nbank = 8 if m256 else 4
                items = [(hd, i) for hd in range(2) for i in range(nbank)]
                info = {}
                steps = []

                def S_rec(k):
                    hd, i = items[k]
                    rows = slice(hd * 64, hd * 64 + 64)
                    b = rot["s"] % 2
                    rot["s"] += 1
                    info[k] = {"sb": b}
                    pr = "ps%d" % b
                    if m256:
                        for kk in range(2):
                            kb = 2 * i + kk
                            N = 256 if (kb % nb) != nb - 1 else 128
                            T.op("pe", _mm(ps[b][:, kk * 256:kk * 256 + N], kT[bi][rows, kb * 128:(kb + 1) * 128],
                                           qT[bi][rows, kb * 128:kb * 128 + N], True, True),
                                 reads=["q%d" % bi, "k%d" % bi], writes=[pr])
                    else:
                        for kk in range(4):
                            kb = 4 * i + kk
                            T.op("pe", _mm(ps[b][:, kk * 128:(kk + 1) * 128], kT[bi][rows, kb * 128:(kb + 1) * 128],
                                           qT[bi][rows, kb * 128:(kb + 1) * 128], True, True),
                                 reads=["q%d" % bi, "k%d" % bi], writes=[pr])

                def rest_rec(k):
                    hd, i = items[k]
                    hh = g * 12 + 2 * hp + hd
                    b = info[k]["sb"]
                    eb = rot["e"] % 2
                    rot["e"] += 1
                    pb = rot["pt"] % 4
                    rot["pt"] += 1
                    info[k]["pt"] = pb
                    T.op("act", _act(Eb[eb], ps[b], AF.Exp), reads=["ps%d" % b], writes=["E%d" % eb])
                    nseg, w = (2, 256) if m256 else (4, 128)
                    T.op("dve", _tt(PT[pb].rearrange("p (s w) -> p s w", s=nseg),
                                    Eb[eb].rearrange("p (s w) -> p s w", s=nseg),
                                    EBT[:, hh, 0:w].unsqueeze(1).broadcast_to([128, nseg, w]), ALU.mult),
                         reads=["E%d" % eb, "EBT"], writes=["PT%d" % pb])
                    vsl = slice(hd * 2, hd * 2 + 2)

                    Vbuf, Vname = (Vb[bi], "V%d" % bi) if hp % 2 == 0 else (Vn[g], "Vn%d" % g)

                    def lhs_v(kb):
                        return Vbuf[:, kb, vsl, :].rearrange("p a d -> p (a d)")
                    qbs = [2 * i, 2 * i + 1] if m256 else [4 * i + j for j in range(4)]
                    for qb in qbs:
                        if qb % 4 == 0:
                            ob = 2 + rot["o"] % 2
                            rot["o"] += 1
                            info[("ob", hd)] = ob
                        ob = info[("ob", hd)]
                        orr = "ps%d" % ob
                        oreg = ps[ob][:, (qb % 4) * 128:(qb % 4 + 1) * 128]
                        if m256:
                            has_prev = (qb % nb) != 0
                            if has_prev:
                                if qb == 2 * i + 1:
                                    T.op("pe", _mm(oreg, lhs_v(qb - 1), PT[pb][:, 128:256], True, False),
                                         reads=["PT%d" % pb, Vname], writes=[orr])
                                else:
                                    ppb = info[k - 1]["pt"]
                                    T.op("pe", _mm(oreg, lhs_v(qb - 1), PT[ppb][:, 384:512], True, False),
                                         reads=["PT%d" % ppb, Vname], writes=[orr])
                            c0 = (qb - 2 * i) * 256
                            T.op("pe", _mm(oreg, lhs_v(qb), PT[pb][:, c0:c0 + 128], not has_prev, True),
                                 reads=["PT%d" % pb, Vname], writes=[orr])
                        else:
                            c0 = (qb - 4 * i) * 128
                            T.op("pe", _mm(oreg, lhs_v(qb), PT[pb][:, c0:c0 + 128], True, True),
                                 reads=["PT%d" % pb, Vname], writes=[orr])
                        if qb % 4 == 3:
                            j = qb // 4
                            an = "acc%d" % hd
                            if g == 0:
                                av, sv = acc[hd][:, j * 512:(j + 1) * 512], ps[ob]
                            elif g == 1:
                                av, sv = acc[hd][:, j:SEQ:4], ps[ob]
                            else:
                                av = acc[hd].rearrange("p (m r) -> p r m", r=16)[:, 4 * j:4 * j + 4, :]
                                sv = ps[ob].rearrange("p (r m) -> p r m", r=4)
                            if g == GROUPS[0]:
                                T.op("dve", _copy(av, sv), reads=[orr], writes=[an])
                            else:
                                T.op("dve", _tt(av, sv, av, ALU.add), reads=[orr, an], writes=[an])

                steps.append(lambda: S_rec(0))
                for k in range(len(items)):
                    def f(k=k):
                        if k + 1 < len(items):
                            S_rec(k + 1)
                        rest_rec(k)
                    steps.append(f)
                if g == GROUPS[-1]:
                    def fin():
                        gb = "gS%d" % (hp % 2)
                        gs = gS[hp % 2]
                        T.op("act", _act(rden[0:64, :], acc[0][64:128, :], AF.Ln), reads=["acc0"], writes=["rdenA"])
                        T.op("act", _act(rden[0:64, :], rden[0:64, :], AF.Exp, scale=-1.0), reads=["rdenA"], writes=["rdenA"])
                        T.op("dve", _tt(acc[0][0:64, :], acc[0][0:64, :], rden[0:64, :], ALU.mult),
                             reads=["acc0", "rdenA"], writes=["acc0"])
                        T.op("dve", _tt(oattT[0:64, hp, :], acc[0][0:64, :], gs[0:64, :], ALU.mult),
                             reads=["acc0", gb], writes=["oattT"])
                        T.op("act", _act(rden[64:128, :], acc[1][0:64, :], AF.Ln), reads=["acc1"], writes=["rdenB"])
                        T.op("act", _act(rden[64:128, :], rden[64:128, :], AF.Exp, scale=-1.0), reads=["rdenB"], writes=["rdenB"])
                        T.op("dve", _tt(acc[1][64:128, :], acc[1][64:128, :], rden[64:128, :], ALU.mult),
                             reads=["acc1", "rdenB"], writes=["acc1"])
                        T.op("dve", _tt(oattT[64:128, hp, :], acc[1][64:128, :], gs[64:128, :], ALU.mult),
                             reads=["acc1", gb], writes=["oattT"])
                    steps.append(fin)
                return steps

            for f in inproj_steps(units[0], 0):
                f()
            for i, u in enumerate(units):
                A = attend_steps(u, i % 2)
                B = inproj_steps(units[i + 1], (i + 1) % 2) if i + 1 < len(units) else []
                for f in _interleave(A, B):
                    f()
                if s == 0:
                    W.emit_conversions(2 * i, 2 * i + 2 if i + 1 < len(units) else 10 ** 9)
            if debug == "oatt":
                for hp in range(6):
                    finals.append(T.dma("pool", "dbg", _dma(dbg_d[s, hp * 128:(hp + 1) * 128, :], oattT[:, hp, :]),
                                        reads=["oattT"]))

        def stream(s):
            T.dma("sp", "c3", _dma(nw_bc, nw_d.partition_broadcast(128)), writes=["nw"])
            T.dma("sp", "c4", _dma(b2_bc, b2_d.partition_broadcast(128)), writes=["b2"])
            T.dma("sp", "c5", _dma(lng_bc, lng_d.partition_broadcast(128)), writes=["lng"])
            T.dma("sp", "c6", _dma(lnb_bc, lnb_d.partition_broadcast(128)), writes=["lnb"])
            T.op("dve", _memset(Sst, 0.0), writes=["S0", "S1", "S2", "S3"])
            T.op("dve", _memset(Sbf, 0.0), writes=["Sb0", "Sb1", "Sb2", "Sb3"])
            T.op("dve", _memset(hist, 0.0), writes=["hist%d" % q for q in range(24)])
            rot = {"u": 0, "c": 0, "k": 0}
            def load_xp(blk):
                tsl_ = slice(blk * 512, (blk + 1) * 512)
                q2 = blk % 2
                T.dma("pool", "xb%d" % q2, _dma(xTb2[q2], xT_d[s].rearrange("(c p) t -> p c t", p=128)[:, :, tsl_]),
                      writes=["xTb%d" % q2])
                T.dma("pool", "pb%d" % q2, _dma(pTb2[q2], pT_d[s].rearrange("(c p) t -> p c t", p=128)[:, :, tsl_]),
                      writes=["pTb%d" % q2])

            load_xp(0)
            for blk in range(4):
                tsl = slice(blk * 512, (blk + 1) * 512)
                xTb, pTb = xTb2[blk % 2], pTb2[blk % 2]
                XB, PB = "xTb%d" % (blk % 2), "pTb%d" % (blk % 2)
                if blk + 1 < 4:
                    load_xp(blk + 1)
                W.block_start()
                T.dma("sp", "xr", _dma(xres, x_d[s, tsl, :].rearrange("(t p) d -> p t d", p=128)),
                      writes=["xres", "xres0", "xres1", "xres2", "xres3"])

                for cg in range(6):
                    (wv,), wr = W.next([(w_in_v[:, :, XBC0 + cg * 512: XBC0 + (cg + 1) * 512], [8, 512])], cached=True)
                    for j in range(4):
                        cc = cg * 4 + j
                        b = rot["u"] % 2
                        rot["u"] += 1
                        pr = "ps%d" % b
                        for c in range(8):
                            T.op("pe", _mm(ps[b], wv[:, c, j * 128:(j + 1) * 128], xTb[:, c, :], c == 0, c == 7),
                                 reads=[wr, XB], writes=[pr])
                        ur, ct = uraw[b], ctmp[b]
                        T.op("act", _acopy(ur[:, 0:3], hist[:, cc, :]), reads=["hist%d" % cc], writes=["urh%d" % b])
                        T.op("act", _acopy(ur[:, 3:515], ps[b]), reads=[pr], writes=["ur%d" % b])
                        T.op("act", _acopy(hist[:, cc, :], ur[:, 512:515]), reads=["ur%d" % b], writes=["hist%d" % cc])
                        T.op("act", _act(ct, ps[b], AF.Identity, bias=cb[:, cc:cc + 1], scale=cw[:, cc, 3:4]),
                             reads=[pr, "cw", "cb"], writes=["ct%d" % b])
                        for k in (2, 1, 0):
                            T.op("dve", _stt(ct, ur[:, k:k + 512], cw[:, cc, k:k + 1], ct, ALU.mult, ALU.add),
                                 reads=["ur%d" % b, "urh%d" % b, "cw", "ct%d" % b], writes=["ct%d" % b])
                        xw = ["XT%d_%d" % (cc // 4, q4) for q4 in range(4)] if cc < 16 else ["XTbc"]
                        T.op("act", _act(XT[:, cc, :], ct, AF.Silu), reads=["ct%d" % b], writes=xw)

                T.barrier()
                (wdt,), wr = W.next([(w_in_v[:, :, DT0:DT0 + 32], [8, 32])], cached=True)
                for c4 in range(4):
                    csl = slice(c4 * 128, (c4 + 1) * 128)
                    for c in range(8):
                        T.op("pe", _mm(ps[2][:, c4 * 32:(c4 + 1) * 32], xTb[:, c, csl], wdt[:, c, :], c == 0, c == 7),
                             reads=[wr, XB], writes=["ps2"])
                T.op("dve", _tt(sm_t, ps[2][:, 0:128].rearrange("p (a h) -> p a h", a=4),
                                dtb_bc.unsqueeze(1).broadcast_to([128, 4, 32]), ALU.add),
                     reads=["ps2", "dtb"], writes=["sm_t"])
                T.op("act", _act(sm_e, sm_t, AF.Exp), reads=["sm_t"], writes=["sm_e"])
                T.op("act", _act(dtc, sm_e, AF.Ln, bias=1.0), reads=["sm_e"], writes=["dtc"])
                T.op("dve", _tt(ac, dtc, A_bc.unsqueeze(1).broadcast_to([128, 4, 32]), ALU.mult),
                     reads=["dtc", "A"], writes=["ac"])
                for c4 in range(4):
                    T.op("pe", _mm(ps[3][:, c4 * 32:(c4 + 1) * 32], tri32, ac[:, c4, :], True, True),
                         reads=["tri32", "ac"], writes=["ps3"])
                    T.op("pe", _mm(ps[3][:, 128 + c4 * 32:128 + (c4 + 1) * 32], ones32, ac[:, c4, :], True, True),
                         reads=["ones32", "ac"], writes=["ps3"])
                cs_ps = ps[3][:, 0:128].rearrange("p (a h) -> p a h", a=4)
                tot_ps = ps[3][:, 128:256].rearrange("p (a h) -> p a h", a=4)
                T.op("act", _acopy(csb, cs_ps), reads=["ps3"], writes=["csb"])
                T.op("act", _act(dstart, cs_ps, AF.Exp), reads=["ps3"], writes=["dstart"])
                T.op("act", _act(cdec, tot_ps, AF.Exp), reads=["ps3"], writes=["cdec"])
                T.op("dve", _tt(sm_t, tot_ps, csb, ALU.subtract), reads=["ps3", "csb"], writes=["sm_t"])
                T.op("act", _act(dend, sm_t, AF.Exp), reads=["sm_t"], writes=["dend"])

                its = [(g, c4) for g in range(4) for c4 in range(4)]
                wzs = {}

                def ssd_front(i):
                    g, c4 = its[i]
                    k = i % 2
                    kk = str(k)
                    hs = slice(8 * g, 8 * g + 8)
                    csl = slice(c4 * 128, (c4 + 1) * 128)
                    xn = "XT%d_%d" % (g, c4)
                    if c4 == 0:
                        wzs[g] = W.next([(w_in_v[:, :, Z0 + g * 512: Z0 + (g + 1) * 512], [8, 512])], cached=True)
                    (wz,), wr = wzs[g]
                    if c4 == 0:
                        for c4b in range(4):
                            for c in range(8):
                                T.op("pe", _mm(ps[0], xTb[:, c, c4b * 128:(c4b + 1) * 128], wz[:, c, :], c == 0, c == 7),
                                     reads=[wr, XB], writes=["ps0"])
                            T.op("act", _act(zsall[:, i + c4b, :], ps[0], AF.Silu), reads=["ps0"], writes=["zs%d" % (i + c4b)])
                    NA = RHSA_ACT
                    for h in range(NA):
                        T.op("act", _act(rhsa[k][:, h, :], tri, AF.Copy, scale=ac[:, c4, 8 * g + h:8 * g + h + 1]),
                             reads=["tri", "ac"], writes=["rhsa%s_%d" % (kk, h // 4)])
                    if NA < 8:
                        T.op("dve", _tt(rhsa[k][:, NA:8, :], tri.unsqueeze(1).broadcast_to([128, 8 - NA, 128]),
                                        ac[:, c4, 8 * g + NA:8 * g + 8].unsqueeze(2).broadcast_to([128, 8 - NA, 128]), ALU.mult),
                             reads=["tri", "ac"], writes=["rhsa%s_1" % kk] + (["rhsa%s_0" % kk] if NA < 4 else []))
                    for j in range(4):
                        T.op("pe", _tr(psb[2][:, j * 128:(j + 1) * 128], XT[:, 4 * g + j, csl], ident),
                             reads=[xn, "ident"], writes=["ps2"])
                    T.op("pe", _tr(psb[2][:, 512:640], XT[:, 16 + g, csl], ident), reads=["XTbc", "ident"], writes=["ps2"])
                    xtm = psb[2][:, 0:512].rearrange("p (h d) -> p h d", h=8)
                    T.op("dve", _tt(xdt[k].rearrange("p (h d) -> p h d", h=8), xtm,
                                    dtc[:, c4, hs].unsqueeze(2).broadcast_to([128, 8, 64]), ALU.mult),
                         reads=["ps2", "dtc"], writes=["xdt" + kk])
                    T.op("dve", _tt(xD[k].rearrange("p (h d) -> p h d", h=8), xtm,
                                    D_bc[:, hs].unsqueeze(2).broadcast_to([128, 8, 64]), ALU.mult),
                         reads=["ps2", "D"], writes=["xD" + kk])
                    T.op("act", _acopy(Btm[k], psb[2][:, 512:640]), reads=["ps2"], writes=["Btm" + kk])
                    T.op(PENG, _tt(xdtd[k].rearrange("p (h d) -> p h d", h=8),
                                     xdt[k].rearrange("p (h d) -> p h d", h=8),
                                     dend[:, c4, hs].unsqueeze(2).broadcast_to([128, 8, 64]), ALU.mult),
                         reads=["xdt" + kk, "dend"], writes=["xdtd" + kk])
                    T.op("pe", _mm(ps[3][:, 256:384], XT[:, 16 + g, csl], XT[:, 20 + g, csl], True, True),
                         reads=["XTbc"], writes=["ps3g"])
                    T.op("act", _acopy(Gm[k], ps[3][:, 256:384]), reads=["ps3g"], writes=["Gm" + kk])
                    for q in range(2):
                        T.op("pe", _mm(ps[4 + q], Umat, rhsa[k][:, 4 * q:4 * q + 4, :].rearrange("p h l -> p (h l)"),
                                       True, False), reads=["U", "rhsa%s_%d" % (kk, q)], writes=["ps%d" % (4 + q)])
                        T.op("pe", _mm(ps[4 + q], ident, NEGM.rearrange("p h l -> p (h l)"), False, True),
                             reads=["ident", "NEGM"], writes=["ps%d" % (4 + q)])
                        T.op("act", _act(Es[q], ps[4 + q], AF.Exp), reads=["ps%d" % (4 + q)], writes=["Es%d" % q])
                        T.op("dve", _tt(MT[k][:, 4 * q:4 * q + 4, :], Es[q].rearrange("p (h l) -> p h l", h=4),
                                        Gm[k].unsqueeze(1).broadcast_to([128, 4, 128]), ALU.mult),
                             reads=["Es%d" % q, "Gm" + kk], writes=["MT" + kk])

                def ssd_back(i):
                    g, c4 = its[i]
                    k = i % 2
                    kk = str(k)
                    hs = slice(8 * g, 8 * g + 8)
                    csl = slice(c4 * 128, (c4 + 1) * 128)
                    xn = "XT%d_%d" % (g, c4)
                    T.op("pe", _c(lambda e, k=k: e.matmul(ps[6], ident, xD[k], start=True, stop=False, skip_group_check=True), 0.216),
                         reads=["ident", "xD" + kk], writes=["ps6"])
                    for h in range(8):
                        T.op("pe", _c(lambda e, k=k, h=h: e.matmul(ps[6][:, h * 64:(h + 1) * 64], MT[k][:, h, :],
                                                                   xdt[k][:, h * 64:(h + 1) * 64], start=False, stop=True,
                                                                   skip_group_check=True), 0.096),
                             reads=["MT" + kk, "xdt" + kk], writes=["ps6"])
                    T.op("pe", _mm(ps[7], XT[:, 20 + g, csl], Sbf[:, g, :], True, True),
                         reads=["XTbc", "Sb%d" % g], writes=["ps7"])
                    T.op("pe", _mm(ps[1], Btm[k], xdtd[k], True, True), reads=["Btm" + kk, "xdtd" + kk], writes=["ps1"])
                    T.op("dve", _tt(t1[k].rearrange("p (h d) -> p h d", h=8), ps[7].rearrange("p (h d) -> p h d", h=8),
                                    dstart[:, c4, hs].unsqueeze(2).broadcast_to([128, 8, 64]), ALU.mult),
                         reads=["ps7", "dstart"], writes=["t1" + kk])
                    T.op("dve", _tt(t1[k], ps[6], t1[k], ALU.add), reads=["ps6", "t1" + kk], writes=["t1" + kk])
                    T.op("dve", _tt(ug[k], t1[k], zsall[:, i, :], ALU.mult), reads=["t1" + kk, "zs%d" % i], writes=["ug" + kk])
                    T.op("act", _act(usq, ug[k], AF.Square, accum_out=ssq[k][:, 0:1]), reads=["ug" + kk],
                         writes=["usq", "ssq" + kk])
                    T.op("act", _act(ssq[k][:, 1:2], ssq[k][:, 0:1], AF.Ln, bias=epsr[:, 0:1], scale=1.0 / 512.0),
                         reads=["ssq" + kk, "epsr"], writes=["ssqb" + kk])
                    T.op("act", _act(ssq[k][:, 3:4], ssq[k][:, 1:2], AF.Exp, scale=-0.5), reads=["ssqb" + kk], writes=["ssqd" + kk])
                    T.op("dve", _stt(yb[k], ug[k], ssq[k][:, 3:4], nw_bc[:, g * 512:(g + 1) * 512], ALU.mult, ALU.mult),
                         reads=["ug" + kk, "ssqd" + kk, "nw"], writes=["yb" + kk])
                    Sg = Sst[:, g, :]
                    T.op(PENG, _tt(Sg.rearrange("p (h d) -> p h d", h=8), Sg.rearrange("p (h d) -> p h d", h=8),
                                     cdec[:, c4, hs].unsqueeze(2).broadcast_to([128, 8, 64]), ALU.mult),
                         reads=["S%d" % g, "cdec"], writes=["S%d" % g])
                    T.op("dve", _tt(Sg, ps[1], Sg, ALU.add), reads=["ps1", "S%d" % g], writes=["S%d" % g])
                    T.op("act", _acopy(Sbf[:, g, :], Sg), reads=["S%d" % g], writes=["Sb%d" % g])
                    for j in range(4):
                        T.op("pe", _tr(psb[3][:, j * 128:(j + 1) * 128], yb[k][:, j * 128:(j + 1) * 128], ident),
                             reads=["yb" + kk, "ident"], writes=["ps3t"])
                    T.op("act", _acopy(XT[:, 4 * g:4 * g + 4, csl], psb[3][:, 0:512].rearrange("p (j t) -> p j t", j=4)),
                         reads=["ps3t"], writes=[xn])

                ssd_front(0)
                for i in range(16):
                    if i + 1 < 16:
                        ssd_front(i + 1)
                    ssd_back(i)

                if debug == "yssm":
                    for j in range(16):
                        finals.append(T.dma("pool", "dbg", _dma(dbg_d[s, j * 128:(j + 1) * 128, tsl], XT[:, j, :]),
                                            reads=["XT%d_%d" % (j // 4, q4) for q4 in range(4)]))
                T.barrier()

                for j in range(8):
                    dsl = slice(j * 128, (j + 1) * 128)
                    (wb0, wb1, wga, wgb), wr = W.next([(w_br_v[:, 0:11, dsl], [11, 128]), (w_br_v[:, 11:22, dsl], [11, 128]),
                                                 (w_in_v[:, :, GM0 + j * 128: GM0 + (j + 1) * 128], [8, 128]),
                                                 (w_in_v[:, :, GM0 + 1024 + j * 128: GM0 + 1024 + (j + 1) * 128], [8, 128])], cached=True)
                    o = 4 * (j % 2)
                    pn = ["ps%d" % (o + q) for q in range(4)]
                    for i in range(6):
                        T.op("pe", _mm(ps[o], wb0[:, i, :], oattT[:, i, tsl], i == 0, i == 5), reads=[wr, "oattT"], writes=[pn[0]])
                    for i in range(16):
                        T.op("pe", _mm(ps[o + 1], (wb0[:, 6 + i, :] if i < 5 else wb1[:, i - 5, :]), XT[:, i, :], i == 0, i == 15),
                             reads=[wr] + ["XT%d_%d" % (i // 4, q4) for q4 in range(4)], writes=[pn[1]])
                    for c in range(8):
                        T.op("pe", _mm(ps[o + 2], wga[:, c, :], xTb[:, c, :], c == 0, c == 7), reads=[wr, XB], writes=[pn[2]])
                    for c in range(8):
                        T.op("pe", _mm(ps[o + 3], wgb[:, c, :], xTb[:, c, :], c == 0, c == 7), reads=[wr, XB], writes=[pn[3]])
                    k = j % 2
                    kk = str(k)
                    T.op("act", _act(sa[k], ps[o + 2], AF.Sigmoid, bias=bg[:, 0, j:j + 1]), reads=[pn[2], "bg"], writes=["sa" + kk])
                    T.op("dve", _tt(m1[k], ps[o], sa[k], ALU.mult), reads=[pn[0], "sa" + kk], writes=["m1" + kk])
                    T.op("act", _act(sgt[k], ps[o + 3], AF.Sigmoid, bias=bg[:, 1, j:j + 1]), reads=[pn[3], "bg"], writes=["sgt" + kk])
                    T.op("dve", _tt(tpt[k], ps[o + 1], sgt[k], ALU.mult), reads=[pn[1], "sgt" + kk], writes=["tpt" + kk])
                    T.op(PENG, _tt(mergedT[:, j, :], m1[k], tpt[k], ALU.add), reads=["m1" + kk, "tpt" + kk], writes=["mergedT"])
                for half in range(2):
                    hsl = slice(half * 512, (half + 1) * 512)
                    (wo,), wr = W.next([(w_out_v[:, :, hsl], [8, 512])], cached=True)
                    for tt in range(4):
                        tts = slice(tt * 128, (tt + 1) * 128)
                        b = tt % 2
                        for j in range(8):
                            T.op("pe", _mm(ps[b], mergedT[:, j, tts], wo[:, j, :], j == 0, j == 7),
                                 reads=[wr, "mergedT"], writes=["ps%d" % b])
                        T.op("dve", _stt(xres[:, tt, hsl], xres[:, tt, hsl], ALPHA, ps[b], ALU.mult, ALU.add),
                             reads=["ps%d" % b, "xres"], writes=["xres"])
                    (wgp, wpl), wr = W.next([(w_in_v[:, :, GPLE0 + half * 512: GPLE0 + (half + 1) * 512], [8, 512]),
                                             (w_ple_v[:, :, hsl], [2, 512])], cached=True)
                    for tt in range(4):
                        tts = slice(tt * 128, (tt + 1) * 128)
                        k = tt % 2
                        kk = str(k)
                        for c in range(8):
                            T.op("pe", _mm(ps[2 + k], xTb[:, c, tts], wgp[:, c, :], c == 0, c == 7),
                                 reads=[wr, XB], writes=["ps%d" % (2 + k)])
                        for c in range(2):
                            T.op("pe", _mm(ps[4 + k], pTb[:, c, tts], wpl[:, c, :], c == 0, c == 1),
                                 reads=[wr, PB], writes=["ps%d" % (4 + k)])
                        T.op("dve", _tt(sa[k], ps[2 + k], b2_bc[:, hsl], ALU.add), reads=["ps%d" % (2 + k), "b2"], writes=["sa" + kk])
                        T.op("act", _act(sgt[k], sa[k], AF.Sigmoid), reads=["sa" + kk], writes=["sgt" + kk])
                        T.op("dve", _tt(tpt[k], ps[4 + k], sgt[k], ALU.mult), reads=["ps%d" % (4 + k), "sgt" + kk], writes=["tpt" + kk])
                        T.op("dve", _tt(xres[:, tt, hsl], xres[:, tt, hsl], tpt[k], ALU.add), reads=["xres", "tpt" + kk], writes=["xres"])
                T.barrier()
                for tt in range(4):
                    k = tt % 2
                    kk = str(k)
                    r = xres[:, tt, :]
                    rn = "xres%d" % tt
                    st_ = lnst[k]
                    T.op("dve", lambda e, st_=st_, r=r: e.reduce_sum(out=st_[:, 0:1], in_=r, axis=AX.X), reads=["xres", rn], writes=["lnst" + kk])
                    T.op("dve", _ts(st_[:, 1:2], st_[:, 0:1], 1.0 / 1024.0, None, ALU.mult), reads=["lnst" + kk], writes=["lnstb" + kk])
                    T.op("dve", _ts(r, r, st_[:, 1:2], None, ALU.subtract), reads=["xres", rn, "lnstb" + kk], writes=[rn])
                    T.op("act", _act(lnsq, r, AF.Square, accum_out=st_[:, 2:3]), reads=[rn], writes=["lnsq", "lnstc" + kk])
                    T.op("act", _act(st_[:, 3:4], st_[:, 2:3], AF.Ln, bias=epsln[:, 0:1], scale=1.0 / 1024.0),
                         reads=["lnstc" + kk, "epsln"], writes=["lnstd" + kk])
                    T.op("act", _act(st_[:, 0:1], st_[:, 3:4], AF.Exp, scale=-0.5), reads=["lnstd" + kk], writes=["lnst" + kk])
                    T.op("dve", _stt(r, r, st_[:, 0:1], lng_bc, ALU.mult, ALU.mult), reads=[rn, "lnst" + kk, "lng"], writes=[rn])
                    T.op("dve", _tt(r, r, lnb_bc, ALU.add), reads=[rn, "lnb"], writes=[rn])
                    t0 = blk * 512 + tt * 128
                    finals.append(T.dma("sp", "st%d" % k, _dma(out_d[s, t0:t0 + 128, :], r), reads=[rn]))

        for s in range(NSEQ):
            attention(s)
            T.barrier()
            if debug != "oatt":
                stream(s)
                T.barrier()
        return finals

    T0 = Tracker(nc)
    W0 = WRing(T0, wslots, None, scratch=wscr)
    construct(T0, W0)
    assert len(W0.cache) <= NCACHE, len(W0.cache)
    T = Tracker(nc)
    W = WRing(T, wslots, W0.plan, cache=W0.cache, scratch=wscr)
    finals = construct(T, W)
    assert W.cur == len(W0.plan) and W.issued == len(W0.plan)
    if os.environ.get("MK_NOSCHED") is None:
        T.schedule()
    T.emit(finals)
    return nc


def _t5_bucket_np(dist):
    max_exact = 16
    d_f = np.maximum(dist, 1).astype(np.float32)
    large = max_exact + (np.log(d_f / np.float32(max_exact)) / np.float32(math.log(2048 / max_exact))
                         * np.float32(32 - max_exact)).astype(np.int32)
    large = np.minimum(large, 31)
    return np.where(dist < max_exact, dist, large)


def _host_consts():
    ki = np.arange(128)[:, None]
    qi = np.arange(128)[None, :]
    d_cur = qi - ki
    d_nxt = qi + 128 - ki
    delta = np.concatenate([d_cur, d_nxt], axis=1)
    valid = (delta >= 0) & (delta <= 128)
    maskT = valid.astype(np.float32)
    idx = np.stack([_t5_bucket_np(np.maximum(delta, 0) * d) for d in DILS])
    p = np.arange(128)[:, None]
    f = np.arange(128)[None, :]
    cmat = np.stack([(p == f), (p <= f), (p > f), np.ones((128, 128), bool)]).astype(np.float32)
    return maskT, idx, cmat


_NC_CACHE = {}


def kernel(x, p, w_in, b_gate, conv_w, conv_b, dt_bias, a_log, d_skip, ssm_norm_w,
           w_branch, w_out, w_ple, ln_g, ln_b, rel_bias):
    debug = os.environ.get("MK_DEBUG") or None
    f32 = np.float32
    x = np.asarray(x, f32)
    p = np.asarray(p, f32)[0]
    maskT, idx, cmat = _host_consts()
    rel_bias = np.asarray(rel_bias, f32)
    biasT = np.stack([rel_bias[idx[hh // 12], hh] for hh in range(36)]).astype(f32)
    cw = np.ascontiguousarray(np.asarray(conv_w, f32)[0].T.reshape(24, 128, 4).transpose(1, 0, 2))
    cb = np.ascontiguousarray(np.asarray(conv_b, f32)[0].reshape(24, 128).T)
    bgate = np.asarray(b_gate, f32)[0]
    bg = np.ascontiguousarray(bgate[0:2].reshape(2, 8, 128).transpose(2, 0, 1))
    shared = {
        "w_in": np.ascontiguousarray(np.asarray(w_in, f32)[0]),
        "w_branch": np.ascontiguousarray(np.asarray(w_branch, f32)[0]),
        "w_out": np.ascontiguousarray(np.asarray(w_out, f32)[0]),
        "w_ple": np.ascontiguousarray(np.asarray(w_ple, f32)[0]),
        "biasT": biasT, "maskT": maskT, "cmat": cmat, "cw": cw, "cb": cb, "bg": bg,
        "b2": np.ascontiguousarray(bgate[2]),
        "dt_bias": np.ascontiguousarray(np.asarray(dt_bias, f32)[0]),
        "a_log": np.ascontiguousarray(np.asarray(a_log, f32)[0]),
        "d_skip": np.ascontiguousarray(np.asarray(d_skip, f32)[0]),
        "ssm_norm_w": np.ascontiguousarray(np.asarray(ssm_norm_w, f32)[0]),
        "ln_g": np.ascontiguousarray(np.asarray(ln_g, f32)[0]),
        "ln_b": np.ascontiguousarray(np.asarray(ln_b, f32)[0]),
    }
    in_maps = []
    for c in range(N_CORES):
        xs = x[c * NSEQ:(c + 1) * NSEQ]
        m = dict(shared)
        m["x"] = np.ascontiguousarray(xs)
        m["xT"] = np.ascontiguousarray(xs.transpose(0, 2, 1))
        m["pT"] = np.ascontiguousarray(p[c * NSEQ:(c + 1) * NSEQ].transpose(0, 2, 1))
        in_maps.append(m)
    if debug not in _NC_CACHE:
        _NC_CACHE[debug] = build_program(debug)
    nc = _NC_CACHE[debug]
    ncores = int(os.environ.get("MK_CORES", N_CORES))
    res = run_bass_kernel_spmd(nc, in_maps[:ncores], core_ids=list(range(ncores)))
    if debug:
        return [r["dbg"] for r in res.results]
    return np.concatenate([r["out"] for r in res.results], axis=0).astype(f32)
```

```python
import math
import os
import contextlib
import numpy as np
import concourse.bass as bass
import concourse.mybir as mybir
from concourse.bass_utils import run_bass_kernel_spmd

F32 = mybir.dt.float32
BF16 = mybir.dt.bfloat16
U8 = mybir.dt.uint8
AF = mybir.ActivationFunctionType
ALU = mybir.AluOpType
AX = mybir.AxisListType

N_CORES = 8
D_MODEL = 1024
SEQ = 2048
NSEQ = 2
K0, V0, GATT0, Z0, XBC0, DT0, GM0, GPLE0, IN_COLS = 2304, 4608, 6912, 7680, 9728, 12800, 12832, 14880, 15904
DILS = (1, 4, 16)
ALPHA = 2.0 ** 0.25
LN_EPS = 1e-5
RMS_EPS = 1e-5
WSLOT = 5120
NRING = 3
RHSA_ACT = int(os.environ.get("MK_RHSA_ACT", "0"))
PENG = "dve"


class Node:
    __slots__ = ("eng", "idx", "fn", "deps", "signal", "sigval", "dma", "cost", "lat", "gidx", "fin", "res", "odeps", "epoch", "table")

    def __init__(self, eng, idx, fn, dma=None):
        self.eng = eng
        self.idx = idx
        self.fn = fn
        self.cost = getattr(fn, "cost", 0.5)
        self.lat = getattr(fn, "lat", 0.0)
        self.table = getattr(fn, "table", None)
        self.gidx = 0
        self.fin = None
        self.res = ()
        self.odeps = []
        self.deps = []
        self.signal = False
        self.sigval = None
        self.dma = dma


class Tracker:
    ENGS = ("pe", "act", "dve", "pool", "sp")

    def __init__(self, nc):
        self.nc = nc
        self.ops = {e: [] for e in self.ENGS}
        self.lastw = {}
        self.readers = {}
        self.dma_cnt = {}
        self.dma_latest = {}
        self.bank_last = {}
        self.pending = {e: [] for e in self.ENGS}

    def _add(self, node, reads, writes):
        self.gcount = getattr(self, "gcount", 0) + 1
        node.gidx = self.gcount
        node.epoch = getattr(self, "epoch", 0)
        node.res = (tuple(reads), tuple(writes))
        deps = {}

        def add_dep(n):
            if n is not None and n is not node:
                deps[id(n)] = n

        for r in reads:
            add_dep(self.lastw.get(r))
        for w in writes:
            add_dep(self.lastw.get(w))
            for n in self.readers.get(w, {}).values():
                add_dep(n)
        banks = {int(r[2]) for r in list(reads) + list(writes) if r.startswith("ps") and r[2].isdigit()}
        for b in banks:
            bl = self.bank_last.setdefault(b, {})
            for e, n in bl.items():
                if e != node.eng:
                    add_dep(n)
            bl[node.eng] = node
        for n in self.pending[node.eng]:
            add_dep(n)
        self.pending[node.eng] = []
        node.deps = list(deps.values())
        key = ("dma", node.dma[0]) if node.dma else node.eng
        for r in reads:
            self.readers.setdefault(r, {})[key] = node
        for w in writes:
            self.lastw[w] = node
            self.readers[w] = {}

    def op(self, eng, fn, reads=(), writes=()):
        node = Node(eng, len(self.ops[eng]), fn)
        self.ops[eng].append(node)
        self._add(node, reads, writes)
        return node

    def dma(self, eng, slot, fn, reads=(), writes=()):
        self.dma_cnt[slot] = self.dma_cnt.get(slot, 0) + 16
        node = Node(eng, len(self.ops[eng]), fn, dma=(slot, self.dma_cnt[slot]))
        self.ops[eng].append(node)
        self._add(node, reads, writes)
        self.dma_latest[slot] = node
        return node

    def barrier(self):
        self.epoch = getattr(self, "epoch", 0) + 1
        last = [self.ops[e][-1] for e in self.ENGS if self.ops[e]]
        last += [n for sl, n in self.dma_latest.items() if sl != "cv"]
        for e in self.ENGS:
            self.pending[e] = list(last)

    def schedule(self, window=48, vis=0.15):
        allnodes = sorted((n for e in self.ENGS for n in self.ops[e]), key=lambda n: n.gidx)
        wcount = {}
        for n in allnodes:
            for w in n.res[1]:
                wcount[w] = wcount.get(w, 0) + 1
        last_acc = {}
        for n in allnodes:
            keys = set()
            for r in n.res[0] + n.res[1]:
                if wcount.get(r, 0) >= 2:
                    keys.add(r)
                if r.startswith("ps") and r[2].isdigit():
                    keys.add(("bank", int(r[2])))
            for k in keys:
                p = last_acc.get((k, n.eng))
                if p is not None:
                    n.odeps.append(p)
                last_acc[(k, n.eng)] = n
        rem = {e: list(self.ops[e]) for e in self.ENGS}
        out = {e: [] for e in self.ENGS}
        free = {e: 0.0 for e in self.ENGS}
        cur_epoch = {e: -1 for e in self.ENGS}
        cur_tab = [None]
        nleft = sum(len(v) for v in rem.values())
        while nleft:
            best = None
            for e in self.ENGS:
                lst = rem[e]
                if not lst:
                    continue
                cand = None
                wnd = lst[:window] if lst[0].epoch == cur_epoch[e] else lst[:1]
                for n in wnd:
                    if n.epoch != lst[0].epoch:
                        break
                    rdy = 0.0
                    ok = True
                    for d in n.odeps:
                        if d.fin is None:
                            ok = False
                            break
                    if not ok:
                        continue
                    for d in n.deps:
                        if d.fin is None:
                            ok = False
                            break
                        if d.eng == "pe" and e == "pe" and d.dma is None and n.dma is None:
                            continue
                        if d.fin + vis > rdy:
                            rdy = d.fin + vis
                    if not ok:
                        continue
                    st = max(rdy, free[e])
                    pen = 0.0
                    if e == "act" and n.table is not None and not (n.table == cur_tab[0] or (n.table == "E" and cur_tab[0] == "L")):
                        pen = 1.3
                    if cand is None or st + pen < cand[0] - 1e-9:
                        cand = (st + pen, n)
                    if st + pen <= free[e] + 1e-9:
                        break
                if cand is not None and (best is None or cand[0] < best[0] - 1e-9 or
                                         (abs(cand[0] - best[0]) <= 1e-9 and cand[1].gidx < best[1].gidx)):
                    best = cand
            st, n = best
            e = n.eng
            if e == "act" and n.table is not None and not (n.table == cur_tab[0] or (n.table == "E" and cur_tab[0] == "L")):
                cur_tab[0] = n.table
            rem[e].remove(n)
            out[e].append(n)
            cur_epoch[e] = n.epoch
            free[e] = st + n.cost
            n.fin = st + n.cost + n.lat
            nleft -= 1
        self.ops = out
        self.est_us = max(free.values())

    def emit(self, final_nodes):
        nc = self.nc
        for e in self.ENGS:
            for n in self.ops[e]:
                for d in n.deps:
                    if d.dma is None:
                        if d.eng == "pe" and n.eng == "pe" and n.dma is None:
                            continue
                        d.signal = True
        for n in final_nodes:
            if n.dma is None:
                n.signal = True
        for e in self.ENGS:
            c = 0
            for n in self.ops[e]:
                if n.dma is None and n.signal:
                    c += 1
                    n.sigval = c
        with contextlib.ExitStack() as st:
            esem = {e: st.enter_context(nc.semaphore("s_" + e)) for e in self.ENGS}
            dsem = {s: st.enter_context(nc.semaphore("d_" + s)) for s in self.dma_cnt}
            block = st.enter_context(nc.Block())

            def run(ename, eng):
                waited = {}
                for n in self.ops[ename]:
                    need = {}
                    for d in n.deps:
                        if d.dma is not None:
                            k, v = ("d", d.dma[0]), d.dma[1]
                        else:
                            if d.eng == "pe" and ename == "pe" and n.dma is None:
                                continue
                            k, v = ("e", d.eng), d.sigval
                        if v > need.get(k, 0):
                            need[k] = v
                    for k, v in need.items():
                        if waited.get(k, 0) >= v:
                            continue
                        waited[k] = v
                        eng.wait_ge(dsem[k[1]] if k[0] == "d" else esem[k[1]], v)
                    ins = n.fn(eng)
                    if n.dma is not None:
                        ins.then_inc(dsem[n.dma[0]], 16)
                    elif n.signal:
                        ins.then_inc(esem[ename], 1)
                if ename == "sp":
                    fin = {}
                    for n in final_nodes:
                        k, v = (("d", n.dma[0]), n.dma[1]) if n.dma is not None else (("e", n.eng), n.sigval)
                        fin[k] = max(fin.get(k, 0), v)
                    for k, v in fin.items():
                        eng.wait_ge(dsem[k[1]] if k[0] == "d" else esem[k[1]], v)

            block.tensor(lambda t: run("pe", t))
            block.scalar(lambda s: run("act", s))
            block.vector(lambda v: run("dve", v))
            block.gpsimd(lambda g: run("pool", g))
            block.sync(lambda sy: run("sp", sy))


class Arena:
    def __init__(self, nc, nbytes):
        self.ap = nc.alloc_sbuf_tensor("arena", [128, nbytes], U8).ap()
        self.nbytes = nbytes
        self.off = 0
        self.peak = 0

    def alloc(self, free, dt):
        esz = 4 if dt == F32 else 2
        n = int(np.prod(free)) * esz
        v = self.ap[:, self.off:self.off + n].bitcast(dt)
        self.off += (n + 31) // 32 * 32
        self.peak = max(self.peak, self.off)
        assert self.off <= self.nbytes, ("SBUF arena overflow", self.off)
        if len(free) == 2:
            v = v.rearrange("p (a b) -> p a b", a=free[0])
        elif len(free) == 3:
            v = v.rearrange("p (a b c) -> p a b c", a=free[0], b=free[1])
        return v


class WRing:
    def __init__(self, T, slots, reqs, cache=None, scratch=None):
        self.T = T
        self.slots = slots
        self.reqs = reqs
        self.plan = []
        self.cur = 0
        self.issued = 0
        self.cidx = 0
        self.cache = cache if cache is not None else {}
        self.scratch = scratch

    def block_start(self):
        self.cidx = 0

    def _views(self, base, parts):
        off = 0
        views = []
        for _, free in parts:
            n = int(np.prod(free))
            v = base[:, off:off + n]
            if len(free) == 2:
                v = v.rearrange("p (a b) -> p a b", a=free[0])
            views.append(v)
            off += n
        assert off <= WSLOT, off
        return views, off

    def emit_conversions(self, lo=0, hi=10 ** 9):
        for ci in sorted(self.cache):
            if not (lo <= ci < hi):
                continue
            parts = self.cache[ci]
            vs, _ = self._views(self.scratch[ci], parts)
            for (src, _), v in zip(parts, vs):
                self.T.dma("pool", "cv", _dma(v, src), writes=["wsc"])

    def next(self, parts, cached=False):
        i = self.cur
        self.cur += 1
        ci = None
        if cached:
            ci = self.cidx
            self.cidx += 1
            if self.reqs is None and ci not in self.cache:
                self.cache[ci] = parts
        self.plan.append((parts, ci))
        if self.reqs is not None:
            while self.issued < min(len(self.reqs), i + NRING):
                j = self.issued
                rparts, rci = self.reqs[j]
                slot = self.slots[j % NRING]
                wn = "w%d" % (j % NRING)
                if rci is None:
                    vs, _ = self._views(slot, rparts)
                    for (src, _), v in zip(rparts, vs):
                        self.T.dma("pool", wn, _dma(v, src), writes=[wn])
                else:
                    _, tot = self._views(slot, rparts)
                    self.T.dma("sp", "v%d" % (j % NRING), _dma(slot[:, 0:tot], self.scratch[rci][:, 0:tot]),
                               reads=["wsc"], writes=[wn])
                self.issued += 1
        return self._views(self.slots[i % NRING], parts)[0], "w%d" % (i % NRING)


def _nfree(ap):
    n = 1
    for d in ap.shape[1:]:
        n *= int(d)
    return n


def _c(f, cost):
    f.cost = cost
    return f


def _dma(out, in_):
    f = lambda e: e.dma_start(out=out, in_=in_)
    f.cost = 1.0
    f.lat = 2.5 + _nfree(out) * 128 * (4 if in_.dtype == F32 else 2) / 200e3
    return f


def _mm(out, lhsT, rhs, start, stop):
    n = _nfree(rhs)
    mult = 4.0 if rhs.dtype == F32 else 1.0
    return _c(lambda e: e.matmul(out, lhsT, rhs, start=start, stop=stop), mult * max(n / 2400.0 + 0.003, 0.096))


def _tr(out, in_, ident):
    return _c(lambda e: e.transpose(out=out, in_=in_, identity=ident), 0.1)


def _act(out, in_, func, bias=None, scale=None, accum_out=None):
    kw = {}
    if bias is not None:
        kw["bias"] = bias
    if scale is not None:
        kw["scale"] = scale
    if accum_out is not None:
        kw["accum_out"] = accum_out
    f = _c(lambda e: e.activation(out=out, in_=in_, func=func, **kw), 0.12 + _nfree(out) / 1000.0)
    f.table = {AF.Silu: "S", AF.Sigmoid: "G", AF.Ln: "L", AF.Exp: "E"}.get(func)
    return f


def _tt(out, in0, in1, op):
    fast = out.dtype == BF16 and in0.dtype == BF16 and in1.dtype == BF16
    return _c(lambda e: e.tensor_tensor(out=out, in0=in0, in1=in1, op=op),
              (0.07 + _nfree(out) / 1950.0) if fast else (0.1 + _nfree(out) / 850.0))


def _ts(out, in0, s1, s2, op0, op1=None):
    cost = 0.1 + _nfree(out) / 850.0
    if op1 is None:
        return _c(lambda e: e.tensor_scalar(out=out, in0=in0, scalar1=s1, scalar2=None, op0=op0), cost)
    return _c(lambda e: e.tensor_scalar(out=out, in0=in0, scalar1=s1, scalar2=s2, op0=op0, op1=op1), cost)


def _stt(out, in0, scalar, in1, op0, op1):
    return _c(lambda e: e.scalar_tensor_tensor(out=out, in0=in0, scalar=scalar, in1=in1, op0=op0, op1=op1),
              0.1 + _nfree(out) / 850.0)


def _copy(out, in_):
    return _c(lambda e: e.tensor_copy(out=out, in_=in_), 0.1 + _nfree(out) / 850.0)


def _acopy(out, in_):
    return _c(lambda e: e.copy(out=out, in_=in_), 0.12 + _nfree(out) / 1000.0)


def _memset(ap, val):
    return _c(lambda e: e.memset(ap, val), 0.1 + _nfree(ap) / 1700.0)


def _recip(out, in_):
    return _c(lambda e: e.reciprocal(out=out, in_=in_), 0.1 + _nfree(out) / 850.0)


def _interleave(A, B):
    out = []
    na, nb = len(A), len(B)
    ia = ib = 0
    while ia < na or ib < nb:
        if ib >= nb or (ia < na and ia * nb <= ib * na):
            out.append(A[ia])
            ia += 1
        else:
            out.append(B[ib])
            ib += 1
    return out


def build_program(debug=None):
    nc = bass.Bass("TRN2", target_bir_lowering=False)

    def din(name, shape):
        return nc.dram_tensor(name, list(shape), F32, kind="ExternalInput").ap()

    xT_d = din("xT", [NSEQ, D_MODEL, SEQ])
    x_d = din("x", [NSEQ, SEQ, D_MODEL])
    pT_d = din("pT", [NSEQ, 256, SEQ])
    w_in_d = din("w_in", [D_MODEL, IN_COLS])
    w_br_d = din("w_branch", [2816, D_MODEL])
    w_out_d = din("w_out", [D_MODEL, D_MODEL])
    w_ple_d = din("w_ple", [256, D_MODEL])
    biasT_d = din("biasT", [36, 128, 256])
    maskT_d = din("maskT", [128, 256])
    cmat_d = din("cmat", [4, 128, 128])
    cw_d = din("cw", [128, 24, 4])
    cb_d = din("cb", [128, 24])
    bg_d = din("bg", [128, 2, 8])
    b2_d = din("b2", [D_MODEL])
    dtb_d = din("dt_bias", [32])
    alog_d = din("a_log", [32])
    dsk_d = din("d_skip", [32])
    nw_d = din("ssm_norm_w", [2048])
    lng_d = din("ln_g", [D_MODEL])
    lnb_d = din("ln_b", [D_MODEL])
    out_d = nc.dram_tensor("out", [NSEQ, SEQ, D_MODEL], F32, kind="ExternalOutput").ap()
    dbg_d = None
    if debug == "oatt":
        dbg_d = nc.dram_tensor("dbg", [NSEQ, 768, SEQ], F32, kind="ExternalOutput").ap()
    elif debug == "yssm":
        dbg_d = nc.dram_tensor("dbg", [NSEQ, 2048, SEQ], F32, kind="ExternalOutput").ap()

    w_in_v = w_in_d.rearrange("(c p) n -> p c n", p=128)
    w_br_v = w_br_d.rearrange("(i p) d -> p i d", p=128)
    w_out_v = w_out_d.rearrange("(c p) n -> p c n", p=128)
    w_ple_v = w_ple_d.rearrange("(c p) n -> p c n", p=128)

    NCACHE = 24
    wscr = nc.dram_tensor("wscratch", [NCACHE, 128, WSLOT], BF16, kind="Internal").ap()
    ar = Arena(nc, 206 * 1024)
    ps = [nc.alloc_psum_tensor("ps%d" % i, [128, 512], F32).ap() for i in range(8)]
    psb = [p.bitcast(BF16) for p in ps]

    ident = ar.alloc([128], BF16)
    tri = ar.alloc([128], BF16)
    Umat = ar.alloc([128], BF16)
    tri32 = ar.alloc([128], F32)
    ones32 = ar.alloc([128], F32)
    dtb_bc = ar.alloc([32], F32)
    A_bc = ar.alloc([32], F32)
    D_bc = ar.alloc([32], F32)
    cw = ar.alloc([24, 4], F32)
    cb = ar.alloc([24], F32)
    bg = ar.alloc([2, 8], F32)
    epsln = ar.alloc([1], F32)
    NEGM = ar.alloc([4, 128], BF16)
    epsr = ar.alloc([1], F32)
    oattT = ar.alloc([6, SEQ], BF16)
    wslots = [ar.alloc([WSLOT], BF16) for _ in range(NRING)]
    base = ar.off

    xT = ar.alloc([8, SEQ], BF16)
    EBT = ar.alloc([36, 256], BF16)
    after_ebt = ar.off
    qT = [ar.alloc([SEQ], BF16) for _ in range(2)]
    kT = [ar.alloc([SEQ], BF16) for _ in range(2)]
    Vb = [ar.alloc([16, 4, 64], BF16) for _ in range(2)]
    Vn = [ar.alloc([16, 4, 64], BF16) for _ in range(3)]
    acc_off = ar.off
    acc = [ar.alloc([SEQ], F32) for _ in range(2)]
    gS = [ar.alloc([SEQ], BF16) for _ in range(2)]
    Eb = [ar.alloc([512], BF16) for _ in range(2)]
    PT = [ar.alloc([512], BF16) for _ in range(4)]
    rden = ar.alloc([SEQ], F32)
    attn_end = ar.off

    ar.off = base
    XT = ar.alloc([24, 512], BF16)
    xTb2 = [ar.alloc([8, 512], BF16) for _ in range(2)]
    pTb2 = [ar.alloc([2, 512], BF16) for _ in range(2)]
    nw_bc = ar.alloc([2048], F32)
    b2_bc = ar.alloc([1024], F32)
    lng_bc = ar.alloc([1024], F32)
    lnb_bc = ar.alloc([1024], F32)
    Sst = ar.alloc([4, 512], F32)
    Sbf = ar.alloc([4, 512], BF16)
    hist = ar.alloc([24, 3], F32)
    dtc = ar.alloc([4, 32], F32)
    ac = ar.alloc([4, 32], F32)
    csb = ar.alloc([4, 32], F32)
    dstart = ar.alloc([4, 32], F32)
    dend = ar.alloc([4, 32], F32)
    cdec = ar.alloc([4, 32], F32)
    sm_t = ar.alloc([4, 32], F32)
    sm_e = ar.alloc([4, 32], F32)
    xres = ar.alloc([4, 1024], F32)
    zsall = ar.alloc([16, 512], BF16)
    lnsq = ar.alloc([1024], BF16)
    lnst = [ar.alloc([4], F32) for _ in range(2)]
    sub = ar.off
    uraw = [ar.alloc([515], F32) for _ in range(2)]
    ctmp = [ar.alloc([512], F32) for _ in range(2)]
    ar.off = sub
    xD = [ar.alloc([512], BF16) for _ in range(2)]
    xdt = [ar.alloc([512], BF16) for _ in range(2)]
    xdtd = [ar.alloc([512], BF16) for _ in range(2)]
    Btm = [ar.alloc([128], BF16) for _ in range(2)]
    Gm = [ar.alloc([128], BF16) for _ in range(2)]
    rhsa = [ar.alloc([8, 128], BF16) for _ in range(2)]
    Es = [ar.alloc([512], BF16) for _ in range(2)]
    MT = [ar.alloc([8, 128], BF16) for _ in range(2)]
    t1 = [ar.alloc([512], F32) for _ in range(2)]
    ug = [ar.alloc([512], F32) for _ in range(2)]
    usq = ar.alloc([512], F32)
    yb = [ar.alloc([512], BF16) for _ in range(2)]
    ssq = [ar.alloc([4], F32) for _ in range(2)]
    ssd_end = ar.off
    ar.off = sub
    mergedT = ar.alloc([8, 512], BF16)
    sa = [ar.alloc([512], F32) for _ in range(2)]
    m1 = [ar.alloc([512], F32) for _ in range(2)]
    sgt = [ar.alloc([512], F32) for _ in range(2)]
    tpt = [ar.alloc([512], F32) for _ in range(2)]
    tail_end = ar.off
    ar.off = acc_off
    braw = ar.alloc([36, 256], F32)
    mraw = ar.alloc([256], F32)

    def construct(T, W):
        finals = []

        cm = cmat_d.rearrange("k p f -> p k f")
        T.dma("pool", "c0_0", _dma(ident, cm[:, 0, :]), writes=["ident"])
        T.dma("pool", "c0_1", _dma(tri, cm[:, 1, :]), writes=["tri"])
        T.dma("pool", "c0_2", _dma(Umat, cm[:, 2, :]), writes=["U"])
        T.dma("sp", "c1_3", _dma(tri32, cm[:, 1, :]), writes=["tri32"])
        T.dma("sp", "c1_4", _dma(ones32, cm[:, 3, :]), writes=["ones32"])
        T.dma("sp", "c1_5", _dma(dtb_bc, dtb_d.partition_broadcast(128)), writes=["dtb"])
        T.dma("sp", "c1_6", _dma(A_bc, alog_d.partition_broadcast(128)), writes=["A"])
        T.dma("sp", "c1_7", _dma(D_bc, dsk_d.partition_broadcast(128)), writes=["D"])
        T.dma("sp", "c1_8", _dma(cw, cw_d), writes=["cw"])
        T.dma("sp", "c1_9", _dma(cb, cb_d), writes=["cb"])
        T.dma("sp", "c1_10", _dma(bg, bg_d), writes=["bg"])
        T.op("dve", _memset(epsln, LN_EPS), writes=["epsln"])
        T.op("dve", _memset(epsr, RMS_EPS), writes=["epsr"])
        T.op("dve", _ts(NEGM, Umat.unsqueeze(1).broadcast_to([128, 4, 128]), -30000.0, None, ALU.mult),
             reads=["U"], writes=["NEGM"])
        T.op("act", _act(A_bc, A_bc, AF.Exp), reads=["A"], writes=["A"])
        T.op("dve", _ts(A_bc, A_bc, -1.0, None, ALU.mult), reads=["A"], writes=["A"])
        T.barrier()

        def attention(s):
            XTA = ["xT0", "xT1", "xT2", "xT3"]
            for tb in range(4):
                T.dma("pool", "xT%d" % tb, _dma(xT[:, :, tb * 512:(tb + 1) * 512],
                                                xT_d[s].rearrange("(c p) t -> p c t", p=128)[:, :, tb * 512:(tb + 1) * 512]),
                      writes=[XTA[tb]])
            OVL = ["acc0", "acc1", "gS0", "gS1", "E0", "E1", "PT0", "PT1", "PT2", "PT3", "rdenA", "rdenB"]
            for h6 in range(6):
                T.dma("sp", "c2", _dma(braw[:, h6 * 6:(h6 + 1) * 6, :], biasT_d[h6 * 6:(h6 + 1) * 6].rearrange("h k q -> k h q")),
                      writes=["braw"] + OVL)
            T.dma("sp", "c7", _dma(mraw, maskT_d), writes=["mraw"] + OVL)
            for h6 in range(6):
                hsl = slice(h6 * 6, (h6 + 1) * 6)
                T.op("act", _act(braw[:, hsl, :], braw[:, hsl, :], AF.Exp), reads=["braw"], writes=["braw"])
                T.op("dve", _tt(EBT[:, hsl, :], braw[:, hsl, :], mraw.unsqueeze(1).broadcast_to([128, 6, 256]), ALU.mult),
                     reads=["braw", "mraw"] + OVL, writes=["EBT"])
            for bi in range(2):
                T.op("dve", _memset(Vb[bi][:, :, 1:3, :], 1.0), writes=["V%d" % bi])
            for gi in range(3):
                T.op("dve", _memset(Vn[gi][:, :, 1:3, :], 1.0), writes=["Vn%d" % gi])
            GROUPS = [int(c) for c in os.environ.get("MK_GROUPS", "012")]
            units = [(hp, g) for hp in range(6) for g in GROUPS]
            rot = {"ip": 0, "s": 0, "o": 0, "e": 0, "pt": 0}

            def inproj_steps(u, bi):
                hp, g = u
                D = DILS[g]
                nb = 16 // D
                steps = []
                st = {}

                def s_load():
                    parts = [(w_in_v[:, :, g * 768 + hp * 128 + off: g * 768 + hp * 128 + off + 128], [8, 128])
                             for off in (0, K0)]
                    if hp % 2 == 0:
                        parts.append((w_in_v[:, :, V0 + g * 768 + hp * 128: V0 + g * 768 + hp * 128 + 256], [8, 256]))
                    else:
                        parts.append((w_in_v[:, :, V0 + g * 768 + hp * 128: V0 + g * 768 + hp * 128 + 2], [8, 2]))
                    if g == GROUPS[0]:
                        parts.append((w_in_v[:, :, GATT0 + hp * 128: GATT0 + hp * 128 + 128], [8, 128]))
                    st["w"], st["wr"] = W.next(parts)
                steps.append(s_load)

                def qk_step(which, tb):
                    def f():
                        wv = st["w"][which]
                        b = 4 + rot["ip"] % 4
                        rot["ip"] += 1
                        pr = "ps%d" % b
                        for c in range(8):
                            T.op("pe", _mm(ps[b], wv[:, c, :], xT[:, c, tb * 512:(tb + 1) * 512], c == 0, c == 7),
                                 reads=[st["wr"], XTA[tb]], writes=[pr])
                        dst = (qT if which == 0 else kT)[bi]
                        dv = dst.rearrange("p (r m) -> p r m", r=D)[:, :, tb * (512 // D):(tb + 1) * (512 // D)]
                        sv = ps[b].rearrange("p (m r) -> p r m", r=D)
                        name = ("q%d" if which == 0 else "k%d") % bi
                        if which == 0:
                            T.op("act", lambda e, dv=dv, sv=sv: e.mul(out=dv, in_=sv, mul=0.125), reads=[pr], writes=[name])
                        else:
                            T.op("act", _acopy(dv, sv), reads=[pr], writes=[name])
                    return f
                for which in (0, 1):
                    for tb in range(4):
                        steps.append(qk_step(which, tb))

                def v_step(kb2):
                    def f():
                        wv = st["w"][2]
                        b = 4 + rot["ip"] % 4
                        rot["ip"] += 1
                        pr = "ps%d" % b
                        for kk in range(2):
                            kbp = kb2 * 2 + kk
                            r, n = kbp // nb, kbp % nb
                            t0 = r + D * 128 * n
                            for c in range(8):
                                T.op("pe", _mm(ps[b][:, kk * 256:(kk + 1) * 256], xT[:, c, t0:t0 + D * 127 + 1:D],
                                               wv[:, c, :], c == 0, c == 7),
                                     reads=[st["wr"]] + XTA[t0 // 512:(t0 + D * 127) // 512 + 1], writes=[pr])
                        sv = ps[b].rearrange("p (k q h d) -> p k q h d", k=2, q=2, h=2)
                        T.op("dve", _copy(Vb[bi][:, kb2 * 2:(kb2 + 1) * 2, 0:4:3, :], sv[:, :, 0, :, :]), reads=[pr], writes=["V%d" % bi])
                        T.op("dve", _copy(Vn[g][:, kb2 * 2:(kb2 + 1) * 2, 0:4:3, :], sv[:, :, 1, :, :]), reads=[pr], writes=["Vn%d" % g])
                    return f
                if hp % 2 == 0:
                    for kb2 in range(8):
                        steps.append(v_step(kb2))

                if g == GROUPS[0]:
                    def g_step(tb):
                        def f():
                            wv = st["w"][3]
                            b = 4 + rot["ip"] % 4
                            rot["ip"] += 1
                            pr = "ps%d" % b
                            for c in range(8):
                                T.op("pe", _mm(ps[b], wv[:, c, :], xT[:, c, tb * 512:(tb + 1) * 512], c == 0, c == 7),
                                     reads=[st["wr"], XTA[tb]], writes=[pr])
                            T.op("act", _act(gS[hp % 2][:, tb * 512:(tb + 1) * 512], ps[b], AF.Silu),
                                 reads=[pr], writes=["gS%d" % (hp % 2)])
                        return f
                    for tb in range(4):
                        steps.append(g_step(tb))
                return steps

            def attend_steps(u, bi):
                hp, g = u
                D = DILS[g]
                nb = 16 // D
                m256 = nb > 1
                nbank = 8 if m256 else 4
                items = [(hd, i) for hd in range(2) for i in range(nbank)]
                info = {}
                steps = []

                def S_rec(k):
                    hd, i = items[k]
                    rows = slice(hd * 64, hd * 64 + 64)
                    b = rot["s"] % 2
                    rot["s"] += 1
                    info[k] = {"sb": b}
                    pr = "ps%d" % b
                    if m256:
                        for kk in range(2):
                            kb = 2 * i + kk
                            N = 256 if (kb % nb) != nb - 1 else 128
                            T.op("pe", _mm(ps[b][:, kk * 256:kk * 256 + N], kT[bi][rows, kb * 128:(kb + 1) * 128],
                                           qT[bi][rows, kb * 128:kb * 128 + N], True, True),
                                 reads=["q%d" % bi, "k%d" % bi], writes=[pr])
                    else:
                        for kk in range(4):
                            kb = 4 * i + kk
                            T.op("pe", _mm(ps[b][:, kk * 128:(kk + 1) * 128], kT[bi][rows, kb * 128:(kb + 1) * 128],
                                           qT[bi][rows, kb * 128:(kb + 1) * 128], True, True),
                                 reads=["q%d" % bi, "k%d" % bi], writes=[pr])

                def rest_rec(k):
                    hd, i = items[k]
                    hh = g * 12 + 2 * hp + hd
                    b = info[k]["sb"]
                    eb = rot["e"] % 2
                    rot["e"] += 1
                    pb = rot["pt"] % 4
                    rot["pt"] += 1
                    info[k]["pt"] = pb
                    T.op("act", _act(Eb[eb], ps[b], AF.Exp), reads=["ps%d" % b], writes=["E%d" % eb])
                    nseg, w = (2, 256) if m256 else (4, 128)
                    T.op("dve", _tt(PT[pb].rearrange("p (s w) -> p s w", s=nseg),
                                    Eb[eb].rearrange("p (s w) -> p s w", s=nseg),
                                    EBT[:, hh, 0:w].unsqueeze(1).broadcast_to([128, nseg, w]), ALU.mult),
                         reads=["E%d" % eb, "EBT"], writes=["PT%d" % pb])
                    vsl = slice(hd * 2, hd * 2 + 2)

                    Vbuf, Vname = (Vb[bi], "V%d" % bi) if hp % 2 == 0 else (Vn[g], "Vn%d" % g)

                    def lhs_v(kb):
                        return Vbuf[:, kb, vsl, :].rearrange("p a d -> p (a d)")
                    qbs = [2 * i, 2 * i + 1] if m256 else [4 * i + j for j in range(4)]
                    for qb in qbs:
                        if qb % 4 == 0:
                            ob = 2 + rot["o"] % 2
                            rot["o"] += 1
                            info[("ob", hd)] = ob
                        ob = info[("ob", hd)]
                        orr = "ps%d" % ob
                        oreg = ps[ob][:, (qb % 4) * 128:(qb % 4 + 1) * 128]
                        if m256:
                            has_prev = (qb % nb) != 0
                            if has_prev:
                                if qb == 2 * i + 1:
                                    T.op("pe", _mm(oreg, lhs_v(qb - 1), PT[pb][:, 128:256], True, False),
                                         reads=["PT%d" % pb, Vname], writes=[orr])
                                else:
                                    ppb = info[k - 1]["pt"]
                                    T.op("pe", _mm(oreg, lhs_v(qb - 1), PT[ppb][:, 384:512], True, False),
                                         reads=["PT%d" % ppb, Vname], writes=[orr])
                            c0 = (qb - 2 * i) * 256
                            T.op("pe", _mm(oreg, lhs_v(qb), PT[pb][:, c0:c0 + 128], not has_prev, True),
                                 reads=["PT%d" % pb, Vname], writes=[orr])
                        else:
                            c0 = (qb - 4 * i) * 128
                            T.op("pe", _mm(oreg, lhs_v(qb), PT[pb][:, c0:c0 + 128], True, True),
                                 reads=["PT%d" % pb, Vname], writes=[orr])
                        if qb % 4 == 3:
                            j = qb // 4
                            an = "acc%d" % hd
                            if g == 0:
                                av, sv = acc[hd][:, j * 512:(j + 1) * 512], ps[ob]
                            elif g == 1:
                                av, sv = acc[hd][:, j:SEQ:4], ps[ob]
                            else:
                                av = acc[hd].rearrange("p (m r) -> p r m", r=16)[:, 4 * j:4 * j + 4, :]
                                sv = ps[ob].rearrange("p (r m) -> p r m", r=4)
                            if g == GROUPS[0]:
                                T.op("dve", _copy(av, sv), reads=[orr], writes=[an])
                            else:
                                T.op("dve", _tt(av, sv, av, ALU.add), reads=[orr, an], writes=[an])

                steps.append(lambda: S_rec(0))
                for k in range(len(items)):
                    def f(k=k):
                        if k + 1 < len(items):
                            S_rec(k + 1)
                        rest_rec(k)
                    steps.append(f)
                if g == GROUPS[-1]:
                    def fin():
                        gb = "gS%d" % (hp % 2)
                        gs = gS[hp % 2]
                        T.op("act", _act(rden[0:64, :], acc[0][64:128, :], AF.Ln), reads=["acc0"], writes=["rdenA"])
                        T.op("act", _act(rden[0:64, :], rden[0:64, :], AF.Exp, scale=-1.0), reads=["rdenA"], writes=["rdenA"])
                        T.op("dve", _tt(acc[0][0:64, :], acc[0][0:64, :], rden[0:64, :], ALU.mult),
                             reads=["acc0", "rdenA"], writes=["acc0"])
                        T.op("dve", _tt(oattT[0:64, hp, :], acc[0][0:64, :], gs[0:64, :], ALU.mult),
                             reads=["acc0", gb], writes=["oattT"])
                        T.op("act", _act(rden[64:128, :], acc[1][0:64, :], AF.Ln), reads=["acc1"], writes=["rdenB"])
                        T.op("act", _act(rden[64:128, :], rden[64:128, :], AF.Exp, scale=-1.0), reads=["rdenB"], writes=["rdenB"])
                        T.op("dve", _tt(acc[1][64:128, :], acc[1][64:128, :], rden[64:128, :], ALU.mult),
                             reads=["acc1", "rdenB"], writes=["acc1"])
                        T.op("dve", _tt(oattT[64:128, hp, :], acc[1][64:128, :], gs[64:128, :], ALU.mult),
                             reads=["acc1", gb], writes=["oattT"])
                    steps.append(fin)
                return steps

            for f in inproj_steps(units[0], 0):
                f()
            for i, u in enumerate(units):
                A = attend_steps(u, i % 2)
                B = inproj_steps(units[i + 1], (i + 1) % 2) if i + 1 < len(units) else []
                for f in _interleave(A, B):
                    f()
                if s == 0:
                    W.emit_conversions(2 * i, 2 * i + 2 if i + 1 < len(units) else 10 ** 9)
            if debug == "oatt":
                for hp in range(6):
                    finals.append(T.dma("pool", "dbg", _dma(dbg_d[s, hp * 128:(hp + 1) * 128, :], oattT[:, hp, :]),
                                        reads=["oattT"]))

        def stream(s):
            T.dma("sp", "c3", _dma(nw_bc, nw_d.partition_broadcast(128)), writes=["nw"])
            T.dma("sp", "c4", _dma(b2_bc, b2_d.partition_broadcast(128)), writes=["b2"])
            T.dma("sp", "c5", _dma(lng_bc, lng_d.partition_broadcast(128)), writes=["lng"])
            T.dma("sp", "c6", _dma(lnb_bc, lnb_d.partition_broadcast(128)), writes=["lnb"])
            T.op("dve", _memset(Sst, 0.0), writes=["S0", "S1", "S2", "S3"])
            T.op("dve", _memset(Sbf, 0.0), writes=["Sb0", "Sb1", "Sb2", "Sb3"])
            T.op("dve", _memset(hist, 0.0), writes=["hist%d" % q for q in range(24)])
            rot = {"u": 0, "c": 0, "k": 0}
            def load_xp(blk):
                tsl_ = slice(blk * 512, (blk + 1) * 512)
                q2 = blk % 2
                T.dma("pool", "xb%d" % q2, _dma(xTb2[q2], xT_d[s].rearrange("(c p) t -> p c t", p=128)[:, :, tsl_]),
                      writes=["xTb%d" % q2])
                T.dma("pool", "pb%d" % q2, _dma(pTb2[q2], pT_d[s].rearrange("(c p) t -> p c t", p=128)[:, :, tsl_]),
                      writes=["pTb%d" % q2])

            load_xp(0)
            for blk in range(4):
                tsl = slice(blk * 512, (blk + 1) * 512)
                xTb, pTb = xTb2[blk % 2], pTb2[blk % 2]
                XB, PB = "xTb%d" % (blk % 2), "pTb%d" % (blk % 2)
                if blk + 1 < 4:
                    load_xp(blk + 1)
                W.block_start()
                T.dma("sp", "xr", _dma(xres, x_d[s, tsl, :].rearrange("(t p) d -> p t d", p=128)),
                      writes=["xres", "xres0", "xres1", "xres2", "xres3"])

                for cg in range(6):
                    (wv,), wr = W.next([(w_in_v[:, :, XBC0 + cg * 512: XBC0 + (cg + 1) * 512], [8, 512])], cached=True)
                    for j in range(4):
                        cc = cg * 4 + j
                        b = rot["u"] % 2
                        rot["u"] += 1
                        pr = "ps%d" % b
                        for c in range(8):
                            T.op("pe", _mm(ps[b], wv[:, c, j * 128:(j + 1) * 128], xTb[:, c, :], c == 0, c == 7),
                                 reads=[wr, XB], writes=[pr])
                        ur, ct = uraw[b], ctmp[b]
                        T.op("act", _acopy(ur[:, 0:3], hist[:, cc, :]), reads=["hist%d" % cc], writes=["urh%d" % b])
                        T.op("act", _acopy(ur[:, 3:515], ps[b]), reads=[pr], writes=["ur%d" % b])
                        T.op("act", _acopy(hist[:, cc, :], ur[:, 512:515]), reads=["ur%d" % b], writes=["hist%d" % cc])
                        T.op("act", _act(ct, ps[b], AF.Identity, bias=cb[:, cc:cc + 1], scale=cw[:, cc, 3:4]),
                             reads=[pr, "cw", "cb"], writes=["ct%d" % b])
                        for k in (2, 1, 0):
                            T.op("dve", _stt(ct, ur[:, k:k + 512], cw[:, cc, k:k + 1], ct, ALU.mult, ALU.add),
                                 reads=["ur%d" % b, "urh%d" % b, "cw", "ct%d" % b], writes=["ct%d" % b])
                        xw = ["XT%d_%d" % (cc // 4, q4) for q4 in range(4)] if cc < 16 else ["XTbc"]
                        T.op("act", _act(XT[:, cc, :], ct, AF.Silu), reads=["ct%d" % b], writes=xw)

                T.barrier()
                (wdt,), wr = W.next([(w_in_v[:, :, DT0:DT0 + 32], [8, 32])], cached=True)
                for c4 in range(4):
                    csl = slice(c4 * 128, (c4 + 1) * 128)
                    for c in range(8):
                        T.op("pe", _mm(ps[2][:, c4 * 32:(c4 + 1) * 32], xTb[:, c, csl], wdt[:, c, :], c == 0, c == 7),
                             reads=[wr, XB], writes=["ps2"])
                T.op("dve", _tt(sm_t, ps[2][:, 0:128].rearrange("p (a h) -> p a h", a=4),
                                dtb_bc.unsqueeze(1).broadcast_to([128, 4, 32]), ALU.add),
                     reads=["ps2", "dtb"], writes=["sm_t"])
                T.op("act", _act(sm_e, sm_t, AF.Exp), reads=["sm_t"], writes=["sm_e"])
                T.op("act", _act(dtc, sm_e, AF.Ln, bias=1.0), reads=["sm_e"], writes=["dtc"])
                T.op("dve", _tt(ac, dtc, A_bc.unsqueeze(1).broadcast_to([128, 4, 32]), ALU.mult),
                     reads=["dtc", "A"], writes=["ac"])
                for c4 in range(4):
                    T.op("pe", _mm(ps[3][:, c4 * 32:(c4 + 1) * 32], tri32, ac[:, c4, :], True, True),
                         reads=["tri32", "ac"], writes=["ps3"])
                    T.op("pe", _mm(ps[3][:, 128 + c4 * 32:128 + (c4 + 1) * 32], ones32, ac[:, c4, :], True, True),
                         reads=["ones32", "ac"], writes=["ps3"])
                cs_ps = ps[3][:, 0:128].rearrange("p (a h) -> p a h", a=4)
                tot_ps = ps[3][:, 128:256].rearrange("p (a h) -> p a h", a=4)
                T.op("act", _acopy(csb, cs_ps), reads=["ps3"], writes=["csb"])
                T.op("act", _act(dstart, cs_ps, AF.Exp), reads=["ps3"], writes=["dstart"])
                T.op("act", _act(cdec, tot_ps, AF.Exp), reads=["ps3"], writes=["cdec"])
                T.op("dve", _tt(sm_t, tot_ps, csb, ALU.subtract), reads=["ps3", "csb"], writes=["sm_t"])
                T.op("act", _act(dend, sm_t, AF.Exp), reads=["sm_t"], writes=["dend"])

                its = [(g, c4) for g in range(4) for c4 in range(4)]
                wzs = {}

                def ssd_front(i):
                    g, c4 = its[i]
                    k = i % 2
                    kk = str(k)
                    hs = slice(8 * g, 8 * g + 8)
                    csl = slice(c4 * 128, (c4 + 1) * 128)
                    xn = "XT%d_%d" % (g, c4)
                    if c4 == 0:
                        wzs[g] = W.next([(w_in_v[:, :, Z0 + g * 512: Z0 + (g + 1) * 512], [8, 512])], cached=True)
                    (wz,), wr = wzs[g]
                    if c4 == 0:
                        for c4b in range(4):
                            for c in range(8):
                                T.op("pe", _mm(ps[0], xTb[:, c, c4b * 128:(c4b + 1) * 128], wz[:, c, :], c == 0, c == 7),
                                     reads=[wr, XB], writes=["ps0"])
                            T.op("act", _act(zsall[:, i + c4b, :], ps[0], AF.Silu), reads=["ps0"], writes=["zs%d" % (i + c4b)])
                    NA = RHSA_ACT
                    for h in range(NA):
                        T.op("act", _act(rhsa[k][:, h, :], tri, AF.Copy, scale=ac[:, c4, 8 * g + h:8 * g + h + 1]),
                             reads=["tri", "ac"], writes=["rhsa%s_%d" % (kk, h // 4)])
                    if NA < 8:
                        T.op("dve", _tt(rhsa[k][:, NA:8, :], tri.unsqueeze(1).broadcast_to([128, 8 - NA, 128]),
                                        ac[:, c4, 8 * g + NA:8 * g + 8].unsqueeze(2).broadcast_to([128, 8 - NA, 128]), ALU.mult),
                             reads=["tri", "ac"], writes=["rhsa%s_1" % kk] + (["rhsa%s_0" % kk] if NA < 4 else []))
                    for j in range(4):
                        T.op("pe", _tr(psb[2][:, j * 128:(j + 1) * 128], XT[:, 4 * g + j, csl], ident),
                             reads=[xn, "ident"], writes=["ps2"])
                    T.op("pe", _tr(psb[2][:, 512:640], XT[:, 16 + g, csl], ident), reads=["XTbc", "ident"], writes=["ps2"])
                    xtm = psb[2][:, 0:512].rearrange("p (h d) -> p h d", h=8)
                    T.op("dve", _tt(xdt[k].rearrange("p (h d) -> p h d", h=8), xtm,
                                    dtc[:, c4, hs].unsqueeze(2).broadcast_to([128, 8, 64]), ALU.mult),
                         reads=["ps2", "dtc"], writes=["xdt" + kk])
                    T.op("dve", _tt(xD[k].rearrange("p (h d) -> p h d", h=8), xtm,
                                    D_bc[:, hs].unsqueeze(2).broadcast_to([128, 8, 64]), ALU.mult),
                         reads=["ps2", "D"], writes=["xD" + kk])
                    T.op("act", _acopy(Btm[k], psb[2][:, 512:640]), reads=["ps2"], writes=["Btm" + kk])
                    T.op(PENG, _tt(xdtd[k].rearrange("p (h d) -> p h d", h=8),
                                     xdt[k].rearrange("p (h d) -> p h d", h=8),
                                     dend[:, c4, hs].unsqueeze(2).broadcast_to([128, 8, 64]), ALU.mult),
                         reads=["xdt" + kk, "dend"], writes=["xdtd" + kk])
                    T.op("pe", _mm(ps[3][:, 256:384], XT[:, 16 + g, csl], XT[:, 20 + g, csl], True, True),
                         reads=["XTbc"], writes=["ps3g"])
                    T.op("act", _acopy(Gm[k], ps[3][:, 256:384]), reads=["ps3g"], writes=["Gm" + kk])
                    for q in range(2):
                        T.op("pe", _mm(ps[4 + q], Umat, rhsa[k][:, 4 * q:4 * q + 4, :].rearrange("p h l -> p (h l)"),
                                       True, False), reads=["U", "rhsa%s_%d" % (kk, q)], writes=["ps%d" % (4 + q)])
                        T.op("pe", _mm(ps[4 + q], ident, NEGM.rearrange("p h l -> p (h l)"), False, True),
                             reads=["ident", "NEGM"], writes=["ps%d" % (4 + q)])
                        T.op("act", _act(Es[q], ps[4 + q], AF.Exp), reads=["ps%d" % (4 + q)], writes=["Es%d" % q])
                        T.op("dve", _tt(MT[k][:, 4 * q:4 * q + 4, :], Es[q].rearrange("p (h l) -> p h l", h=4),
                                        Gm[k].unsqueeze(1).broadcast_to([128, 4, 128]), ALU.mult),
                             reads=["Es%d" % q, "Gm" + kk], writes=["MT" + kk])

                def ssd_back(i):
                    g, c4 = its[i]
                    k = i % 2
                    kk = str(k)
                    hs = slice(8 * g, 8 * g + 8)
                    csl = slice(c4 * 128, (c4 + 1) * 128)
                    xn = "XT%d_%d" % (g, c4)
                    T.op("pe", _c(lambda e, k=k: e.matmul(ps[6], ident, xD[k], start=True, stop=False, skip_group_check=True), 0.216),
                         reads=["ident", "xD" + kk], writes=["ps6"])
                    for h in range(8):
                        T.op("pe", _c(lambda e, k=k, h=h: e.matmul(ps[6][:, h * 64:(h + 1) * 64], MT[k][:, h, :],
                                                                   xdt[k][:, h * 64:(h + 1) * 64], start=False, stop=True,
                                                                   skip_group_check=True), 0.096),
                             reads=["MT" + kk, "xdt" + kk], writes=["ps6"])
                    T.op("pe", _mm(ps[7], XT[:, 20 + g, csl], Sbf[:, g, :], True, True),
                         reads=["XTbc", "Sb%d" % g], writes=["ps7"])
                    T.op("pe", _mm(ps[1], Btm[k], xdtd[k], True, True), reads=["Btm" + kk, "xdtd" + kk], writes=["ps1"])
                    T.op("dve", _tt(t1[k].rearrange("p (h d) -> p h d", h=8), ps[7].rearrange("p (h d) -> p h d", h=8),
                                    dstart[:, c4, hs].unsqueeze(2).broadcast_to([128, 8, 64]), ALU.mult),
                         reads=["ps7", "dstart"], writes=["t1" + kk])
                    T.op("dve", _tt(t1[k], ps[6], t1[k], ALU.add), reads=["ps6", "t1" + kk], writes=["t1" + kk])
                    T.op("dve", _tt(ug[k], t1[k], zsall[:, i, :], ALU.mult), reads=["t1" + kk, "zs%d" % i], writes=["ug" + kk])
                    T.op("act", _act(usq, ug[k], AF.Square, accum_out=ssq[k][:, 0:1]), reads=["ug" + kk],
                         writes=["usq", "ssq" + kk])
                    T.op("act", _act(ssq[k][:, 1:2], ssq[k][:, 0:1], AF.Ln, bias=epsr[:, 0:1], scale=1.0 / 512.0),
                         reads=["ssq" + kk, "epsr"], writes=["ssqb" + kk])
                    T.op("act", _act(ssq[k][:, 3:4], ssq[k][:, 1:2], AF.Exp, scale=-0.5), reads=["ssqb" + kk], writes=["ssqd" + kk])
                    T.op("dve", _stt(yb[k], ug[k], ssq[k][:, 3:4], nw_bc[:, g * 512:(g + 1) * 512], ALU.mult, ALU.mult),
                         reads=["ug" + kk, "ssqd" + kk, "nw"], writes=["yb" + kk])
                    Sg = Sst[:, g, :]
                    T.op(PENG, _tt(Sg.rearrange("p (h d) -> p h d", h=8), Sg.rearrange("p (h d) -> p h d", h=8),
                                     cdec[:, c4, hs].unsqueeze(2).broadcast_to([128, 8, 64]), ALU.mult),
                         reads=["S%d" % g, "cdec"], writes=["S%d" % g])
                    T.op("dve", _tt(Sg, ps[1], Sg, ALU.add), reads=["ps1", "S%d" % g], writes=["S%d" % g])
                    T.op("act", _acopy(Sbf[:, g, :], Sg), reads=["S%d" % g], writes=["Sb%d" % g])
                    for j in range(4):
                        T.op("pe", _tr(psb[3][:, j * 128:(j + 1) * 128], yb[k][:, j * 128:(j + 1) * 128], ident),
                             reads=["yb" + kk, "ident"], writes=["ps3t"])
                    T.op("act", _acopy(XT[:, 4 * g:4 * g + 4, csl], psb[3][:, 0:512].rearrange("p (j t) -> p j t", j=4)),
                         reads=["ps3t"], writes=[xn])

                ssd_front(0)
                for i in range(16):
                    if i + 1 < 16:
                        ssd_front(i + 1)
                    ssd_back(i)

                if debug == "yssm":
                    for j in range(16):
                        finals.append(T.dma("pool", "dbg", _dma(dbg_d[s, j * 128:(j + 1) * 128, tsl], XT[:, j, :]),
                                            reads=["XT%d_%d" % (j // 4, q4) for q4 in range(4)]))
                T.barrier()

                for j in range(8):
                    dsl = slice(j * 128, (j + 1) * 128)
                    (wb0, wb1, wga, wgb), wr = W.next([(w_br_v[:, 0:11, dsl], [11, 128]), (w_br_v[:, 11:22, dsl], [11, 128]),
                                                 (w_in_v[:, :, GM0 + j * 128: GM0 + (j + 1) * 128], [8, 128]),
                                                 (w_in_v[:, :, GM0 + 1024 + j * 128: GM0 + 1024 + (j + 1) * 128], [8, 128])], cached=True)
                    o = 4 * (j % 2)
                    pn = ["ps%d" % (o + q) for q in range(4)]
                    for i in range(6):
                        T.op("pe", _mm(ps[o], wb0[:, i, :], oattT[:, i, tsl], i == 0, i == 5), reads=[wr, "oattT"], writes=[pn[0]])
                    for i in range(16):
                        T.op("pe", _mm(ps[o + 1], (wb0[:, 6 + i, :] if i < 5 else wb1[:, i - 5, :]), XT[:, i, :], i == 0, i == 15),
                             reads=[wr] + ["XT%d_%d" % (i // 4, q4) for q4 in range(4)], writes=[pn[1]])
                    for c in range(8):
                        T.op("pe", _mm(ps[o + 2], wga[:, c, :], xTb[:, c, :], c == 0, c == 7), reads=[wr, XB], writes=[pn[2]])
                    for c in range(8):
                        T.op("pe", _mm(ps[o + 3], wgb[:, c, :], xTb[:, c, :], c == 0, c == 7), reads=[wr, XB], writes=[pn[3]])
                    k = j % 2
                    kk = str(k)
                    T.op("act", _act(sa[k], ps[o + 2], AF.Sigmoid, bias=bg[:, 0, j:j + 1]), reads=[pn[2], "bg"], writes=["sa" + kk])
                    T.op("dve", _tt(m1[k], ps[o], sa[k], ALU.mult), reads=[pn[0], "sa" + kk], writes=["m1" + kk])
                    T.op("act", _act(sgt[k], ps[o + 3], AF.Sigmoid, bias=bg[:, 1, j:j + 1]), reads=[pn[3], "bg"], writes=["sgt" + kk])
                    T.op("dve", _tt(tpt[k], ps[o + 1], sgt[k], ALU.mult), reads=[pn[1], "sgt" + kk], writes=["tpt" + kk])
                    T.op(PENG, _tt(mergedT[:, j, :], m1[k], tpt[k], ALU.add), reads=["m1" + kk, "tpt" + kk], writes=["mergedT"])
                for half in range(2):
                    hsl = slice(half * 512, (half + 1) * 512)
                    (wo,), wr = W.next([(w_out_v[:, :, hsl], [8, 512])], cached=True)
                    for tt in range(4):
                        tts = slice(tt * 128, (tt + 1) * 128)
                        b = tt % 2
                        for j in range(8):
                            T.op("pe", _mm(ps[b], mergedT[:, j, tts], wo[:, j, :], j == 0, j == 7),
                                 reads=[wr, "mergedT"], writes=["ps%d" % b])
                        T.op("dve", _stt(xres[:, tt, hsl], xres[:, tt, hsl], ALPHA, ps[b], ALU.mult, ALU.add),
                             reads=["ps%d" % b, "xres"], writes=["xres"])
                    (wgp, wpl), wr = W.next([(w_in_v[:, :, GPLE0 + half * 512: GPLE0 + (half + 1) * 512], [8, 512]),
                                             (w_ple_v[:, :, hsl], [2, 512])], cached=True)
                    for tt in range(4):
                        tts = slice(tt * 128, (tt + 1) * 128)
                        k = tt % 2
                        kk = str(k)
                        for c in range(8):
                            T.op("pe", _mm(ps[2 + k], xTb[:, c, tts], wgp[:, c, :], c == 0, c == 7),
                                 reads=[wr, XB], writes=["ps%d" % (2 + k)])
                        for c in range(2):
                            T.op("pe", _mm(ps[4 + k], pTb[:, c, tts], wpl[:, c, :], c == 0, c == 1),
                                 reads=[wr, PB], writes=["ps%d" % (4 + k)])
                        T.op("dve", _tt(sa[k], ps[2 + k], b2_bc[:, hsl], ALU.add), reads=["ps%d" % (2 + k), "b2"], writes=["sa" + kk])
                        T.op("act", _act(sgt[k], sa[k], AF.Sigmoid), reads=["sa" + kk], writes=["sgt" + kk])
                        T.op("dve", _tt(tpt[k], ps[4 + k], sgt[k], ALU.mult), reads=["ps%d" % (4 + k), "sgt" + kk], writes=["tpt" + kk])
                        T.op("dve", _tt(xres[:, tt, hsl], xres[:, tt, hsl], tpt[k], ALU.add), reads=["xres", "tpt" + kk], writes=["xres"])
                T.barrier()
                for tt in range(4):
                    k = tt % 2
                    kk = str(k)
                    r = xres[:, tt, :]
                    rn = "xres%d" % tt
                    st_ = lnst[k]
                    T.op("dve", lambda e, st_=st_, r=r: e.reduce_sum(out=st_[:, 0:1], in_=r, axis=AX.X), reads=["xres", rn], writes=["lnst" + kk])
                    T.op("dve", _ts(st_[:, 1:2], st_[:, 0:1], 1.0 / 1024.0, None, ALU.mult), reads=["lnst" + kk], writes=["lnstb" + kk])
                    T.op("dve", _ts(r, r, st_[:, 1:2], None, ALU.subtract), reads=["xres", rn, "lnstb" + kk], writes=[rn])
                    T.op("act", _act(lnsq, r, AF.Square, accum_out=st_[:, 2:3]), reads=[rn], writes=["lnsq", "lnstc" + kk])
                    T.op("act", _act(st_[:, 3:4], st_[:, 2:3], AF.Ln, bias=epsln[:, 0:1], scale=1.0 / 1024.0),
                         reads=["lnstc" + kk, "epsln"], writes=["lnstd" + kk])
                    T.op("act", _act(st_[:, 0:1], st_[:, 3:4], AF.Exp, scale=-0.5), reads=["lnstd" + kk], writes=["lnst" + kk])
                    T.op("dve", _stt(r, r, st_[:, 0:1], lng_bc, ALU.mult, ALU.mult), reads=[rn, "lnst" + kk, "lng"], writes=[rn])
                    T.op("dve", _tt(r, r, lnb_bc, ALU.add), reads=[rn, "lnb"], writes=[rn])
                    t0 = blk * 512 + tt * 128
                    finals.append(T.dma("sp", "st%d" % k, _dma(out_d[s, t0:t0 + 128, :], r), reads=[rn]))

        for s in range(NSEQ):
            attention(s)
            T.barrier()
            if debug != "oatt":
                stream(s)
                T.barrier()
        return finals

    T0 = Tracker(nc)
    W0 = WRing(T0, wslots, None, scratch=wscr)
    construct(T0, W0)
    assert len(W0.cache) <= NCACHE, len(W0.cache)
    T = Tracker(nc)
    W = WRing(T, wslots, W0.plan, cache=W0.cache, scratch=wscr)
    finals = construct(T, W)
    assert W.cur == len(W0.plan) and W.issued == len(W0.plan)
    if os.environ.get("MK_NOSCHED") is None:
        T.schedule()
    T.emit(finals)
    return nc


def _t5_bucket_np(dist):
    max_exact = 16
    d_f = np.maximum(dist, 1).astype(np.float32)
    large = max_exact + (np.log(d_f / np.float32(max_exact)) / np.float32(math.log(2048 / max_exact))
                         * np.float32(32 - max_exact)).astype(np.int32)
    large = np.minimum(large, 31)
    return np.where(dist < max_exact, dist, large)


def _host_consts():
    ki = np.arange(128)[:, None]
    qi = np.arange(128)[None, :]
    d_cur = qi - ki
    d_nxt = qi + 128 - ki
    delta = np.concatenate([d_cur, d_nxt], axis=1)
    valid = (delta >= 0) & (delta <= 128)
    maskT = valid.astype(np.float32)
    idx = np.stack([_t5_bucket_np(np.maximum(delta, 0) * d) for d in DILS])
    p = np.arange(128)[:, None]
    f = np.arange(128)[None, :]
    cmat = np.stack([(p == f), (p <= f), (p > f), np.ones((128, 128), bool)]).astype(np.float32)
    return maskT, idx, cmat


_NC_CACHE = {}


def kernel(x, p, w_in, b_gate, conv_w, conv_b, dt_bias, a_log, d_skip, ssm_norm_w,
           w_branch, w_out, w_ple, ln_g, ln_b, rel_bias):
    debug = os.environ.get("MK_DEBUG") or None
    f32 = np.float32
    x = np.asarray(x, f32)
    p = np.asarray(p, f32)[0]
    maskT, idx, cmat = _host_consts()
    rel_bias = np.asarray(rel_bias, f32)
    biasT = np.stack([rel_bias[idx[hh // 12], hh] for hh in range(36)]).astype(f32)
    cw = np.ascontiguousarray(np.asarray(conv_w, f32)[0].T.reshape(24, 128, 4).transpose(1, 0, 2))
    cb = np.ascontiguousarray(np.asarray(conv_b, f32)[0].reshape(24, 128).T)
    bgate = np.asarray(b_gate, f32)[0]
    bg = np.ascontiguousarray(bgate[0:2].reshape(2, 8, 128).transpose(2, 0, 1))
    shared = {
        "w_in": np.ascontiguousarray(np.asarray(w_in, f32)[0]),
        "w_branch": np.ascontiguousarray(np.asarray(w_branch, f32)[0]),
        "w_out": np.ascontiguousarray(np.asarray(w_out, f32)[0]),
        "w_ple": np.ascontiguousarray(np.asarray(w_ple, f32)[0]),
        "biasT": biasT, "maskT": maskT, "cmat": cmat, "cw": cw, "cb": cb, "bg": bg,
        "b2": np.ascontiguousarray(bgate[2]),
        "dt_bias": np.ascontiguousarray(np.asarray(dt_bias, f32)[0]),
        "a_log": np.ascontiguousarray(np.asarray(a_log, f32)[0]),
        "d_skip": np.ascontiguousarray(np.asarray(d_skip, f32)[0]),
        "ssm_norm_w": np.ascontiguousarray(np.asarray(ssm_norm_w, f32)[0]),
        "ln_g": np.ascontiguousarray(np.asarray(ln_g, f32)[0]),
        "ln_b": np.ascontiguousarray(np.asarray(ln_b, f32)[0]),
    }
    in_maps = []
    for c in range(N_CORES):
        xs = x[c * NSEQ:(c + 1) * NSEQ]
        m = dict(shared)
        m["x"] = np.ascontiguousarray(xs)
        m["xT"] = np.ascontiguousarray(xs.transpose(0, 2, 1))
        m["pT"] = np.ascontiguousarray(p[c * NSEQ:(c + 1) * NSEQ].transpose(0, 2, 1))
        in_maps.append(m)
    if debug not in _NC_CACHE:
        _NC_CACHE[debug] = build_program(debug)
    nc = _NC_CACHE[debug]
    ncores = int(os.environ.get("MK_CORES", N_CORES))
    res = run_bass_kernel_spmd(nc, in_maps[:ncores], core_ids=list(range(ncores)))
    if debug:
        return [r["dbg"] for r in res.results]
    return np.concatenate([r["out"] for r in res.results], axis=0).astype(f32)
```

```python
import math
import os
import contextlib
import numpy as np
import concourse.bass as bass
import concourse.mybir as mybir
from concourse.bass_utils import run_bass_kernel_spmd

F32 = mybir.dt.float32
BF16 = mybir.dt.bfloat16
U8 = mybir.dt.uint8
AF = mybir.ActivationFunctionType
ALU = mybir.AluOpType
AX = mybir.AxisListType

N_CORES = 8
D_MODEL = 1024
SEQ = 2048
NSEQ = 2
K0, V0, GATT0, Z0, XBC0, DT0, GM0, GPLE0, IN_COLS = 2304, 4608, 6912, 7680, 9728, 12800, 12832, 14880, 15904
DILS = (1, 4, 16)
ALPHA = 2.0 ** 0.25
LN_EPS = 1e-5
RMS_EPS = 1e-5
WSLOT = 5120
NRING = 3
RHSA_ACT = int(os.environ.get("MK_RHSA_ACT", "0"))
PENG = "dve"


class Node:
    __slots__ = ("eng", "idx", "fn", "deps", "signal", "sigval", "dma", "cost", "lat", "gidx", "fin", "res", "odeps", "epoch", "table")

    def __init__(self, eng, idx, fn, dma=None):
        self.eng = eng
        self.idx = idx
        self.fn = fn
        self.cost = getattr(fn, "cost", 0.5)
        self.lat = getattr(fn, "lat", 0.0)
        self.table = getattr(fn, "table", None)
        self.gidx = 0
        self.fin = None
        self.res = ()
        self.odeps = []
        self.deps = []
        self.signal = False
        self.sigval = None
        self.dma = dma


class Tracker:
    ENGS = ("pe", "act", "dve", "pool", "sp")

    def __init__(self, nc):
        self.nc = nc
        self.ops = {e: [] for e in self.ENGS}
        self.lastw = {}
        self.readers = {}
        self.dma_cnt = {}
        self.dma_latest = {}
        self.bank_last = {}
        self.pending = {e: [] for e in self.ENGS}

    def _add(self, node, reads, writes):
        self.gcount = getattr(self, "gcount", 0) + 1
        node.gidx = self.gcount
        node.epoch = getattr(self, "epoch", 0)
        node.res = (tuple(reads), tuple(writes))
        deps = {}

        def add_dep(n):
            if n is not None and n is not node:
                deps[id(n)] = n

        for r in reads:
            add_dep(self.lastw.get(r))
        for w in writes:
            add_dep(self.lastw.get(w))
            for n in self.readers.get(w, {}).values():
                add_dep(n)
        banks = {int(r[2]) for r in list(reads) + list(writes) if r.startswith("ps") and r[2].isdigit()}
        for b in banks:
            bl = self.bank_last.setdefault(b, {})
            for e, n in bl.items():
                if e != node.eng:
                    add_dep(n)
            bl[node.eng] = node
        for n in self.pending[node.eng]:
            add_dep(n)
        self.pending[node.eng] = []
        node.deps = list(deps.values())
        key = ("dma", node.dma[0]) if node.dma else node.eng
        for r in reads:
            self.readers.setdefault(r, {})[key] = node
        for w in writes:
            self.lastw[w] = node
            self.readers[w] = {}

    def op(self, eng, fn, reads=(), writes=()):
        node = Node(eng, len(self.ops[eng]), fn)
        self.ops[eng].append(node)
        self._add(node, reads, writes)
        return node

    def dma(self, eng, slot, fn, reads=(), writes=()):
        self.dma_cnt[slot] = self.dma_cnt.get(slot, 0) + 16
        node = Node(eng, len(self.ops[eng]), fn, dma=(slot, self.dma_cnt[slot]))
        self.ops[eng].append(node)
        self._add(node, reads, writes)
        self.dma_latest[slot] = node
        return node

    def barrier(self):
        self.epoch = getattr(self, "epoch", 0) + 1
        last = [self.ops[e][-1] for e in self.ENGS if self.ops[e]]
        last += [n for sl, n in self.dma_latest.items() if sl != "cv"]
        for e in self.ENGS:
            self.pending[e] = list(last)

    def schedule(self, window=48, vis=0.15):
        allnodes = sorted((n for e in self.ENGS for n in self.ops[e]), key=lambda n: n.gidx)
        wcount = {}
        for n in allnodes:
            for w in n.res[1]:
                wcount[w] = wcount.get(w, 0) + 1
        last_acc = {}
        for n in allnodes:
            keys = set()
            for r in n.res[0] + n.res[1]:
                if wcount.get(r, 0) >= 2:
                    keys.add(r)
                if r.startswith("ps") and r[2].isdigit():
                    keys.add(("bank", int(r[2])))
            for k in keys:
                p = last_acc.get((k, n.eng))
                if p is not None:
                    n.odeps.append(p)
                last_acc[(k, n.eng)] = n
        rem = {e: list(self.ops[e]) for e in self.ENGS}
        out = {e: [] for e in self.ENGS}
        free = {e: 0.0 for e in self.ENGS}
        cur_epoch = {e: -1 for e in self.ENGS}
        cur_tab = [None]
        nleft = sum(len(v) for v in rem.values())
        while nleft:
            best = None
            for e in self.ENGS:
                lst = rem[e]
                if not lst:
                    continue
                cand = None
                wnd = lst[:window] if lst[0].epoch == cur_epoch[e] else lst[:1]
                for n in wnd:
                    if n.epoch != lst[0].epoch:
                        break
                    rdy = 0.0
                    ok = True
                    for d in n.odeps:
                        if d.fin is None:
                            ok = False
                            break
                    if not ok:
                        continue
                    for d in n.deps:
                        if d.fin is None:
                            ok = False
                            break
                        if d.eng == "pe" and e == "pe" and d.dma is None and n.dma is None:
                            continue
                        if d.fin + vis > rdy:
                            rdy = d.fin + vis
                    if not ok:
                        continue
                    st = max(rdy, free[e])
                    pen = 0.0
                    if e == "act" and n.table is not None and not (n.table == cur_tab[0] or (n.table == "E" and cur_tab[0] == "L")):
                        pen = 1.3
                    if cand is None or st + pen < cand[0] - 1e-9:
                        cand = (st + pen, n)
                    if st + pen <= free[e] + 1e-9:
                        break
                if cand is not None and (best is None or cand[0] < best[0] - 1e-9 or
                                         (abs(cand[0] - best[0]) <= 1e-9 and cand[1].gidx < best[1].gidx)):
                    best = cand
            st, n = best
            e = n.eng
            if e == "act" and n.table is not None and not (n.table == cur_tab[0] or (n.table == "E" and cur_tab[0] == "L")):
                cur_tab[0] = n.table
            rem[e].remove(n)
            out[e].append(n)
            cur_epoch[e] = n.epoch
            free[e] = st + n.cost
            n.fin = st + n.cost + n.lat
            nleft -= 1
        self.ops = out
        self.est_us = max(free.values())

    def emit(self, final_nodes):
        nc = self.nc
        for e in self.ENGS:
            for n in self.ops[e]:
                for d in n.deps:
                    if d.dma is None:
                        if d.eng == "pe" and n.eng == "pe" and n.dma is None:
                            continue
                        d.signal = True
        for n in final_nodes:
            if n.dma is None:
                n.signal = True
        for e in self.ENGS:
            c = 0
            for n in self.ops[e]:
                if n.dma is None and n.signal:
                    c += 1
                    n.sigval = c
        with contextlib.ExitStack() as st:
            esem = {e: st.enter_context(nc.semaphore("s_" + e)) for e in self.ENGS}
            dsem = {s: st.enter_context(nc.semaphore("d_" + s)) for s in self.dma_cnt}
            block = st.enter_context(nc.Block())

            def run(ename, eng):
                waited = {}
                for n in self.ops[ename]:
                    need = {}
                    for d in n.deps:
                        if d.dma is not None:
                            k, v = ("d", d.dma[0]), d.dma[1]
                        else:
                            if d.eng == "pe" and ename == "pe" and n.dma is None:
                                continue
                            k, v = ("e", d.eng), d.sigval
                        if v > need.get(k, 0):
                            need[k] = v
                    for k, v in need.items():
                        if waited.get(k, 0) >= v:
                            continue
                        waited[k] = v
                        eng.wait_ge(dsem[k[1]] if k[0] == "d" else esem[k[1]], v)
                    ins = n.fn(eng)
                    if n.dma is not None:
                        ins.then_inc(dsem[n.dma[0]], 16)
                    elif n.signal:
                        ins.then_inc(esem[ename], 1)
                if ename == "sp":
                    fin = {}
                    for n in final_nodes:
                        k, v = (("d", n.dma[0]), n.dma[1]) if n.dma is not None else (("e", n.eng), n.sigval)
                        fin[k] = max(fin.get(k, 0), v)
                    for k, v in fin.items():
                        eng.wait_ge(dsem[k[1]] if k[0] == "d" else esem[k[1]], v)

            block.tensor(lambda t: run("pe", t))
            block.scalar(lambda s: run("act", s))
            block.vector(lambda v: run("dve", v))
            block.gpsimd(lambda g: run("pool", g))
            block.sync(lambda sy: run("sp", sy))


class Arena:
    def __init__(self, nc, nbytes):
        self.ap = nc.alloc_sbuf_tensor("arena", [128, nbytes], U8).ap()
        self.nbytes = nbytes
        self.off = 0
        self.peak = 0

    def alloc(self, free, dt):
        esz = 4 if dt == F32 else 2
        n = int(np.prod(free)) * esz
        v = self.ap[:, self.off:self.off + n].bitcast(dt)
        self.off += (n + 31) // 32 * 32
        self.peak = max(self.peak, self.off)
        assert self.off <= self.nbytes, ("SBUF arena overflow", self.off)
        if len(free) == 2:
            v = v.rearrange("p (a b) -> p a b", a=free[0])
        elif len(free) == 3:
            v = v.rearrange("p (a b c) -> p a b c", a=free[0], b=free[1])
        return v


class WRing:
    def __init__(self, T, slots, reqs, cache=None, scratch=None):
        self.T = T
        self.slots = slots
        self.reqs = reqs
        self.plan = []
        self.cur = 0
        self.issued = 0
        self.cidx = 0
        self.cache = cache if cache is not None else {}
        self.scratch = scratch

    def block_start(self):
        self.cidx = 0

    def _views(self, base, parts):
        off = 0
        views = []
        for _, free in parts:
            n = int(np.prod(free))
            v = base[:, off:off + n]
            if len(free) == 2:
                v = v.rearrange("p (a b) -> p a b", a=free[0])
            views.append(v)
            off += n
        assert off <= WSLOT, off
        return views, off

    def emit_conversions(self, lo=0, hi=10 ** 9):
        for ci in sorted(self.cache):
            if not (lo <= ci < hi):
                continue
            parts = self.cache[ci]
            vs, _ = self._views(self.scratch[ci], parts)
            for (src, _), v in zip(parts, vs):
                self.T.dma("pool", "cv", _dma(v, src), writes=["wsc"])

    def next(self, parts, cached=False):
        i = self.cur
        self.cur += 1
        ci = None
        if cached:
            ci = self.cidx
            self.cidx += 1
            if self.reqs is None and ci not in self.cache:
                self.cache[ci] = parts
        self.plan.append((parts, ci))
        if self.reqs is not None:
            while self.issued < min(len(self.reqs), i + NRING):
                j = self.issued
                rparts, rci = self.reqs[j]
                slot = self.slots[j % NRING]
                wn = "w%d" % (j % NRING)
                if rci is None:
                    vs, _ = self._views(slot, rparts)
                    for (src, _), v in zip(rparts, vs):
                        self.T.dma("pool", wn, _dma(v, src), writes=[wn])
                else:
                    _, tot = self._views(slot, rparts)
                    self.T.dma("sp", "v%d" % (j % NRING), _dma(slot[:, 0:tot], self.scratch[rci][:, 0:tot]),
                               reads=["wsc"], writes=[wn])
                self.issued += 1
        return self._views(self.slots[i % NRING], parts)[0], "w%d" % (i % NRING)


def _nfree(ap):
    n = 1
    for d in ap.shape[1:]:
        n *= int(d)
    return n


def _c(f, cost):
    f.cost = cost
    return f


def _dma(out, in_):
    f = lambda e: e.dma_start(out=out, in_=in_)
    f.cost = 1.0
    f.lat = 2.5 + _nfree(out) * 128 * (4 if in_.dtype == F32 else 2) / 200e3
    return f


def _mm(out, lhsT, rhs, start, stop):
    n = _nfree(rhs)
    mult = 4.0 if rhs.dtype == F32 else 1.0
    return _c(lambda e: e.matmul(out, lhsT, rhs, start=start, stop=stop), mult * max(n / 2400.0 + 0.003, 0.096))


def _tr(out, in_, ident):
    return _c(lambda e: e.transpose(out=out, in_=in_, identity=ident), 0.1)


def _act(out, in_, func, bias=None, scale=None, accum_out=None):
    kw = {}
    if bias is not None:
        kw["bias"] = bias
    if scale is not None:
        kw["scale"] = scale
    if accum_out is not None:
        kw["accum_out"] = accum_out
    f = _c(lambda e: e.activation(out=out, in_=in_, func=func, **kw), 0.12 + _nfree(out) / 1000.0)
    f.table = {AF.Silu: "S", AF.Sigmoid: "G", AF.Ln: "L", AF.Exp: "E"}.get(func)
    return f


def _tt(out, in0, in1, op):
    fast = out.dtype == BF16 and in0.dtype == BF16 and in1.dtype == BF16
    return _c(lambda e: e.tensor_tensor(out=out, in0=in0, in1=in1, op=op),
              (0.07 + _nfree(out) / 1950.0) if fast else (0.1 + _nfree(out) / 850.0))


def _ts(out, in0, s1, s2, op0, op1=None):
    cost = 0.1 + _nfree(out) / 850.0
    if op1 is None:
        return _c(lambda e: e.tensor_scalar(out=out, in0=in0, scalar1=s1, scalar2=None, op0=op0), cost)
    return _c(lambda e: e.tensor_scalar(out=out, in0=in0, scalar1=s1, scalar2=s2, op0=op0, op1=op1), cost)


def _stt(out, in0, scalar, in1, op0, op1):
    return _c(lambda e: e.scalar_tensor_tensor(out=out, in0=in0, scalar=scalar, in1=in1, op0=op0, op1=op1),
              0.1 + _nfree(out) / 850.0)


def _copy(out, in_):
    return _c(lambda e: e.tensor_copy(out=out, in_=in_), 0.1 + _nfree(out) / 850.0)


def _acopy(out, in_):
    return _c(lambda e: e.copy(out=out, in_=in_), 0.12 + _nfree(out) / 1000.0)


def _memset(ap, val):
    return _c(lambda e: e.memset(ap, val), 0.1 + _nfree(ap) / 1700.0)


def _recip(out, in_):
    return _c(lambda e: e.reciprocal(out=out, in_=in_), 0.1 + _nfree(out) / 850.0)


def _interleave(A, B):
    out = []
    na, nb = len(A), len(B)
    ia = ib = 0
    while ia < na or ib < nb:
        if ib >= nb or (ia < na and ia * nb <= ib * na):
            out.append(A[ia])
            ia += 1
        else:
            out.append(B[ib])
            ib += 1
    return out


def build_program(debug=None):
    nc = bass.Bass("TRN2", target_bir_lowering=False)

    def din(name, shape):
        return nc.dram_tensor(name, list(shape), F32, kind="ExternalInput").ap()

    xT_d = din("xT", [NSEQ, D_MODEL, SEQ])
    x_d = din("x", [NSEQ, SEQ, D_MODEL])
    pT_d = din("pT", [NSEQ, 256, SEQ])
    w_in_d = din("w_in", [D_MODEL, IN_COLS])
    w_br_d = din("w_branch", [2816, D_MODEL])
    w_out_d = din("w_out", [D_MODEL, D_MODEL])
    w_ple_d = din("w_ple", [256, D_MODEL])
    biasT_d = din("biasT", [36, 128, 256])
    maskT_d = din("maskT", [128, 256])
    cmat_d = din("cmat", [4, 128, 128])
    cw_d = din("cw", [128, 24, 4])
    cb_d = din("cb", [128, 24])
    bg_d = din("bg", [128, 2, 8])
    b2_d = din("b2", [D_MODEL])
    dtb_d = din("dt_bias", [32])
    alog_d = din("a_log", [32])
    dsk_d = din("d_skip", [32])
    nw_d = din("ssm_norm_w", [2048])
    lng_d = din("ln_g", [D_MODEL])
    lnb_d = din("ln_b", [D_MODEL])
    out_d = nc.dram_tensor("out", [NSEQ, SEQ, D_MODEL], F32, kind="ExternalOutput").ap()
    dbg_d = None
    if debug == "oatt":
        dbg_d = nc.dram_tensor("dbg", [NSEQ, 768, SEQ], F32, kind="ExternalOutput").ap()
    elif debug == "yssm":
        dbg_d = nc.dram_tensor("dbg", [NSEQ, 2048, SEQ], F32, kind="ExternalOutput").ap()

    w_in_v = w_in_d.rearrange("(c p) n -> p c n", p=128)
    w_br_v = w_br_d.rearrange("(i p) d -> p i d", p=128)
    w_out_v = w_out_d.rearrange("(c p) n -> p c n", p=128)
    w_ple_v = w_ple_d.rearrange("(c p) n -> p c n", p=128)

    NCACHE = 24
    wscr = nc.dram_tensor("wscratch", [NCACHE, 128, WSLOT], BF16, kind="Internal").ap()
    ar = Arena(nc, 206 * 1024)
    ps = [nc.alloc_psum_tensor("ps%d" % i, [128, 512], F32).ap() for i in range(8)]
    psb = [p.bitcast(BF16) for p in ps]

    ident = ar.alloc([128], BF16)
    tri = ar.alloc([128], BF16)
    Umat = ar.alloc([128], BF16)
    tri32 = ar.alloc([128], F32)
    ones32 = ar.alloc([128], F32)
    dtb_bc = ar.alloc([32], F32)
    A_bc = ar.alloc([32], F32)
    D_bc = ar.alloc([32], F32)
    cw = ar.alloc([24, 4], F32)
    cb = ar.alloc([24], F32)
    bg = ar.alloc([2, 8], F32)
    epsln = ar.alloc([1], F32)
    NEGM = ar.alloc([4, 128], BF16)
    epsr = ar.alloc([1], F32)
    oattT = ar.alloc([6, SEQ], BF16)
    wslots = [ar.alloc([WSLOT], BF16) for _ in range(NRING)]
    base = ar.off

    xT = ar.alloc([8, SEQ], BF16)
    EBT = ar.alloc([36, 256], BF16)
    after_ebt = ar.off
    qT = [ar.alloc([SEQ], BF16) for _ in range(2)]
    kT = [ar.alloc([SEQ], BF16) for _ in range(2)]
    Vb = [ar.alloc([16, 4, 64], BF16) for _ in range(2)]
    Vn = [ar.alloc([16, 4, 64], BF16) for _ in range(3)]
    acc_off = ar.off
    acc = [ar.alloc([SEQ], F32) for _ in range(2)]
    gS = [ar.alloc([SEQ], BF16) for _ in range(2)]
    Eb = [ar.alloc([512], BF16) for _ in range(2)]
    PT = [ar.alloc([512], BF16) for _ in range(4)]
    rden = ar.alloc([SEQ], F32)
    attn_end = ar.off

    ar.off = base
    XT = ar.alloc([24, 512], BF16)
    xTb2 = [ar.alloc([8, 512], BF16) for _ in range(2)]
    pTb2 = [ar.alloc([2, 512], BF16) for _ in range(2)]
    nw_bc = ar.alloc([2048], F32)
    b2_bc = ar.alloc([1024], F32)
    lng_bc = ar.alloc([1024], F32)
    lnb_bc = ar.alloc([1024], F32)
    Sst = ar.alloc([4, 512], F32)
    Sbf = ar.alloc([4, 512], BF16)
    hist = ar.alloc([24, 3], F32)
    dtc = ar.alloc([4, 32], F32)
    ac = ar.alloc([4, 32], F32)
    csb = ar.alloc([4, 32], F32)
    dstart = ar.alloc([4, 32], F32)
    dend = ar.alloc([4, 32], F32)
    cdec = ar.alloc([4, 32], F32)
    sm_t = ar.alloc([4, 32], F32)
    sm_e = ar.alloc([4, 32], F32)
    xres = ar.alloc([4, 1024], F32)
    zsall = ar.alloc([16, 512], BF16)
    lnsq = ar.alloc([1024], BF16)
    lnst = [ar.alloc([4], F32) for _ in range(2)]
    sub = ar.off
    uraw = [ar.alloc([515], F32) for _ in range(2)]
    ctmp = [ar.alloc([512], F32) for _ in range(2)]
    ar.off = sub
    xD = [ar.alloc([512], BF16) for _ in range(2)]
    xdt = [ar.alloc([512], BF16) for _ in range(2)]
    xdtd = [ar.alloc([512], BF16) for _ in range(2)]
    Btm = [ar.alloc([128], BF16) for _ in range(2)]
    Gm = [ar.alloc([128], BF16) for _ in range(2)]
    rhsa = [ar.alloc([8, 128], BF16) for _ in range(2)]
    Es = [ar.alloc([512], BF16) for _ in range(2)]
    MT = [ar.alloc([8, 128], BF16) for _ in range(2)]
    t1 = [ar.alloc([512], F32) for _ in range(2)]
    ug = [ar.alloc([512], F32) for _ in range(2)]
    usq = ar.alloc([512], F32)
    yb = [ar.alloc([512], BF16) for _ in range(2)]
    ssq = [ar.alloc([4], F32) for _ in range(2)]
    ssd_end = ar.off
    ar.off = sub
    mergedT = ar.alloc([8, 512], BF16)
    sa = [ar.alloc([512], F32) for _ in range(2)]
    m1 = [ar.alloc([512], F32) for _ in range(2)]
    sgt = [ar.alloc([512], F32) for _ in range(2)]
    tpt = [ar.alloc([512], F32) for _ in range(2)]
    tail_end = ar.off
    ar.off = acc_off
    braw = ar.alloc([36, 256], F32)
    mraw = ar.alloc([256], F32)

    def construct(T, W):
        finals = []

        cm = cmat_d.rearrange("k p f -> p k f")
        T.dma("pool", "c0_0", _dma(ident, cm[:, 0, :]), writes=["ident"])
        T.dma("pool", "c0_1", _dma(tri, cm[:, 1, :]), writes=["tri"])
        T.dma("pool", "c0_2", _dma(Umat, cm[:, 2, :]), writes=["U"])
        T.dma("sp", "c1_3", _dma(tri32, cm[:, 1, :]), writes=["tri32"])
        T.dma("sp", "c1_4", _dma(ones32, cm[:, 3, :]), writes=["ones32"])
        T.dma("sp", "c1_5", _dma(dtb_bc, dtb_d.partition_broadcast(128)), writes=["dtb"])
        T.dma("sp", "c1_6", _dma(A_bc, alog_d.partition_broadcast(128)), writes=["A"])
        T.dma("sp", "c1_7", _dma(D_bc, dsk_d.partition_broadcast(128)), writes=["D"])
        T.dma("sp", "c1_8", _dma(cw, cw_d), writes=["cw"])
        T.dma("sp", "c1_9", _dma(cb, cb_d), writes=["cb"])
        T.dma("sp", "c1_10", _dma(bg, bg_d), writes=["bg"])
        T.op("dve", _memset(epsln, LN_EPS), writes=["epsln"])
        T.op("dve", _memset(epsr, RMS_EPS), writes=["epsr"])
        T.op("dve", _ts(NEGM, Umat.unsqueeze(1).broadcast_to([128, 4, 128]), -30000.0, None, ALU.mult),
             reads=["U"], writes=["NEGM"])
        T.op("act", _act(A_bc, A_bc, AF.Exp), reads=["A"], writes=["A"])
        T.op("dve", _ts(A_bc, A_bc, -1.0, None, ALU.mult), reads=["A"], writes=["A"])
        T.barrier()

        def attention(s):
            XTA = ["xT0", "xT1", "xT2", "xT3"]
            for tb in range(4):
                T.dma("pool", "xT%d" % tb, _dma(xT[:, :, tb * 512:(tb + 1) * 512],
                                                xT_d[s].rearrange("(c p) t -> p c t", p=128)[:, :, tb * 512:(tb + 1) * 512]),
                      writes=[XTA[tb]])
            OVL = ["acc0", "acc1", "gS0", "gS1", "E0", "E1", "PT0", "PT1", "PT2", "PT3", "rdenA", "rdenB"]
            for h6 in range(6):
                T.dma("sp", "c2", _dma(braw[:, h6 * 6:(h6 + 1) * 6, :], biasT_d[h6 * 6:(h6 + 1) * 6].rearrange("h k q -> k h q")),
                      writes=["braw"] + OVL)
            T.dma("sp", "c7", _dma(mraw, maskT_d), writes=["mraw"] + OVL)
            for h6 in range(6):
                hsl = slice(h6 * 6, (h6 + 1) * 6)
                T.op("act", _act(braw[:, hsl, :], braw[:, hsl, :], AF.Exp), reads=["braw"], writes=["braw"])
                T.op("dve", _tt(EBT[:, hsl, :], braw[:, hsl, :], mraw.unsqueeze(1).broadcast_to([128, 6, 256]), ALU.mult),
                     reads=["braw", "mraw"] + OVL, writes=["EBT"])
            for bi in range(2):
                T.op("dve", _memset(Vb[bi][:, :, 1:3, :], 1.0), writes=["V%d" % bi])
            for gi in range(3):
                T.op("dve", _memset(Vn[gi][:, :, 1:3, :], 1.0), writes=["Vn%d" % gi])
            GROUPS = [int(c) for c in os.environ.get("MK_GROUPS", "012")]
            units = [(hp, g) for hp in range(6) for g in GROUPS]
            rot = {"ip": 0, "s": 0, "o": 0, "e": 0, "pt": 0}

            def inproj_steps(u, bi):
                hp, g = u
                D = DILS[g]
                nb = 16 // D
                steps = []
                st = {}

                def s_load():
                    parts = [(w_in_v[:, :, g * 768 + hp * 128 + off: g * 768 + hp * 128 + off + 128], [8, 128])
                             for off in (0, K0)]
                    if hp % 2 == 0:
                        parts.append((w_in_v[:, :, V0 + g * 768 + hp * 128: V0 + g * 768 + hp * 128 + 256], [8, 256]))
                    else:
                        parts.append((w_in_v[:, :, V0 + g * 768 + hp * 128: V0 + g * 768 + hp * 128 + 2], [8, 2]))
                    if g == GROUPS[0]:
                        parts.append((w_in_v[:, :, GATT0 + hp * 128: GATT0 + hp * 128 + 128], [8, 128]))
                    st["w"], st["wr"] = W.next(parts)
                steps.append(s_load)

                def qk_step(which, tb):
                    def f():
                        wv = st["w"][which]
                        b = 4 + rot["ip"] % 4
                        rot["ip"] += 1
                        pr = "ps%d" % b
                        for c in range(8):
                            T.op("pe", _mm(ps[b], wv[:, c, :], xT[:, c, tb * 512:(tb + 1) * 512], c == 0, c == 7),
                                 reads=[st["wr"], XTA[tb]], writes=[pr])
                        dst = (qT if which == 0 else kT)[bi]
                        dv = dst.rearrange("p (r m) -> p r m", r=D)[:, :, tb * (512 // D):(tb + 1) * (512 // D)]
                        sv = ps[b].rearrange("p (m r) -> p r m", r=D)
                        name = ("q%d" if which == 0 else "k%d") % bi
                        if which == 0:
                            T.op("act", lambda e, dv=dv, sv=sv: e.mul(out=dv, in_=sv, mul=0.125), reads=[pr], writes=[name])
                        else:
                            T.op("act", _acopy(dv, sv), reads=[pr], writes=[name])
                    return f
                for which in (0, 1):
                    for tb in range(4):
                        steps.append(qk_step(which, tb))

                def v_step(kb2):
                    def f():
                        wv = st["w"][2]
                        b = 4 + rot["ip"] % 4
                        rot["ip"] += 1
                        pr = "ps%d" % b
                        for kk in range(2):
                            kbp = kb2 * 2 + kk
                            r, n = kbp // nb, kbp % nb
                            t0 = r + D * 128 * n
                            for c in range(8):
                                T.op("pe", _mm(ps[b][:, kk * 256:(kk + 1) * 256], xT[:, c, t0:t0 + D * 127 + 1:D],
                                               wv[:, c, :], c == 0, c == 7),
                                     reads=[st["wr"]] + XTA[t0 // 512:(t0 + D * 127) // 512 + 1], writes=[pr])
                        sv = ps[b].rearrange("p (k q h d) -> p k q h d", k=2, q=2, h=2)
                        T.op("dve", _copy(Vb[bi][:, kb2 * 2:(kb2 + 1) * 2, 0:4:3, :], sv[:, :, 0, :, :]), reads=[pr], writes=["V%d" % bi])
                        T.op("dve", _copy(Vn[g][:, kb2 * 2:(kb2 + 1) * 2, 0:4:3, :], sv[:, :, 1, :, :]), reads=[pr], writes=["Vn%d" % g])
                    return f
                if hp % 2 == 0:
                    for kb2 in range(8):
                        steps.append(v_step(kb2))

                if g == GROUPS[0]:
                    def g_step(tb):
                        def f():
                            wv = st["w"][3]
                            b = 4 + rot["ip"] % 4
                            rot["ip"] += 1
                            pr = "ps%d" % b
                            for c in range(8):
                                T.op("pe", _mm(ps[b], wv[:, c, :], xT[:, c, tb * 512:(tb + 1) * 512], c == 0, c == 7),
                                     reads=[st["wr"], XTA[tb]], writes=[pr])
                            T.op("act", _act(gS[hp % 2][:, tb * 512:(tb + 1) * 512], ps[b], AF.Silu),
                                 reads=[pr], writes=["gS%d" % (hp % 2)])
                        return f
                    for tb in range(4):
                        steps.append(g_step(tb))
                return steps

            def attend_steps(u, bi):
                hp, g = u
                D = DILS[g]
                nb = 16 // D
                m256 = nb > 1
                nbank = 8 if m256 else 4
                items = [(hd, i) for hd in range(2) for i in range(nbank)]
                info = {}
                steps = []

                def S_rec(k):
                    hd, i = items[k]
                    rows = slice(hd * 64, hd * 64 + 64)
                    b = rot["s"] % 2
                    rot["s"] += 1
                    info[k] = {"sb": b}
                    pr = "ps%d" % b
                    if m256:
                        for kk in range(2):
                            kb = 2 * i + kk
                            N = 256 if (kb % nb) != nb - 1 else 128
                            T.op("pe", _mm(ps[b][:, kk * 256:kk * 256 + N], kT[bi][rows, kb * 128:(kb + 1) * 128],
                                           qT[bi][rows, kb * 128:kb * 128 + N], True, True),
                                 reads=["q%d" % bi, "k%d" % bi], writes=[pr])
                    else:
                        for kk in range(4):
                            kb = 4 * i + kk
                            T.op("pe", _mm(ps[b][:, kk * 128:(kk + 1) * 128], kT[bi][rows, kb * 128:(kb + 1) * 128],
                                           qT[bi][rows, kb * 128:(kb + 1) * 128], True, True),
                                 reads=["q%d" % bi, "k%d" % bi], writes=[pr])

                def rest_rec(k):
                    hd, i = items[k]
                    hh = g * 12 + 2 * hp + hd
                    b = info[k]["sb"]
                    eb = rot["e"] % 2
                    rot["e"] += 1
                    pb = rot["pt"] % 4
                    rot["pt"] += 1
                    info[k]["pt"] = pb
                    T.op("act", _act(Eb[eb], ps[b], AF.Exp), reads=["ps%d" % b], writes=["E%d" % eb])
                    nseg, w = (2, 256) if m256 else (4, 128)
                    T.op("dve", _tt(PT[pb].rearrange("p (s w) -> p s w", s=nseg),
                                    Eb[eb].rearrange("p (s w) -> p s w", s=nseg),
                                    EBT[:, hh, 0:w].unsqueeze(1).broadcast_to([128, nseg, w]), ALU.mult),
                         reads=["E%d" % eb, "EBT"], writes=["PT%d" % pb])
                    vsl = slice(hd * 2, hd * 2 + 2)

                    Vbuf, Vname = (Vb[bi], "V%d" % bi) if hp % 2 == 0 else (Vn[g], "Vn%d" % g)

                    def lhs_v(kb):
                        return Vbuf[:, kb, vsl, :].rearrange("p a d -> p (a d)")
                    qbs = [2 * i, 2 * i + 1] if m256 else [4 * i + j for j in range(4)]
                    for qb in qbs:
                        if qb % 4 == 0:
                            ob = 2 + rot["o"] % 2
                            rot["o"] += 1
                            info[("ob", hd)] = ob
                        ob = info[("ob", hd)]
                        orr = "ps%d" % ob
                        oreg = ps[ob][:, (qb % 4) * 128:(qb % 4 + 1) * 128]
                        if m256:
                            has_prev = (qb % nb) != 0
                            if has_prev:
                                if qb == 2 * i + 1:
                                    T.op("pe", _mm(oreg, lhs_v(qb - 1), PT[pb][:, 128:256], True, False),
                                         reads=["PT%d" % pb, Vname], writes=[orr])
                                else:
                                    ppb = info[k - 1]["pt"]
                                    T.op("pe", _mm(oreg, lhs_v(qb - 1), PT[ppb][:, 384:512], True, False),
                                         reads=["PT%d" % ppb, Vname], writes=[orr])
                            c0 = (qb - 2 * i) * 256
                            T.op("pe", _mm(oreg, lhs_v(qb), PT[pb][:, c0:c0 + 128], not has_prev, True),
                                 reads=["PT%d" % pb, Vname], writes=[orr])
                        else:
                            c0 = (qb - 4 * i) * 128
                            T.op("pe", _mm(oreg, lhs_v(qb), PT[pb][:, c0:c0 + 128], True, True),
                                 reads=["PT%d" % pb, Vname], writes=[orr])
                        if qb % 4 == 3:
                            j = qb // 4
                            an = "acc%d" % hd
                            if g == 0:
                                av, sv = acc[hd][:, j * 512:(j + 1) * 512], ps[ob]
                            elif g == 1:
                                av, sv = acc[hd][:, j:SEQ:4], ps[ob]
                            else:
                                av = acc[hd].rearrange("p (m r) -> p r m", r=16)[:, 4 * j:4 * j + 4, :]
                                sv = ps[ob].rearrange("p (r m) -> p r m", r=4)
                            if g == GROUPS[0]:
                                T.op("dve", _copy(av, sv), reads=[orr], writes=[an])
                            else:
                                T.op("dve", _tt(av, sv, av, ALU.add), reads=[orr, an], writes=[an])

                steps.append(lambda: S_rec(0))
                for k in range(len(items)):
                    def f(k=k):
                        if k + 1 < len(items):
                            S_rec(k + 1)
                        rest_rec(k)
                    steps.append(f)
                if g == GROUPS[-1]:
                    def fin():
                        gb = "gS%d" % (hp % 2)
                        gs = gS[hp % 2]
                        T.op("act", _act(rden[0:64, :], acc[0][64:128, :], AF.Ln), reads=["acc0"], writes=["rdenA"])
                        T.op("act", _act(rden[0:64, :], rden[0:64, :], AF.Exp, scale=-1.0), reads=["rdenA"], writes=["rdenA"])
                        T.op("dve", _tt(acc[0][0:64, :], acc[0][0:64, :], rden[0:64, :], ALU.mult),
                             reads=["acc0", "rdenA"], writes=["acc0"])
                        T.op("dve", _tt(oattT[0:64, hp, :], acc[0][0:64, :], gs[0:64, :], ALU.mult),
                             reads=["acc0", gb], writes=["oattT"])
                        T.op("act", _act(rden[64:128, :], acc[1][0:64, :], AF.Ln), reads=["acc1"], writes=["rdenB"])
                        T.op("act", _act(rden[64:128, :], rden[64:128, :], AF.Exp, scale=-1.0), reads=["rdenB"], writes=["rdenB"])
                        T.op("dve", _tt(acc[1][64:128, :], acc[1][64:128, :], rden[64:128, :], ALU.mult),
                             reads=["acc1", "rdenB"], writes=["acc1"])
                        T.op("dve", _tt(oattT[64:128, hp, :], acc[1][64:128, :], gs[64:128, :], ALU.mult),
                             reads=["acc1", gb], writes=["oattT"])
                    steps.append(fin)
                return steps

            for f in inproj_steps(units[0], 0):
                f()
            for i, u in enumerate(units):
                A = attend_steps(u, i % 2)
                B = inproj_steps(units[i + 1], (i + 1) % 2) if i + 1 < len(units) else []
                for f in _interleave(A, B):
                    f()
                if s == 0:
                    W.emit_conversions(2 * i, 2 * i + 2 if i + 1 < len(units) else 10 ** 9)
            if debug == "oatt":
                for hp in range(6):
                    finals.append(T.dma("pool", "dbg", _dma(dbg_d[s, hp * 128:(hp + 1) * 128, :], oattT[:, hp, :]),
                                        reads=["oattT"]))

        def stream(s):
            T.dma("sp", "c3", _dma(nw_bc, nw_d.partition_broadcast(128)), writes=["nw"])
            T.dma("sp", "c4", _dma(b2_bc, b2_d.partition_broadcast(128)), writes=["b2"])
            T.dma("sp", "c5", _dma(lng_bc, lng_d.partition_broadcast(128)), writes=["lng"])
            T.dma("sp", "c6", _dma(lnb_bc, lnb_d.partition_broadcast(128)), writes=["lnb"])
            T.op("dve", _memset(Sst, 0.0), writes=["S0", "S1", "S2", "S3"])
            T.op("dve", _memset(Sbf, 0.0), writes=["Sb0", "Sb1", "Sb2", "Sb3"])
            T.op("dve", _memset(hist, 0.0), writes=["hist%d" % q for q in range(24)])
            rot = {"u": 0, "c": 0, "k": 0}
            def load_xp(blk):
                tsl_ = slice(blk * 512, (blk + 1) * 512)
                q2 = blk % 2
                T.dma("pool", "xb%d" % q2, _dma(xTb2[q2], xT_d[s].rearrange("(c p) t -> p c t", p=128)[:, :, tsl_]),
                      writes=["xTb%d" % q2])
                T.dma("pool", "pb%d" % q2, _dma(pTb2[q2], pT_d[s].rearrange("(c p) t -> p c t", p=128)[:, :, tsl_]),
                      writes=["pTb%d" % q2])

            load_xp(0)
            for blk in range(4):
                tsl = slice(blk * 512, (blk + 1) * 512)
                xTb, pTb = xTb2[blk % 2], pTb2[blk % 2]
                XB, PB = "xTb%d" % (blk % 2), "pTb%d" % (blk % 2)
                if blk + 1 < 4:
                    load_xp(blk + 1)
                W.block_start()
                T.dma("sp", "xr", _dma(xres, x_d[s, tsl, :].rearrange("(t p) d -> p t d", p=128)),
                      writes=["xres", "xres0", "xres1", "xres2", "xres3"])

                for cg in range(6):
                    (wv,), wr = W.next([(w_in_v[:, :, XBC0 + cg * 512: XBC0 + (cg + 1) * 512], [8, 512])], cached=True)
                    for j in range(4):
                        cc = cg * 4 + j
                        b = rot["u"] % 2
                        rot["u"] += 1
                        pr = "ps%d" % b
                        for c in range(8):
                            T.op("pe", _mm(ps[b], wv[:, c, j * 128:(j + 1) * 128], xTb[:, c, :], c == 0, c == 7),
                                 reads=[wr, XB], writes=[pr])
                        ur, ct = uraw[b], ctmp[b]
                        T.op("act", _acopy(ur[:, 0:3], hist[:, cc, :]), reads=["hist%d" % cc], writes=["urh%d" % b])
                        T.op("act", _acopy(ur[:, 3:515], ps[b]), reads=[pr], writes=["ur%d" % b])
                        T.op("act", _acopy(hist[:, cc, :], ur[:, 512:515]), reads=["ur%d" % b], writes=["hist%d" % cc])
                        T.op("act", _act(ct, ps[b], AF.Identity, bias=cb[:, cc:cc + 1], scale=cw[:, cc, 3:4]),
                             reads=[pr, "cw", "cb"], writes=["ct%d" % b])
                        for k in (2, 1, 0):
                            T.op("dve", _stt(ct, ur[:, k:k + 512], cw[:, cc, k:k + 1], ct, ALU.mult, ALU.add),
                                 reads=["ur%d" % b, "urh%d" % b, "cw", "ct%d" % b], writes=["ct%d" % b])
                        xw = ["XT%d_%d" % (cc // 4, q4) for q4 in range(4)] if cc < 16 else ["XTbc"]
                        T.op("act", _act(XT[:, cc, :], ct, AF.Silu), reads=["ct%d" % b], writes=xw)

                T.barrier()
                (wdt,), wr = W.next([(w_in_v[:, :, DT0:DT0 + 32], [8, 32])], cached=True)
                for c4 in range(4):
                    csl = slice(c4 * 128, (c4 + 1) * 128)
                    for c in range(8):
                        T.op("pe", _mm(ps[2][:, c4 * 32:(c4 + 1) * 32], xTb[:, c, csl], wdt[:, c, :], c == 0, c == 7),
                             reads=[wr, XB], writes=["ps2"])
                T.op("dve", _tt(sm_t, ps[2][:, 0:128].rearrange("p (a h) -> p a h", a=4),
                                dtb_bc.unsqueeze(1).broadcast_to([128, 4, 32]), ALU.add),
                     reads=["ps2", "dtb"], writes=["sm_t"])
                T.op("act", _act(sm_e, sm_t, AF.Exp), reads=["sm_t"], writes=["sm_e"])
                T.op("act", _act(dtc, sm_e, AF.Ln, bias=1.0), reads=["sm_e"], writes=["dtc"])
                T.op("dve", _tt(ac, dtc, A_bc.unsqueeze(1).broadcast_to([128, 4, 32]), ALU.mult),
                     reads=["dtc", "A"], writes=["ac"])
                for c4 in range(4):
                    T.op("pe", _mm(ps[3][:, c4 * 32:(c4 + 1) * 32], tri32, ac[:, c4, :], True, True),
                         reads=["tri32", "ac"], writes=["ps3"])
                    T.op("pe", _mm(ps[3][:, 128 + c4 * 32:128 + (c4 + 1) * 32], ones32, ac[:, c4, :], True, True),
                         reads=["ones32", "ac"], writes=["ps3"])
                cs_ps = ps[3][:, 0:128].rearrange("p (a h) -> p a h", a=4)
                tot_ps = ps[3][:, 128:256].rearrange("p (a h) -> p a h", a=4)
                T.op("act", _acopy(csb, cs_ps), reads=["ps3"], writes=["csb"])
                T.op("act", _act(dstart, cs_ps, AF.Exp), reads=["ps3"], writes=["dstart"])
                T.op("act", _act(cdec, tot_ps, AF.Exp), reads=["ps3"], writes=["cdec"])
                T.op("dve", _tt(sm_t, tot_ps, csb, ALU.subtract), reads=["ps3", "csb"], writes=["sm_t"])
                T.op("act", _act(dend, sm_t, AF.Exp), reads=["sm_t"], writes=["dend"])

                its = [(g, c4) for g in range(4) for c4 in range(4)]
                wzs = {}

                def ssd_front(i):
                    g, c4 = its[i]
                    k = i % 2
                    kk = str(k)
                    hs = slice(8 * g, 8 * g + 8)
                    csl = slice(c4 * 128, (c4 + 1) * 128)
                    xn = "XT%d_%d" % (g, c4)
                    if c4 == 0:
                        wzs[g] = W.next([(w_in_v[:, :, Z0 + g * 512: Z0 + (g + 1) * 512], [8, 512])], cached=True)
                    (wz,), wr = wzs[g]
                    if c4 == 0:
                        for c4b in range(4):
                            for c in range(8):
                                T.op("pe", _mm(ps[0], xTb[:, c, c4b * 128:(c4b + 1) * 128], wz[:, c, :], c == 0, c == 7),
                                     reads=[wr, XB], writes=["ps0"])
                            T.op("act", _act(zsall[:, i + c4b, :], ps[0], AF.Silu), reads=["ps0"], writes=["zs%d" % (i + c4b)])
                    NA = RHSA_ACT
                    for h in range(NA):
                        T.op("act", _act(rhsa[k][:, h, :], tri, AF.Copy, scale=ac[:, c4, 8 * g + h:8 * g + h + 1]),
                             reads=["tri", "ac"], writes=["rhsa%s_%d" % (kk, h // 4)])
                    if NA < 8:
                        for q2 in range(2):
                            T.op("dve", _tt(rhsa[k][:, 4 * q2:4 * q2 + 4, :], tri.unsqueeze(1).broadcast_to([128, 4, 128]),
                                            ac[:, c4, 8 * g + 4 * q2:8 * g + 4 * q2 + 4].unsqueeze(2).broadcast_to([128, 4, 128]), ALU.mult),
                                 reads=["tri", "ac"], writes=["rhsa%s_%d" % (kk, q2)])
                    for j in range(4):
                        T.op("pe", _tr(psb[2][:, j * 128:(j + 1) * 128], XT[:, 4 * g + j, csl], ident),
                             reads=[xn, "ident"], writes=["ps2"])
                    T.op("pe", _tr(psb[2][:, 512:640], XT[:, 16 + g, csl], ident), reads=["XTbc", "ident"], writes=["ps2"])
                    xtm = psb[2][:, 0:512].rearrange("p (h d) -> p h d", h=8)
                    T.op("dve", _tt(xdt[k].rearrange("p (h d) -> p h d", h=8), xtm,
                                    dtc[:, c4, hs].unsqueeze(2).broadcast_to([128, 8, 64]), ALU.mult),
                         reads=["ps2", "dtc"], writes=["xdt" + kk])
                    T.op("dve", _tt(xD[k].rearrange("p (h d) -> p h d", h=8), xtm,
                                    D_bc[:, hs].unsqueeze(2).broadcast_to([128, 8, 64]), ALU.mult),
                         reads=["ps2", "D"], writes=["xD" + kk])
                    T.op("act", _acopy(Btm[k], psb[2][:, 512:640]), reads=["ps2"], writes=["Btm" + kk])
                    T.op(PENG, _tt(xdtd[k].rearrange("p (h d) -> p h d", h=8),
                                     xdt[k].rearrange("p (h d) -> p h d", h=8),
                                     dend[:, c4, hs].unsqueeze(2).broadcast_to([128, 8, 64]), ALU.mult),
                         reads=["xdt" + kk, "dend"], writes=["xdtd" + kk])
                    T.op("pe", _mm(ps[3][:, 256:384], XT[:, 16 + g, csl], XT[:, 20 + g, csl], True, True),
                         reads=["XTbc"], writes=["ps3g"])
                    T.op("act", _acopy(Gm[k], ps[3][:, 256:384]), reads=["ps3g"], writes=["Gm" + kk])
                    for q in range(2):
                        T.op("pe", _mm(ps[4 + q], Umat, rhsa[k][:, 4 * q:4 * q + 4, :].rearrange("p h l -> p (h l)"),
                                       True, False), reads=["U", "rhsa%s_%d" % (kk, q)], writes=["ps%d" % (4 + q)])
                        T.op("pe", _mm(ps[4 + q], ident, NEGM.rearrange("p h l -> p (h l)"), False, True),
                             reads=["ident", "NEGM"], writes=["ps%d" % (4 + q)])
                        T.op("act", _act(Es[q], ps[4 + q], AF.Exp), reads=["ps%d" % (4 + q)], writes=["Es%d" % q])
                        T.op("dve", _tt(MT[k][:, 4 * q:4 * q + 4, :], Es[q].rearrange("p (h l) -> p h l", h=4),
                                        Gm[k].unsqueeze(1).broadcast_to([128, 4, 128]), ALU.mult),
                             reads=["Es%d" % q, "Gm" + kk], writes=["MT" + kk])

                def ssd_back(i):
                    g, c4 = its[i]
                    k = i % 2
                    kk = str(k)
                    hs = slice(8 * g, 8 * g + 8)
                    csl = slice(c4 * 128, (c4 + 1) * 128)
                    xn = "XT%d_%d" % (g, c4)
                    T.op("pe", _c(lambda e, k=k: e.matmul(ps[6], ident, xD[k], start=True, stop=False, skip_group_check=True), 0.216),
                         reads=["ident", "xD" + kk], writes=["ps6"])
                    for h in range(8):
                        T.op("pe", _c(lambda e, k=k, h=h: e.matmul(ps[6][:, h * 64:(h + 1) * 64], MT[k][:, h, :],
                                                                   xdt[k][:, h * 64:(h + 1) * 64], start=False, stop=True,
                                                                   skip_group_check=True), 0.096),
                             reads=["MT" + kk, "xdt" + kk], writes=["ps6"])
                    T.op("pe", _mm(ps[7], XT[:, 20 + g, csl], Sbf[:, g, :], True, True),
                         reads=["XTbc", "Sb%d" % g], writes=["ps7"])
                    T.op("pe", _mm(ps[1], Btm[k], xdtd[k], True, True), reads=["Btm" + kk, "xdtd" + kk], writes=["ps1"])
                    T.op("dve", _tt(t1[k].rearrange("p (h d) -> p h d", h=8), ps[7].rearrange("p (h d) -> p h d", h=8),
                                    dstart[:, c4, hs].unsqueeze(2).broadcast_to([128, 8, 64]), ALU.mult),
                         reads=["ps7", "dstart"], writes=["t1" + kk])
                    T.op("dve", _tt(t1[k], ps[6], t1[k], ALU.add), reads=["ps6", "t1" + kk], writes=["t1" + kk])
                    T.op("dve", _tt(ug[k], t1[k], zsall[:, i, :], ALU.mult), reads=["t1" + kk, "zs%d" % i], writes=["ug" + kk])
                    T.op("act", _act(usq, ug[k], AF.Square, accum_out=ssq[k][:, 0:1]), reads=["ug" + kk],
                         writes=["usq", "ssq" + kk])
                    T.op("act", _act(ssq[k][:, 1:2], ssq[k][:, 0:1], AF.Ln, bias=epsr[:, 0:1], scale=1.0 / 512.0),
                         reads=["ssq" + kk, "epsr"], writes=["ssqb" + kk])
                    T.op("act", _act(ssq[k][:, 3:4], ssq[k][:, 1:2], AF.Exp, scale=-0.5), reads=["ssqb" + kk], writes=["ssqd" + kk])
                    T.op("dve", _stt(yb[k], ug[k], ssq[k][:, 3:4], nw_bc[:, g * 512:(g + 1) * 512], ALU.mult, ALU.mult),
                         reads=["ug" + kk, "ssqd" + kk, "nw"], writes=["yb" + kk])
                    Sg = Sst[:, g, :]
                    T.op(PENG, _tt(Sg.rearrange("p (h d) -> p h d", h=8), Sg.rearrange("p (h d) -> p h d", h=8),
                                     cdec[:, c4, hs].unsqueeze(2).broadcast_to([128, 8, 64]), ALU.mult),
                         reads=["S%d" % g, "cdec"], writes=["S%d" % g])
                    T.op("dve", _tt(Sg, ps[1], Sg, ALU.add), reads=["ps1", "S%d" % g], writes=["S%d" % g])
                    T.op("act", _acopy(Sbf[:, g, :], Sg), reads=["S%d" % g], writes=["Sb%d" % g])
                    for j in range(4):
                        T.op("pe", _tr(psb[3][:, j * 128:(j + 1) * 128], yb[k][:, j * 128:(j + 1) * 128], ident),
                             reads=["yb" + kk, "ident"], writes=["ps3t"])
                    T.op("act", _acopy(XT[:, 4 * g:4 * g + 4, csl], psb[3][:, 0:512].rearrange("p (j t) -> p j t", j=4)),
                         reads=["ps3t"], writes=[xn])

                ssd_front(0)
                for i in range(16):
                    if i + 1 < 16:
                        ssd_front(i + 1)
                    ssd_back(i)

                if debug == "yssm":
                    for j in range(16):
                        finals.append(T.dma("pool", "dbg", _dma(dbg_d[s, j * 128:(j + 1) * 128, tsl], XT[:, j, :]),
                                            reads=["XT%d_%d" % (j // 4, q4) for q4 in range(4)]))
                T.barrier()

                for j in range(8):
                    dsl = slice(j * 128, (j + 1) * 128)
                    (wb0, wb1, wga, wgb), wr = W.next([(w_br_v[:, 0:11, dsl], [11, 128]), (w_br_v[:, 11:22, dsl], [11, 128]),
                                                 (w_in_v[:, :, GM0 + j * 128: GM0 + (j + 1) * 128], [8, 128]),
                                                 (w_in_v[:, :, GM0 + 1024 + j * 128: GM0 + 1024 + (j + 1) * 128], [8, 128])], cached=True)
                    o = 4 * (j % 2)
                    pn = ["ps%d" % (o + q) for q in range(4)]
                    for i in range(6):
                        T.op("pe", _mm(ps[o], wb0[:, i, :], oattT[:, i, tsl], i == 0, i == 5), reads=[wr, "oattT"], writes=[pn[0]])
                    for i in range(16):
                        T.op("pe", _mm(ps[o + 1], (wb0[:, 6 + i, :] if i < 5 else wb1[:, i - 5, :]), XT[:, i, :], i == 0, i == 15),
                             reads=[wr] + ["XT%d_%d" % (i // 4, q4) for q4 in range(4)], writes=[pn[1]])
                    for c in range(8):
                        T.op("pe", _mm(ps[o + 2], wga[:, c, :], xTb[:, c, :], c == 0, c == 7), reads=[wr, XB], writes=[pn[2]])
                    for c in range(8):
                        T.op("pe", _mm(ps[o + 3], wgb[:, c, :], xTb[:, c, :], c == 0, c == 7), reads=[wr, XB], writes=[pn[3]])
                    k = j % 2
                    kk = str(k)
                    T.op("act", _act(sa[k], ps[o + 2], AF.Sigmoid, bias=bg[:, 0, j:j + 1]), reads=[pn[2], "bg"], writes=["sa" + kk])
                    T.op("dve", _tt(m1[k], ps[o], sa[k], ALU.mult), reads=[pn[0], "sa" + kk], writes=["m1" + kk])
                    T.op("act", _act(sgt[k], ps[o + 3], AF.Sigmoid, bias=bg[:, 1, j:j + 1]), reads=[pn[3], "bg"], writes=["sgt" + kk])
                    T.op("dve", _tt(tpt[k], ps[o + 1], sgt[k], ALU.mult), reads=[pn[1], "sgt" + kk], writes=["tpt" + kk])
                    T.op(PENG, _tt(mergedT[:, j, :], m1[k], tpt[k], ALU.add), reads=["m1" + kk, "tpt" + kk], writes=["mergedT"])
                for half in range(2):
                    hsl = slice(half * 512, (half + 1) * 512)
                    (wo,), wr = W.next([(w_out_v[:, :, hsl], [8, 512])], cached=True)
                    for tt in range(4):
                        tts = slice(tt * 128, (tt + 1) * 128)
                        b = tt % 2
                        for j in range(8):
                            T.op("pe", _mm(ps[b], mergedT[:, j, tts], wo[:, j, :], j == 0, j == 7),
                                 reads=[wr, "mergedT"], writes=["ps%d" % b])
                        T.op("dve", _stt(xres[:, tt, hsl], xres[:, tt, hsl], ALPHA, ps[b], ALU.mult, ALU.add),
                             reads=["ps%d" % b, "xres"], writes=["xres"])
                    (wgp, wpl), wr = W.next([(w_in_v[:, :, GPLE0 + half * 512: GPLE0 + (half + 1) * 512], [8, 512]),
                                             (w_ple_v[:, :, hsl], [2, 512])], cached=True)
                    for tt in range(4):
                        tts = slice(tt * 128, (tt + 1) * 128)
                        k = tt % 2
                        kk = str(k)
                        for c in range(8):
                            T.op("pe", _mm(ps[2 + k], xTb[:, c, tts], wgp[:, c, :], c == 0, c == 7),
                                 reads=[wr, XB], writes=["ps%d" % (2 + k)])
                        for c in range(2):
                            T.op("pe", _mm(ps[4 + k], pTb[:, c, tts], wpl[:, c, :], c == 0, c == 1),
                                 reads=[wr, PB], writes=["ps%d" % (4 + k)])
                        T.op("dve", _tt(sa[k], ps[2 + k], b2_bc[:, hsl], ALU.add), reads=["ps%d" % (2 + k), "b2"], writes=["sa" + kk])
                        T.op("act", _act(sgt[k], sa[k], AF.Sigmoid), reads=["sa" + kk], writes=["sgt" + kk])
                        T.op("dve", _tt(tpt[k], ps[4 + k], sgt[k], ALU.mult), reads=["ps%d" % (4 + k), "sgt" + kk], writes=["tpt" + kk])
                        T.op("dve", _tt(xres[:, tt, hsl], xres[:, tt, hsl], tpt[k], ALU.add), reads=["xres", "tpt" + kk], writes=["xres"])
                T.barrier()
                for tt in range(4):
                    k = tt % 2
                    kk = str(k)
                    r = xres[:, tt, :]
                    rn = "xres%d" % tt
                    st_ = lnst[k]
                    T.op("dve", lambda e, st_=st_, r=r: e.reduce_sum(out=st_[:, 0:1], in_=r, axis=AX.X), reads=["xres", rn], writes=["lnst" + kk])
                    T.op("dve", _ts(st_[:, 1:2], st_[:, 0:1], 1.0 / 1024.0, None, ALU.mult), reads=["lnst" + kk], writes=["lnstb" + kk])
                    T.op("dve", _ts(r, r, st_[:, 1:2], None, ALU.subtract), reads=["xres", rn, "lnstb" + kk], writes=[rn])
                    T.op("act", _act(lnsq, r, AF.Square, accum_out=st_[:, 2:3]), reads=[rn], writes=["lnsq", "lnstc" + kk])
                    T.op("act", _act(st_[:, 3:4], st_[:, 2:3], AF.Ln, bias=epsln[:, 0:1], scale=1.0 / 1024.0),
                         reads=["lnstc" + kk, "epsln"], writes=["lnstd" + kk])
                    T.op("act", _act(st_[:, 0:1], st_[:, 3:4], AF.Exp, scale=-0.5), reads=["lnstd" + kk], writes=["lnst" + kk])
                    T.op("dve", _stt(r, r, st_[:, 0:1], lng_bc, ALU.mult, ALU.mult), reads=[rn, "lnst" + kk, "lng"], writes=[rn])
                    T.op("dve", _tt(r, r, lnb_bc, ALU.add), reads=[rn, "lnb"], writes=[rn])
                    t0 = blk * 512 + tt * 128
                    finals.append(T.dma("sp", "st%d" % k, _dma(out_d[s, t0:t0 + 128, :], r), reads=[rn]))

        for s in range(NSEQ):
            attention(s)
            T.barrier()
            if debug != "oatt":
                stream(s)
                T.barrier()
        return finals

    T0 = Tracker(nc)
    W0 = WRing(T0, wslots, None, scratch=wscr)
    construct(T0, W0)
    assert len(W0.cache) <= NCACHE, len(W0.cache)
    T = Tracker(nc)
    W = WRing(T, wslots, W0.plan, cache=W0.cache, scratch=wscr)
    finals = construct(T, W)
    assert W.cur == len(W0.plan) and W.issued == len(W0.plan)
    if os.environ.get("MK_NOSCHED") is None:
        T.schedule()
    T.emit(finals)
    return nc


def _t5_bucket_np(dist):
    max_exact = 16
    d_f = np.maximum(dist, 1).astype(np.float32)
    large = max_exact + (np.log(d_f / np.float32(max_exact)) / np.float32(math.log(2048 / max_exact))
                         * np.float32(32 - max_exact)).astype(np.int32)
    large = np.minimum(large, 31)
    return np.where(dist < max_exact, dist, large)


def _host_consts():
    ki = np.arange(128)[:, None]
    qi = np.arange(128)[None, :]
    d_cur = qi - ki
    d_nxt = qi + 128 - ki
    delta = np.concatenate([d_cur, d_nxt], axis=1)
    valid = (delta >= 0) & (delta <= 128)
    maskT = valid.astype(np.float32)
    idx = np.stack([_t5_bucket_np(np.maximum(delta, 0) * d) for d in DILS])
    p = np.arange(128)[:, None]
    f = np.arange(128)[None, :]
    cmat = np.stack([(p == f), (p <= f), (p > f), np.ones((128, 128), bool)]).astype(np.float32)
    return maskT, idx, cmat


_NC_CACHE = {}


def kernel(x, p, w_in, b_gate, conv_w, conv_b, dt_bias, a_log, d_skip, ssm_norm_w,
           w_branch, w_out, w_ple, ln_g, ln_b, rel_bias):
    debug = os.environ.get("MK_DEBUG") or None
    f32 = np.float32
    x = np.asarray(x, f32)
    p = np.asarray(p, f32)[0]
    maskT, idx, cmat = _host_consts()
    rel_bias = np.asarray(rel_bias, f32)
    biasT = np.stack([rel_bias[idx[hh // 12], hh] for hh in range(36)]).astype(f32)
    cw = np.ascontiguousarray(np.asarray(conv_w, f32)[0].T.reshape(24, 128, 4).transpose(1, 0, 2))
    cb = np.ascontiguousarray(np.asarray(conv_b, f32)[0].reshape(24, 128).T)
    bgate = np.asarray(b_gate, f32)[0]
    bg = np.ascontiguousarray(bgate[0:2].reshape(2, 8, 128).transpose(2, 0, 1))
    shared = {
        "w_in": np.ascontiguousarray(np.asarray(w_in, f32)[0]),
        "w_branch": np.ascontiguousarray(np.asarray(w_branch, f32)[0]),
        "w_out": np.ascontiguousarray(np.asarray(w_out, f32)[0]),
        "w_ple": np.ascontiguousarray(np.asarray(w_ple, f32)[0]),
        "biasT": biasT, "maskT": maskT, "cmat": cmat, "cw": cw, "cb": cb, "bg": bg,
        "b2": np.ascontiguousarray(bgate[2]),
        "dt_bias": np.ascontiguousarray(np.asarray(dt_bias, f32)[0]),
        "a_log": np.ascontiguousarray(np.asarray(a_log, f32)[0]),
        "d_skip": np.ascontiguousarray(np.asarray(d_skip, f32)[0]),
        "ssm_norm_w": np.ascontiguousarray(np.asarray(ssm_norm_w, f32)[0]),
        "ln_g": np.ascontiguousarray(np.asarray(ln_g, f32)[0]),
        "ln_b": np.ascontiguousarray(np.asarray(ln_b, f32)[0]),
    }
    in_maps = []
    for c in range(N_CORES):
        xs = x[c * NSEQ:(c + 1) * NSEQ]
        m = dict(shared)
        m["x"] = np.ascontiguousarray(xs)
        m["xT"] = np.ascontiguousarray(xs.transpose(0, 2, 1))
        m["pT"] = np.ascontiguousarray(p[c * NSEQ:(c + 1) * NSEQ].transpose(0, 2, 1))
        in_maps.append(m)
    if debug not in _NC_CACHE:
        _NC_CACHE[debug] = build_program(debug)
    nc = _NC_CACHE[debug]
    ncores = int(os.environ.get("MK_CORES", N_CORES))
    res = run_bass_kernel_spmd(nc, in_maps[:ncores], core_ids=list(range(ncores)))
    if debug:
        return [r["dbg"] for r in res.results]
    return np.concatenate([r["out"] for r in res.results], axis=0).astype(f32)
```
